# Optimizing a Trainium2 kernel written in Bass

```python
import math
import jax, jax.numpy as jnp
from jax import lax
import numpy as np

D_MODEL = 1024
BATCH = 4
SEQ = 4096
DEPTH = 1

EPS = 1e-6
RET_HEADS = 4
RET_DK = 64
RET_DV = 128
RET_CHUNK = 128
ROPE_BASE = 10000.0
SWA_HEADS = 8
SWA_KV_HEADS = 2
SWA_DH = 64
WINDOW = 128
REL_BUCKETS = 32
REL_MAX_DIST = 128
PEER_HEADS = 8
PEER_NKEYS = 128
PEER_N_EXPERTS = PEER_NKEYS * PEER_NKEYS
PEER_DQ = 256
PEER_TOPK = 16
PEER_TOKEN_BLOCK = 128

RET_Q = RET_HEADS * RET_DK
RET_V = RET_HEADS * RET_DV
SWA_Q = SWA_HEADS * SWA_DH
SWA_KV = SWA_KV_HEADS * SWA_DH
D_IN = 2 * RET_Q + 2 * RET_V + SWA_Q + 2 * SWA_KV
D_MIX = RET_V + SWA_Q

kernel_name = "hybrid_retention_swa_peer_block"


def rms_norm(x, g):
    xf = x.astype(jnp.float32)
    y = xf * lax.rsqrt(jnp.mean(xf * xf, axis=-1, keepdims=True) + EPS)
    return (y * g.astype(jnp.float32)).astype(x.dtype)


def rotary(x, pos):
    d = x.shape[-1]
    inv = 1.0 / (ROPE_BASE ** (np.arange(0, d, 2, dtype=np.float32) / d))
    ang = pos.astype(jnp.float32)[:, None] * jnp.asarray(inv)[None, :]
    cos = jnp.cos(ang)[None, :, None, :]
    sin = jnp.sin(ang)[None, :, None, :]
    xf = x.astype(jnp.float32)
    x1, x2 = xf[..., : d // 2], xf[..., d // 2:]
    return jnp.concatenate([x1 * cos - x2 * sin, x1 * sin + x2 * cos], axis=-1).astype(x.dtype)


def retention(q, k, v):
    B, S, H, dk = q.shape
    dv = v.shape[-1]
    C = RET_CHUNK
    n = S // C
    f32 = jnp.float32
    log_gamma = jnp.log(1.0 - 2.0 ** (-5.0 - jnp.arange(H, dtype=f32)))
    qc = q.astype(f32).reshape(B, n, C, H, dk)
    kc = k.astype(f32).reshape(B, n, C, H, dk)
    vc = v.astype(f32).reshape(B, n, C, H, dv)
    idx = jnp.arange(C, dtype=f32)
    diff = idx[:, None] - idx[None, :]
    decay_mask = jnp.where(diff[None] >= 0,
                           jnp.exp(jnp.maximum(diff, 0.0)[None] * log_gamma[:, None, None]),
                           0.0)
    scores = jnp.einsum('bnihd,bnjhd->bnhij', qc, kc) * decay_mask[None, None]
    o_inner = jnp.einsum('bnhij,bnjhe->bnihe', scores, vc)
    zeta = jnp.exp((C - 1.0 - idx)[None, :] * log_gamma[:, None])
    contrib = jnp.einsum('bnjhd,hj,bnjhe->nbhde', kc, zeta, vc)
    chunk_decay = jnp.exp(C * log_gamma)[None, :, None, None]

    def step(state, c):
        return state * chunk_decay + c, state

    _, prev = lax.scan(step, jnp.zeros((B, H, dk, dv), f32), contrib)
    xi = jnp.exp((idx + 1.0)[None, :] * log_gamma[:, None])
    o_cross = jnp.einsum('bnihd,nbhde,hi->bnihe', qc, prev, xi)
    return (o_inner + o_cross).reshape(B, S, H, dv)


def t5_bucket(rel):
    max_exact = REL_BUCKETS // 2
    n = np.maximum(rel, 0)
    large = max_exact + (np.log(np.maximum(n, 1).astype(np.float32) / max_exact)
                         / math.log(REL_MAX_DIST / max_exact)
                         * (REL_BUCKETS - max_exact)).astype(np.int32)
    large = np.minimum(large, REL_BUCKETS - 1)
    return np.where(n < max_exact, n, large).astype(np.int32)


def sliding_window_gqa(q, k, v, sinks, rel_bias):
    B, S, Hq, d = q.shape
    Hkv = k.shape[2]
    G = Hq // Hkv
    W = WINDOW
    nb = S // W
    f32 = jnp.float32
    qb = q.reshape(B, nb, W, Hkv, G, d)
    pad = jnp.zeros((B, W, Hkv, d), k.dtype)
    kp = jnp.concatenate([pad, k], axis=1).reshape(B, nb + 1, W, Hkv, d)
    vp = jnp.concatenate([pad, v], axis=1).reshape(B, nb + 1, W, Hkv, d)
    kw = jnp.concatenate([kp[:, :-1], kp[:, 1:]], axis=2)
    vw = jnp.concatenate([vp[:, :-1], vp[:, 1:]], axis=2)
    s = jnp.einsum('bnqhgd,bnkhd->bnhgqk', qb, kw, preferred_element_type=f32) * (d ** -0.5)
    rel = np.arange(W)[:, None] + W - np.arange(2 * W)[None, :]
    band = (rel >= 0) & (rel < W)
    bias = rel_bias[t5_bucket(rel)].astype(f32)
    bias = jnp.transpose(bias, (2, 0, 1)).reshape(Hkv, G, W, 2 * W)
    key_pos = np.arange(nb)[:, None] * W - W + np.arange(2 * W)[None, :]
    valid = band[None] & (key_pos >= 0)[:, None, :]
    s = s + bias[None, None]
    s = jnp.where(valid[None, :, None, None], s, -jnp.inf)
    sink = sinks.astype(f32).reshape(Hkv, G)[None, None, :, :, None, None]
    m = jnp.maximum(jnp.max(s, axis=-1, keepdims=True), sink)
    p = jnp.exp(s - m)
    p = p / (jnp.sum(p, axis=-1, keepdims=True) + jnp.exp(sink - m))
    o = jnp.einsum('bnhgqk,bnkhd->bnqhgd', p, vw.astype(f32))
    return o.reshape(B, S, Hq * d)


def peer(x, wq, subkeys, u_tab, v_tab):
    B, S, D = x.shape
    T = B * S
    H, K = PEER_HEADS, PEER_TOPK
    f32 = jnp.float32
    xt = x.reshape(T, D)
    q = jnp.matmul(xt, wq).reshape(T, H, 2, PEER_DQ // 2)
    scores = jnp.einsum('thpc,hpkc->thpk', q, subkeys, preferred_element_type=f32)
    half_s, half_i = lax.top_k(scores, K)
    cand_s = (half_s[:, :, 0, :, None] + half_s[:, :, 1, None, :]).reshape(T, H, K * K)
    cand_i = (half_i[:, :, 0, :, None] * PEER_NKEYS + half_i[:, :, 1, None, :]).reshape(T, H, K * K)
    top_s, pos = lax.top_k(cand_s, K)
    eidx = jnp.take_along_axis(cand_i, pos, axis=-1)
    gates = jax.nn.softmax(top_s, axis=-1)
    Tb = PEER_TOKEN_BLOCK
    nblk = T // Tb

    def block(args):
        xb, ib, gb = args
        ue = u_tab[ib]
        hcur = jnp.einsum('thkd,td->thk', ue, xb, preferred_element_type=f32)
        a = gb * jax.nn.gelu(hcur, approximate=False)
        ve = v_tab[ib]
        return jnp.einsum('thk,thkd->td', a, ve.astype(f32)).astype(x.dtype)

    out = lax.map(block, (xt.reshape(nblk, Tb, D),
                          eidx.reshape(nblk, Tb, H, K),
                          gates.reshape(nblk, Tb, H, K)))
    return out.reshape(B, S, D)


def setup_inputs(seed: int = 0) -> dict:
    key = jax.random.key(seed)
    ks = jax.random.split(key, 16)
    f32 = jnp.float32
    L, D = DEPTH, D_MODEL
    nrm = lambda k, shape, std: jax.random.normal(k, shape, f32) * std
    return {
        "x": nrm(ks[0], (BATCH, SEQ, D), 1.0),
        "norm1_g": 1.0 + nrm(ks[1], (L, D), 0.02),
        "w_in": nrm(ks[2], (L, D, D_IN), D ** -0.5),
        "ret_gn_g": 1.0 + nrm(ks[3], (L, RET_V), 0.02),
        "swa_sinks": nrm(ks[4], (L, SWA_HEADS), 0.5),
        "rel_bias": nrm(ks[5], (REL_BUCKETS, SWA_HEADS), 0.5),
        "w_out": nrm(ks[6], (L, D_MIX, D), D_MIX ** -0.5),
        "norm2_g": 1.0 + nrm(ks[7], (L, D), 0.02),
        "peer_wq": nrm(ks[8], (L, D, PEER_HEADS * PEER_DQ), D ** -0.5),
        "peer_subkeys": nrm(ks[9], (L, PEER_HEADS, 2, PEER_NKEYS, PEER_DQ // 2), (PEER_DQ // 2) ** -0.5),
        "peer_u": nrm(ks[10], (L, PEER_N_EXPERTS, D), D ** -0.5),
        "peer_v": nrm(ks[11], (L, PEER_N_EXPERTS, D), 0.1),
        "final_g": 1.0 + nrm(ks[12], (D,), 0.02),
    }


def reference(x, norm1_g, w_in, ret_gn_g, swa_sinks, rel_bias, w_out, norm2_g,
              peer_wq, peer_subkeys, peer_u, peer_v, final_g):
    B, S, D = x.shape
    pos = jnp.arange(S)
    split_at = list(np.cumsum([RET_Q, RET_Q, RET_V, RET_V, SWA_Q, SWA_KV]))
    h = x
    for l in range(DEPTH):
        y = rms_norm(h, norm1_g[l])
        proj = jnp.matmul(y, w_in[l])
        r_q, r_k, r_v, r_g, s_q, s_k, s_v = jnp.split(proj, split_at, axis=-1)
        rq = rotary(r_q.reshape(B, S, RET_HEADS, RET_DK), pos)
        rk = rotary(r_k.reshape(B, S, RET_HEADS, RET_DK), pos) * (RET_DK ** -0.5)
        rv = r_v.reshape(B, S, RET_HEADS, RET_DV)
        ro = retention(rq, rk, rv)
        mu = jnp.mean(ro, axis=-1, keepdims=True)
        var = jnp.mean(jnp.square(ro - mu), axis=-1, keepdims=True)
        ro = ((ro - mu) * lax.rsqrt(var + EPS)).reshape(B, S, RET_V) * ret_gn_g[l].astype(jnp.float32)
        ret_out = (jax.nn.silu(r_g.astype(jnp.float32)) * ro).astype(h.dtype)
        swa_out = sliding_window_gqa(s_q.reshape(B, S, SWA_HEADS, SWA_DH),
                                     s_k.reshape(B, S, SWA_KV_HEADS, SWA_DH),
                                     s_v.reshape(B, S, SWA_KV_HEADS, SWA_DH),
                                     swa_sinks[l], rel_bias).astype(h.dtype)
        mixed = jnp.concatenate([ret_out, swa_out], axis=-1)
        h = h + jnp.matmul(mixed, w_out[l])
        h = h + peer(rms_norm(h, norm2_g[l]), peer_wq[l], peer_subkeys[l], peer_u[l], peer_v[l])
    return rms_norm(h, final_g)
```

```python
import os
import math
from contextlib import ExitStack

import numpy as np
import concourse.bass as bass
import concourse.mybir as mybir
from concourse.bass_utils import run_bass_kernel_spmd

F32 = mybir.dt.float32
BF16 = mybir.dt.bfloat16
I32 = mybir.dt.int32
U32 = mybir.dt.uint32
AF = mybir.ActivationFunctionType
ALU = mybir.AluOpType
AX = mybir.AxisListType

NCORES = 8
D = 1024
TOK = 2048
NCH = 16
DIN = 2304
NEG = -30000.0
EPS = 1e-6
LOGG = [math.log(1.0 - 2.0 ** (-5.0 - h)) for h in range(4)]
CDEC = [math.exp(128.0 * lg) for lg in LOGG]
NSLOT = 16


class Buf:
    __slots__ = ("name", "w", "r", "excl")

    def __init__(self, name=""):
        self.name = name
        self.w = None
        self.r = {}
        self.excl = False


class Tl:
    def __init__(self, t, name):
        self.t = t
        self.b = Buf(name)


class Sched:
    ENG = ["pe", "act", "dve", "pool", "sp"]

    def __init__(self, nc, stack):
        self.nc = nc
        self.stack = stack
        self.cnt = {e: 0 for e in self.ENG}
        self.ops = {e: [] for e in self.ENG}
        self.waited = {e: {} for e in self.ENG}
        self.dma_cnt = {}
        self.sems = {}
        for e in self.ENG:
            self.sems[e] = stack.enter_context(nc.semaphore("s_" + e))

    def sem(self, key):
        if key not in self.sems:
            self.sems[key] = self.stack.enter_context(nc_semaphore(self.nc, "d%d" % len(self.sems)))
        return self.sems[key]

    def _need(self, eng, key, val):
        if val <= 0:
            return
        if self.waited[eng].get(key, 0) >= val:
            return
        self.waited[eng][key] = val
        self.ops[eng].append(("wait", key, val))

    def op(self, eng, fn, R=(), W=()):
        for b in R:
            if b.w is not None:
                self._need(eng, *b.w)
            if b.excl:
                for k, v in b.r.items():
                    if k != eng:
                        self._need(eng, k, v)
        strict = eng != "pe"
        for b in W:
            if b.w is not None and (strict or b.w[0] != eng):
                self._need(eng, *b.w)
            for k, v in b.r.items():
                if strict or k != eng:
                    self._need(eng, k, v)
        self.cnt[eng] += 1
        c = self.cnt[eng]
        self.ops[eng].append(("op", fn, eng, 1))
        for b in R:
            b.r[eng] = c
        for b in W:
            b.w = (eng, c)
            b.r = {}

    def dma(self, q, fn, R=(), W=(), key=None):
        key = "dma:" + key
        self.sem(key)
        prev = self.dma_cnt.get(key, 0)
        self._need(q, key, prev)
        for b in R:
            if b.w is not None and b.w[0] != key:
                self._need(q, *b.w)
        for b in W:
            if b.w is not None and b.w[0] != key:
                self._need(q, *b.w)
            for k, v in b.r.items():
                if k != key:
                    self._need(q, k, v)
        cur = prev + 16
        self.dma_cnt[key] = cur
        self.ops[q].append(("op", fn, key, 16))
        for b in R:
            b.r[key] = cur
        for b in W:
            b.w = (key, cur)
            b.r = {}

    def barrier(self):
        for e in self.ENG:
            for e2 in self.ENG:
                if e2 != e:
                    self._need(e, e2, self.cnt[e2])
            for k, v in self.dma_cnt.items():
                self._need(e, k, v)

    def final(self):
        for k, v in self.dma_cnt.items():
            self._need("sp", k, v)
        for e2 in self.ENG:
            if e2 != "sp":
                self._need("sp", e2, self.cnt[e2])

    def emit(self):
        nc = self.nc
        with nc.Block() as blk:
            for eng, deco in (("sp", blk.sync), ("act", blk.scalar), ("dve", blk.vector),
                              ("pool", blk.gpsimd), ("pe", blk.tensor)):
                def body(e, eng=eng):
                    for it in self.ops[eng]:
                        if it[0] == "wait":
                            e.wait_ge(self.sems[it[1]], it[2])
                        else:
                            it[1](e).then_inc(self.sems[it[2]], it[3])
                deco(body)


def nc_semaphore(nc, name):
    return nc.semaphore(name)


def build(stage=99):
    nc = bass.Bass("TRN2", target_bir_lowering=False)

    def din(name, shape, dt=F32):
        return nc.dram_tensor(name, list(shape), dt, kind="ExternalInput").ap()

    xs = din("xs", [2 * TOK, D])
    w_in = din("w_in", [D, DIN])
    g1 = din("g1", [128, 8])
    rot = din("rot", [2 * NCH, 128, 256])
    dmt = din("dmt", [128, 512])
    xit = din("xit", [128, 256])
    zeta = din("zeta", [128, 4])
    gnb = din("gnb", [128, 512])
    sinkb = din("sinkb", [128, 8])
    bias = din("bias", [128, 2048])
    mask = din("mask", [128, 256])
    mask0 = din("mask0", [128, 256])
    ident = din("ident", [128, 128])
    w_out = din("w_out", [D, D])
    g2b = din("g2b", [128, D])
    wq = din("wq", [D, 2048])
    skt = din("skt", [16, 128, 128])
    uv_d = din("uv", [16384, 2 * D]) if stage >= 4 else None
    fgb = din("fgb", [128, D])
    iota_d = din("iota", [128, 256])
    out_d = nc.dram_tensor("out", [TOK, D], F32, kind="ExternalOutput").ap()
    h1_d = nc.dram_tensor("h1s", [TOK, D], F32, kind="Internal").ap()
    uvb_d = nc.dram_tensor("uvb", [16384, 2 * D], BF16, kind="Internal").ap() if stage >= 4 else None
    dbg_d = None
    if stage < 99:
        dbg_d = nc.dram_tensor("dbg", [NCH, 128, DIN], F32, kind="ExternalOutput").ap()

    stack = ExitStack()
    with stack:
        S = Sched(nc, stack)
        names = [0]

        def sb(shape, dt, st=stack, name=None):
            names[0] += 1
            nm = name or ("t%d" % names[0])
            return Tl(st.enter_context(nc.sbuf_tensor(nm, list(shape), dt)), nm)

        def ps(shape, dt, name):
            t = Tl(stack.enter_context(nc.psum_tensor(name, list(shape), dt)), name)
            t.b.excl = True
            return t

        PT = ps([128, 1024], BF16, "pt")
        PB = [ps([128, 512], F32, "pb%d" % i) for i in range(1, 8)]

        identf = sb([128, 128], F32)
        identb = sb([128, 128], BF16)
        g1t = sb([128, 8], F32)
        S.dma("sp", lambda e: e.dma_start(out=identf.t[:], in_=ident), W=[identf.b], key="identf")
        S.dma("sp", lambda e: e.dma_start(out=g1t.t[:], in_=g1), W=[g1t.b], key="g1t")
        S.op("dve", lambda e: e.tensor_copy(out=identb.t[:], in_=identf.t[:]), R=[identf.b], W=[identb.b])

        FA = sb([128, DIN], F32, name="FA")
        FB = sb([128, DIN], F32, name="FB")

        def load_cast(dst_ap_fn, src_ap_fn, n, width, scale_fn=None, wb=None):
            for i in range(n):
                stg = FA if i % 2 == 0 else FB
                S.dma("sp", lambda e, i=i, stg=stg: e.dma_start(out=stg.t[:, 0:width], in_=src_ap_fn(i)),
                      W=[stg.b], key=stg.b.name)
                if i % 2 == 0:
                    if scale_fn is None:
                        S.op("act", lambda e, i=i, stg=stg: e.activation(out=dst_ap_fn(i), in_=stg.t[:, 0:width],
                                                                         func=AF.Copy), R=[stg.b], W=[wb])
                    else:
                        S.op("act", lambda e, i=i, stg=stg: e.activation(out=dst_ap_fn(i), in_=stg.t[:, 0:width],
                                                                         func=AF.Copy, scale=scale_fn(i)),
                             R=[stg.b, g1t.b], W=[wb])
                else:
                    if scale_fn is None:
                        S.op("dve", lambda e, i=i, stg=stg: e.tensor_copy(out=dst_ap_fn(i), in_=stg.t[:, 0:width]),
                             R=[stg.b], W=[wb])
                    else:
                        S.op("dve", lambda e, i=i, stg=stg: e.tensor_scalar(out=dst_ap_fn(i), in0=stg.t[:, 0:width],
                                                                            scalar1=scale_fn(i), scalar2=None,
                                                                            op0=ALU.mult), R=[stg.b, g1t.b], W=[wb])

        p1 = ExitStack()
        with p1:
            def sb1(shape, dt, name=None):
                return sb(shape, dt, st=p1, name=name)

            WIN = sb1([128, 8, DIN], BF16, "WIN")
            WOUT = sb1([128, 8, D], BF16, "WOUT")
            w_in_v = w_in.rearrange("(kc p) n -> p kc n", p=128)
            w_out_v = w_out.rearrange("(kc p) n -> p kc n", p=128)
            load_cast(lambda i: WIN.t[:, i, :], lambda i: w_in_v[:, i, :], 8, DIN,
                      scale_fn=lambda i: g1t.t[:, i:i + 1], wb=WIN.b)
            load_cast(lambda i: WOUT.t[:, 2 * i:2 * i + 2, :], lambda i: w_out_v[:, 2 * i:2 * i + 2, :], 4, 2048,
                      wb=WOUT.b)

            def cload(src, shape, name):
                t = sb1(shape, F32, name)
                S.dma("sp", lambda e: e.dma_start(out=t.t[:], in_=src), W=[t.b], key=name)
                return t

            DMT = cload(dmt, [128, 512], "DMT")
            XIT = cload(xit, [128, 256], "XIT")
            ZET = cload(zeta, [128, 4], "ZET")
            GNB = cload(gnb, [128, 512], "GNB")
            SNK = cload(sinkb, [128, 8], "SNK")
            BIA = cload(bias, [128, 2048], "BIA")
            MSK = cload(mask, [128, 256], "MSK")
            MS0 = cload(mask0, [128, 256], "MS0")
            S.op("dve", lambda e: e.tensor_tensor(
                out=BIA.t[:].rearrange("p (h k) -> p h k", h=8),
                in0=BIA.t[:].rearrange("p (h k) -> p h k", h=8),
                in1=MSK.t[:].unsqueeze(1).to_broadcast([128, 8, 256]), op=ALU.add),
                R=[BIA.b, MSK.b], W=[BIA.b])

            XT = [sb1([128, D], F32, "XT%d" % i) for i in range(2)]
            ROT = [sb1([128, 256], F32, "ROT%d" % i) for i in range(2)]
            JD = sb1([128, D], BF16, "JD")
            FD = sb1([128, 1024], F32, "FD")
            ss = sb1([128, 1], F32, "ss")
            sd = sb1([128, 1], F32, "sd")
            rstd = sb1([128, 1], F32, "rstd")
            YB = sb1([128, D], BF16, "YB")
            YT = sb1([128, 8, 128], BF16, "YT")
            RQK = sb1([128, 512], BF16, "RQK")
            RQKT = sb1([128, 4, 128], BF16, "RQKT")
            QXT = sb1([128, 2, 128], BF16, "QXT")
            KZ = sb1([128, 256], BF16, "KZ")
            RVB = sb1([128, 512], BF16, "RVB")
            STB = sb1([128, 4, 128], BF16, "STB")
            STATE = sb1([128, 2, 128], F32, "STATE")
            STATEB = sb1([128, 2, 128], BF16, "STATEB")
            SQB = sb1([128, 640], BF16, "SQB")
            SQT = sb1([128, 4, 128], BF16, "SQT")
            SKW = sb1([128, 256], BF16, "SKW")
            SVW = sb1([128, 2, 128], BF16, "SVW")
            RO = sb1([128, 512], F32, "RO")
            GS = sb1([128, 512], F32, "GS")
            st1 = sb1([128, 4], F32, "st1")
            st2 = sb1([128, 4], F32, "st2")
            mean = sb1([128, 4], F32, "mean")
            var = sb1([128, 4], F32, "var")
            grs = sb1([128, 4], F32, "grs")
            SL = sb1([128, 2048], F32, "SL")
            PBF = sb1([128, 8, 256], BF16, "PBF")
            PTT = sb1([128, 16, 128], BF16, "PTT")
            mx = sb1([128, 8], F32, "mx")
            negm = sb1([128, 8], F32, "negm")
            rs = sb1([128, 8], F32, "rs")
            es = sb1([128, 8], F32, "es")
            rden = sb1([128, 8], F32, "rden")
            MIX = sb1([128, D], BF16, "MIX")
            MIXT = sb1([128, 8, 128], BF16, "MIXT")

            S.op("pool", lambda e: e.memset(STATE.t[:], 0.0), W=[STATE.b])
            S.op("pool", lambda e: e.memset(STATEB.t[:], 0.0), W=[STATEB.b])
            S.op("pool", lambda e: e.memset(SKW.t[:], 0.0), W=[SKW.b])
            S.op("pool", lambda e: e.memset(SVW.t[:], 0.0), W=[SVW.b])

            GROUPS = [(0, 512), (512, 512), (1024, 512), (1536, 512), (2048, 256)]

            NP0 = 64
            if stage >= 4:
                STG = [sb1([128, 2, 2 * D], F32, "STG%d" % i) for i in range(2)]
                BFS = [sb1([128, 2, 2 * D], BF16, "BFS%d" % i) for i in range(2)]
                uv_v = uv_d.rearrange("(n p) d -> p n d", p=128)
                uvb_v = uvb_d.rearrange("(n p) d -> p n d", p=128)

                def p0_load(i):
                    st_ = STG[i % 2]
                    S.dma("pool", lambda e, i=i, st_=st_: e.dma_start(out=st_.t[:], in_=uv_v[:, 2 * i:2 * i + 2, :]),
                          W=[st_.b], key=st_.b.name)

                def p0_step(i):
                    if i == 0:
                        p0_load(0)
                    if i + 1 < NP0:
                        p0_load(i + 1)
                    st_ = STG[i % 2]
                    bf_ = BFS[i % 2]
                    S.op("act", lambda e, st_=st_, bf_=bf_: e.activation(
                        out=bf_.t[:].rearrange("p a b -> p (a b)"), in_=st_.t[:].rearrange("p a b -> p (a b)"),
                        func=AF.Copy), R=[st_.b], W=[bf_.b])
                    S.dma("pool", lambda e, i=i, bf_=bf_: e.dma_start(out=uvb_v[:, 2 * i:2 * i + 2, :], in_=bf_.t[:]),
                          R=[bf_.b], key=bf_.b.name)

            YBs = [YB, sb1([128, D], BF16, "YB1")]
            NST = [(ss, sd, rstd), (sb1([128, 1], F32, "ss_b"), sb1([128, 1], F32, "sd_b"), sb1([128, 1], F32, "rstd_b"))]

            def norm1(c):
                xt = XT[c % 2]
                rt = ROT[c % 2]
                yb = YBs[c % 2]
                ss_, sd_, rstd_ = NST[c % 2]
                S.dma("sp", lambda e: e.dma_start(out=xt.t[:], in_=xs[c * 128:(c + 1) * 128, :]), W=[xt.b],
                      key=xt.b.name)
                S.dma("sp", lambda e: e.dma_start(out=rt.t[:], in_=rot[c]), W=[rt.b], key=rt.b.name)
                S.op("dve", lambda e: e.scalar_tensor_tensor(out=JD.t[:], in0=xt.t[:], scalar=1.0, in1=xt.t[:],
                                                             op0=ALU.mult, op1=ALU.mult, accum_out=ss_.t[:]),
                     R=[xt.b], W=[JD.b, ss_.b])
                S.op("dve", lambda e: e.tensor_scalar(out=sd_.t[:], in0=ss_.t[:], scalar1=1.0 / D, scalar2=EPS,
                                                      op0=ALU.mult, op1=ALU.add), R=[ss_.b], W=[sd_.b])
                S.op("act", lambda e: e.activation(out=sd_.t[:], in_=sd_.t[:], func=AF.Sqrt), R=[sd_.b], W=[sd_.b])
                S.op("dve", lambda e: e.reciprocal(out=rstd_.t[:], in_=sd_.t[:]), R=[sd_.b], W=[rstd_.b])
                S.op("dve", lambda e: e.tensor_scalar(out=yb.t[:], in0=xt.t[:], scalar1=rstd_.t[:], scalar2=None,
                                                      op0=ALU.mult), R=[xt.b, rstd_.b], W=[yb.b])

            def chunk1(c):
                own = c >= NCH
                xt = XT[c % 2]
                rt = ROT[c % 2]
                yb = YBs[c % 2]
                for kc in range(8):
                    S.op("pe", lambda e, kc=kc: e.transpose(out=PT.t[:, kc * 128:(kc + 1) * 128],
                                                            in_=yb.t[:, kc * 128:(kc + 1) * 128],
                                                            identity=identb.t[:]), R=[yb.b, identb.b], W=[PT.b])
                S.op("act", lambda e: e.activation(out=YT.t[:].rearrange("p a b -> p (a b)"), in_=PT.t[:],
                                                   func=AF.Copy), R=[PT.b], W=[YT.b])
                glist = list(range(5)) if own else ([0, 1, 4] if c == NCH - 1 else [0, 1])
                for gi in glist:
                    c0, wd = GROUPS[gi]
                    pb = PB[gi]
                    for kc in range(8):
                        S.op("pe", lambda e, kc=kc, c0=c0, wd=wd, pb=pb: e.matmul(
                            pb.t[:, 0:wd], lhsT=YT.t[:, kc, :], rhs=WIN.t[:, kc, c0:c0 + wd],
                            start=(kc == 0), stop=(kc == 7)), R=[YT.b, WIN.b], W=[pb.b])
                    if gi % 2 == 0:
                        S.op("act", lambda e, c0=c0, wd=wd, pb=pb: e.activation(
                            out=FA.t[:, c0:c0 + wd], in_=pb.t[:, 0:wd], func=AF.Copy), R=[pb.b], W=[FA.b])
                    else:
                        S.op("dve", lambda e, c0=c0, wd=wd, pb=pb: e.tensor_copy(
                            out=FA.t[:, c0:c0 + wd], in_=pb.t[:, 0:wd]), R=[pb.b], W=[FA.b])
                if stage == 1:
                    if own:
                        S.dma("sp", lambda e: e.dma_start(out=dbg_d[c - NCH], in_=FA.t[:]), R=[FA.b], key="FA")
                    return
                qk = FA.t[:, 0:512].rearrange("p (a h d) -> p a h d", a=2, h=4)
                cosb = rt.t[:, 0:128].rearrange("p (a d) -> p a d", a=2).unsqueeze(2).to_broadcast([128, 2, 4, 64])
                sinv = rt.t[:, 128:256].rearrange("p (a d) -> p a d", a=2)
                t1 = FD.t[:, 0:512].rearrange("p (a h d) -> p a h d", a=2, h=4)
                t2 = FD.t[:, 512:1024].rearrange("p (a h d) -> p a h d", a=2, h=4)
                S.op("dve", lambda e: e.tensor_tensor(out=t1, in0=qk, in1=cosb, op=ALU.mult),
                     R=[FA.b, rt.b], W=[FD.b])
                S.op("dve", lambda e: e.tensor_tensor(
                    out=t2[:, :, :, 0:32], in0=qk[:, :, :, 32:64],
                    in1=sinv[:, :, 0:32].unsqueeze(2).to_broadcast([128, 2, 4, 32]), op=ALU.mult),
                    R=[FA.b, rt.b], W=[FD.b])
                S.op("dve", lambda e: e.tensor_tensor(
                    out=t2[:, :, :, 32:64], in0=qk[:, :, :, 0:32],
                    in1=sinv[:, :, 32:64].unsqueeze(2).to_broadcast([128, 2, 4, 32]), op=ALU.mult),
                    R=[FA.b, rt.b], W=[FD.b])
                S.op("dve", lambda e: e.tensor_tensor(out=RQK.t[:], in0=FD.t[:, 0:512], in1=FD.t[:, 512:1024],
                                                      op=ALU.add), R=[FD.b], W=[RQK.b])
                S.op("dve", lambda e: e.tensor_tensor(
                    out=KZ.t[:].rearrange("p (h d) -> p h d", h=4),
                    in0=RQK.t[:, 256:512].rearrange("p (h d) -> p h d", h=4),
                    in1=ZET.t[:].unsqueeze(2).to_broadcast([128, 4, 64]), op=ALU.mult),
                    R=[RQK.b, ZET.b], W=[KZ.b])
                S.op("act", lambda e: e.activation(out=RVB.t[:], in_=FA.t[:, 512:1024], func=AF.Copy),
                     R=[FA.b], W=[RVB.b])
                if own or c == NCH - 1:
                    if own:
                        for a in range(2):
                            S.op("act", lambda e, a=a: e.activation(
                                out=SQB.t[:, 0:512].rearrange("p (j a d) -> p j a d", j=4, a=2)[:, :, a, :],
                                in_=FA.t[:, 1536 + a * 256:1792 + a * 256].rearrange("p (j d) -> p j d", j=4),
                                func=AF.Copy), R=[FA.b], W=[SQB.b])
                    S.op("act", lambda e: e.activation(out=SQB.t[:, 512:640], in_=FA.t[:, 2048:2176], func=AF.Copy),
                         R=[FA.b], W=[SQB.b])
                    S.op("act", lambda e: e.activation(out=SVW.t[:, 1, :], in_=FA.t[:, 2176:2304], func=AF.Copy),
                         R=[FA.b], W=[SVW.b])
                if own:
                    for j in range(4):
                        S.op("pe", lambda e, j=j: e.transpose(out=PT.t[:, j * 128:(j + 1) * 128],
                                                              in_=RQK.t[:, j * 128:(j + 1) * 128],
                                                              identity=identb.t[:]), R=[RQK.b, identb.b], W=[PT.b])
                    for j in range(4):
                        S.op("pe", lambda e, j=j: e.transpose(out=PT.t[:, (4 + j) * 128:(5 + j) * 128],
                                                              in_=SQB.t[:, j * 128:(j + 1) * 128],
                                                              identity=identb.t[:]), R=[SQB.b, identb.b], W=[PT.b])
                    S.op("dve", lambda e: e.tensor_copy(out=RQKT.t[:].rearrange("p a b -> p (a b)"),
                                                        in_=PT.t[:, 0:512]), R=[PT.b], W=[RQKT.b])
                    S.op("act", lambda e: e.activation(out=SQT.t[:].rearrange("p a b -> p (a b)"),
                                                       in_=PT.t[:, 512:1024], func=AF.Copy), R=[PT.b], W=[SQT.b])
                elif c == NCH - 1:
                    S.op("pe", lambda e: e.transpose(out=PT.t[:, 0:128], in_=SQB.t[:, 512:640],
                                                     identity=identb.t[:]), R=[SQB.b, identb.b], W=[PT.b])
                    S.op("dve", lambda e: e.tensor_copy(out=SKW.t[:, 128:256], in_=PT.t[:, 0:128]),
                         R=[PT.b], W=[SKW.b])

            PARTS = os.environ.get("KPART", "ret,swa,st").split(",")

            def chunk1b(c):
                own = c >= NCH
                xt = XT[c % 2]
                if own and "swa" in PARTS:
                    S.op("pe", lambda e: e.transpose(out=PT.t[:, 0:128], in_=SQB.t[:, 512:640],
                                                     identity=identb.t[:]), R=[SQB.b, identb.b], W=[PT.b])
                    S.op("dve", lambda e: e.tensor_copy(out=SKW.t[:, 128:256], in_=PT.t[:, 0:128]),
                         R=[PT.b], W=[SKW.b])
                if own and "ret" in PARTS:
                    S.op("dve", lambda e: e.tensor_tensor(out=QXT.t[:], in0=RQKT.t[:, 0:2, :],
                                                          in1=XIT.t[:].rearrange("p (a i) -> p a i", a=2),
                                                          op=ALU.mult), R=[RQKT.b, XIT.b], W=[QXT.b])
                    for h in range(4):
                        a, jj = h % 2, h // 2
                        pb = PB[a]
                        S.op("pe", lambda e, a=a, jj=jj, pb=pb: e.matmul(
                            pb.t[:, jj * 128:(jj + 1) * 128], lhsT=RQKT.t[a * 64:(a + 1) * 64, 2 + jj, :],
                            rhs=RQKT.t[a * 64:(a + 1) * 64, jj, :], start=True, stop=True),
                            R=[RQKT.b], W=[pb.b])
                    for a in range(2):
                        S.op("dve", lambda e, a=a: e.tensor_tensor(
                            out=STB.t[:, a::2, :], in0=PB[a].t[:, 0:256].rearrange("p (j i) -> p j i", j=2),
                            in1=DMT.t[:].rearrange("p (h i) -> p h i", h=4)[:, a::2, :], op=ALU.mult),
                            R=[PB[a].b, DMT.b], W=[STB.b])
                    for h in range(4):
                        a, jj = h % 2, h // 2
                        pb = PB[2 + a]
                        S.op("pe", lambda e, h=h, jj=jj, pb=pb: e.matmul(
                            pb.t[:, jj * 128:(jj + 1) * 128], lhsT=STB.t[:, h, :],
                            rhs=RVB.t[:, h * 128:(h + 1) * 128], start=True, stop=False),
                            R=[STB.b, RVB.b], W=[pb.b])
                        S.op("pe", lambda e, a=a, jj=jj, pb=pb: e.matmul(
                            pb.t[:, jj * 128:(jj + 1) * 128], lhsT=QXT.t[a * 64:(a + 1) * 64, jj, :],
                            rhs=STATEB.t[a * 64:(a + 1) * 64, jj, :], start=False, stop=True),
                            R=[QXT.b, STATEB.b], W=[pb.b])
                    for a in range(2):
                        S.op("act", lambda e, a=a: e.activation(
                            out=RO.t[:].rearrange("p (h e) -> p h e", h=4)[:, a::2, :],
                            in_=PB[2 + a].t[:, 0:256].rearrange("p (j e) -> p j e", j=2), func=AF.Copy),
                            R=[PB[2 + a].b], W=[RO.b])
                for h in (range(4) if "st" in PARTS else []):
                    a, jj = h % 2, h // 2
                    S.op("pe", lambda e, h=h, a=a, jj=jj: e.matmul(
                        PB[4].t[a * 64:(a + 1) * 64, jj * 128:(jj + 1) * 128], lhsT=KZ.t[:, h * 64:(h + 1) * 64],
                        rhs=RVB.t[:, h * 128:(h + 1) * 128], start=True, stop=True),
                        R=[KZ.b, RVB.b], W=[PB[4].b])
                for h in (range(4) if "st" in PARTS else []):
                    a, jj = h % 2, h // 2
                    S.op("dve", lambda e, h=h, a=a, jj=jj: e.scalar_tensor_tensor(
                        out=STATE.t[a * 64:(a + 1) * 64, jj, :], in0=STATE.t[a * 64:(a + 1) * 64, jj, :],
                        scalar=CDEC[h], in1=PB[4].t[a * 64:(a + 1) * 64, jj * 128:(jj + 1) * 128],
                        op0=ALU.mult, op1=ALU.add), R=[STATE.b, PB[4].b], W=[STATE.b])
                S.op("dve", lambda e: e.tensor_copy(out=STATEB.t[:], in_=STATE.t[:]), R=[STATE.b], W=[STATEB.b])
                if not own:
                    if c == NCH - 1:
                        roll()
                    return
                if "swa" in PARTS:
                    swa_part(c)
                if "ret" in PARTS:
                    gn_part(c)
                tail_part(c)

            def gn_part(c):
                ro3 = RO.t[:].rearrange("p (h e) -> p h e", h=4)
                S.op("dve", lambda e: e.tensor_reduce(out=st1.t[:], in_=ro3, axis=AX.X, op=ALU.add),
                     R=[RO.b], W=[st1.b])
                S.op("dve", lambda e: e.tensor_tensor(out=FD.t[:, 0:512], in0=RO.t[:], in1=RO.t[:], op=ALU.mult),
                     R=[RO.b], W=[FD.b])
                S.op("dve", lambda e: e.tensor_reduce(out=st2.t[:], in_=FD.t[:, 0:512].rearrange(
                    "p (h e) -> p h e", h=4), axis=AX.X, op=ALU.add), R=[FD.b], W=[st2.b])
                S.op("dve", lambda e: e.tensor_scalar(out=mean.t[:], in0=st1.t[:], scalar1=1.0 / 128, scalar2=None,
                                                      op0=ALU.mult), R=[st1.b], W=[mean.b])
                S.op("dve", lambda e: e.tensor_tensor(out=var.t[:], in0=mean.t[:], in1=mean.t[:], op=ALU.mult),
                     R=[mean.b], W=[var.b])
                S.op("dve", lambda e: e.scalar_tensor_tensor(out=var.t[:], in0=st2.t[:], scalar=1.0 / 128,
                                                             in1=var.t[:], op0=ALU.mult, op1=ALU.subtract),
                     R=[st2.b, var.b], W=[var.b])
                S.op("dve", lambda e: e.tensor_scalar(out=var.t[:], in0=var.t[:], scalar1=EPS, scalar2=None,
                                                      op0=ALU.add), R=[var.b], W=[var.b])
                S.op("act", lambda e: e.activation(out=var.t[:], in_=var.t[:], func=AF.Sqrt), R=[var.b], W=[var.b])
                S.op("dve", lambda e: e.reciprocal(out=grs.t[:], in_=var.t[:]), R=[var.b], W=[grs.b])
                S.op("dve", lambda e: e.tensor_tensor(out=ro3, in0=ro3,
                                                      in1=mean.t[:].unsqueeze(2).to_broadcast([128, 4, 128]),
                                                      op=ALU.subtract), R=[RO.b, mean.b], W=[RO.b])
                S.op("dve", lambda e: e.tensor_tensor(out=ro3, in0=ro3,
                                                      in1=grs.t[:].unsqueeze(2).to_broadcast([128, 4, 128]),
                                                      op=ALU.mult), R=[RO.b, grs.b], W=[RO.b])
                S.op("act", lambda e: e.activation(out=GS.t[:], in_=FA.t[:, 1024:1536], func=AF.Silu),
                     R=[FA.b], W=[GS.b])
                S.op("dve", lambda e: e.tensor_tensor(out=GS.t[:], in0=GS.t[:], in1=GNB.t[:], op=ALU.mult),
                     R=[GS.b, GNB.b], W=[GS.b])
                S.op("dve", lambda e: e.tensor_tensor(out=MIX.t[:, 0:512], in0=RO.t[:], in1=GS.t[:], op=ALU.mult),
                     R=[RO.b, GS.b], W=[MIX.b])

            def swa_part(c):
                sl3 = SL.t[:].rearrange("p (h k) -> p h k", h=8)
                bi3 = BIA.t[:].rearrange("p (h k) -> p h k", h=8)
                for jp in range(2):
                    for j in (2 * jp, 2 * jp + 1):
                        for a in range(2):
                            pb = PB[5 + a]
                            S.op("pe", lambda e, j=j, a=a, pb=pb: e.matmul(
                                pb.t[:, (j % 2) * 256:(j % 2 + 1) * 256], lhsT=SQT.t[a * 64:(a + 1) * 64, j, :],
                                rhs=SKW.t[a * 64:(a + 1) * 64, :], start=True, stop=True),
                                R=[SQT.b, SKW.b], W=[pb.b])
                    for a in range(2):
                        h0 = 4 * a + 2 * jp
                        S.op("dve", lambda e, a=a, h0=h0: e.scalar_tensor_tensor(
                            out=SL.t[:, h0 * 256:(h0 + 2) * 256], in0=PB[5 + a].t[:], scalar=0.125,
                            in1=BIA.t[:, h0 * 256:(h0 + 2) * 256], op0=ALU.mult, op1=ALU.add),
                            R=[PB[5 + a].b, BIA.b], W=[SL.b])
                if c == NCH:
                    S.op("dve", lambda e: e.tensor_tensor(out=sl3, in0=sl3,
                                                          in1=MS0.t[:].unsqueeze(1).to_broadcast([128, 8, 256]),
                                                          op=ALU.add), R=[SL.b, MS0.b], W=[SL.b])
                S.op("dve", lambda e: e.tensor_reduce(out=mx.t[:], in_=sl3, axis=AX.X, op=ALU.max),
                     R=[SL.b], W=[mx.b])
                S.op("dve", lambda e: e.tensor_tensor(out=mx.t[:], in0=mx.t[:], in1=SNK.t[:], op=ALU.max),
                     R=[mx.b, SNK.b], W=[mx.b])
                S.op("dve", lambda e: e.tensor_scalar(out=negm.t[:], in0=mx.t[:], scalar1=-1.0, scalar2=None,
                                                      op0=ALU.mult), R=[mx.b], W=[negm.b])
                for h in range(8):
                    S.op("act", lambda e, h=h: e.activation(out=PBF.t[:, h, :], in_=SL.t[:, h * 256:(h + 1) * 256],
                                                            func=AF.Exp, bias=negm.t[:, h:h + 1], scale=1.0,
                                                            accum_out=rs.t[:, h:h + 1]),
                         R=[SL.b, negm.b], W=[PBF.b, rs.b])
                S.op("dve", lambda e: e.tensor_tensor(out=es.t[:], in0=SNK.t[:], in1=mx.t[:], op=ALU.subtract),
                     R=[SNK.b, mx.b], W=[es.b])
                S.op("act", lambda e: e.activation(out=es.t[:], in_=es.t[:], func=AF.Exp), R=[es.b], W=[es.b])
                S.op("dve", lambda e: e.tensor_tensor(out=es.t[:], in0=es.t[:], in1=rs.t[:], op=ALU.add),
                     R=[es.b, rs.b], W=[es.b])
                S.op("dve", lambda e: e.reciprocal(out=rden.t[:], in_=es.t[:]), R=[es.b], W=[rden.b])
                for rnd in range(2):
                    for i in range(8):
                        idx = rnd * 8 + i
                        h, half = idx // 2, idx % 2
                        S.op("pe", lambda e, i=i, h=h, half=half: e.transpose(
                            out=PT.t[:, i * 128:(i + 1) * 128], in_=PBF.t[:, h, half * 128:(half + 1) * 128],
                            identity=identb.t[:]), R=[PBF.b, identb.b], W=[PT.b])
                    if rnd == 0:
                        S.op("act", lambda e: e.activation(out=PTT.t[:, 0:8, :].rearrange("p a b -> p (a b)"),
                                                           in_=PT.t[:], func=AF.Copy), R=[PT.b], W=[PTT.b])
                    else:
                        S.op("dve", lambda e: e.tensor_copy(out=PTT.t[:, 8:16, :].rearrange("p a b -> p (a b)"),
                                                            in_=PT.t[:]), R=[PT.b], W=[PTT.b])
                for h in range(8):
                    kv = h // 4
                    for half in range(2):
                        S.op("pe", lambda e, h=h, kv=kv, half=half: e.matmul(
                            PB[4].t[:, h * 64:(h + 1) * 64], lhsT=PTT.t[:, 2 * h + half, :],
                            rhs=SVW.t[:, half, kv * 64:(kv + 1) * 64], start=(half == 0), stop=(half == 1)),
                            R=[PTT.b, SVW.b], W=[PB[4].b])
                S.op("dve", lambda e: e.tensor_tensor(
                    out=MIX.t[:, 512:1024].rearrange("p (h d) -> p h d", h=8),
                    in0=PB[4].t[:].rearrange("p (h d) -> p h d", h=8),
                    in1=rden.t[:].unsqueeze(2).to_broadcast([128, 8, 64]), op=ALU.mult),
                    R=[PB[4].b, rden.b], W=[MIX.b])

            def tail_part(c):
                xt = XT[c % 2]
                roll()
                if stage == 2:
                    S.op("dve", lambda e: e.tensor_copy(out=FA.t[:, 0:1024], in_=MIX.t[:]), R=[MIX.b], W=[FA.b])
                    S.dma("sp", lambda e: e.dma_start(out=dbg_d[c - NCH], in_=FA.t[:]), R=[FA.b], key="FA")
                    return
                for kc in range(8):
                    S.op("pe", lambda e, kc=kc: e.transpose(out=PT.t[:, kc * 128:(kc + 1) * 128],
                                                            in_=MIX.t[:, kc * 128:(kc + 1) * 128],
                                                            identity=identb.t[:]), R=[MIX.b, identb.b], W=[PT.b])
                S.op("act", lambda e: e.activation(out=MIXT.t[:].rearrange("p a b -> p (a b)"), in_=PT.t[:],
                                                   func=AF.Copy), R=[PT.b], W=[MIXT.b])
                for nh in range(2):
                    pb = PB[nh]
                    for kc in range(8):
                        S.op("pe", lambda e, kc=kc, nh=nh, pb=pb: e.matmul(
                            pb.t[:], lhsT=MIXT.t[:, kc, :], rhs=WOUT.t[:, kc, nh * 512:(nh + 1) * 512],
                            start=(kc == 0), stop=(kc == 7)), R=[MIXT.b, WOUT.b], W=[pb.b])
                    S.op("dve", lambda e, nh=nh, pb=pb: e.tensor_tensor(
                        out=xt.t[:, nh * 512:(nh + 1) * 512], in0=pb.t[:], in1=xt.t[:, nh * 512:(nh + 1) * 512],
                        op=ALU.add), R=[pb.b, xt.b], W=[xt.b])
                S.dma("sp", lambda e: e.dma_start(out=h1_d[(c - NCH) * 128:(c - NCH + 1) * 128, :], in_=xt.t[:]),
                      R=[xt.b], key=xt.b.name)
                if stage == 3:
                    S.dma("sp", lambda e: e.dma_start(out=dbg_d[c - NCH, :, 0:1024], in_=xt.t[:]), R=[xt.b],
                          key=xt.b.name)

            def roll():
                S.op("pool", lambda e: e.tensor_copy(out=SKW.t[:, 0:128], in_=SKW.t[:, 128:256]),
                     R=[SKW.b], W=[SKW.b])
                S.op("pool", lambda e: e.tensor_copy(out=SVW.t[:, 0, :], in_=SVW.t[:, 1, :]),
                     R=[SVW.b], W=[SVW.b])

            chs = range(2 * NCH)
            if os.environ.get("KCH"):
                chs = [int(v) for v in os.environ["KCH"].split(",")]
            p0_done = 0
            chs = list(chs)
            norm1(chs[0])
            for ci, c in enumerate(chs):
                chunk1(c)
                if ci + 1 < len(chs):
                    norm1(chs[ci + 1])
                if stage >= 2:
                    chunk1b(c)
                if stage >= 4:
                    want = (NP0 * (ci + 1)) // len(chs)
                    while p0_done < want:
                        p0_step(p0_done)
                        p0_done += 1
            S.barrier()

        if stage >= 4:
            p2 = ExitStack()
            with p2:
                def sb2(shape, dt, name=None):
                    return sb(shape, dt, st=p2, name=name)

                WQ = sb2([128, 8, 2048], BF16, "WQ")
                SKT = sb2([128, 16, 128], BF16, "SKT")
                wq_v = wq.rearrange("(kc p) n -> p kc n", p=128)
                load_cast(lambda i: WQ.t[:, i, :], lambda i: wq_v[:, i, :], 8, 2048, wb=WQ.b)
                skt_v = skt.rearrange("f c k -> c f k")
                S.dma("sp", lambda e: e.dma_start(out=FA.t[:, 0:2048].rearrange("p (f k) -> p f k", f=16),
                                                  in_=skt_v), W=[FA.b], key="FA")
                S.op("dve", lambda e: e.tensor_copy(out=SKT.t[:].rearrange("p a b -> p (a b)"),
                                                    in_=FA.t[:, 0:2048]), R=[FA.b], W=[SKT.b])
                G2B = sb2([128, D], F32, "G2B")
                FGB = sb2([128, D], F32, "FGB")
                S.dma("sp", lambda e: e.dma_start(out=G2B.t[:], in_=g2b), W=[G2B.b], key="G2B")
                S.dma("sp", lambda e: e.dma_start(out=FGB.t[:], in_=fgb), W=[FGB.b], key="FGB")

                XT2 = [sb2([128, D], F32, "XT2%d" % i) for i in range(3)]
                XNs = [sb2([128, D], F32, "XN%d" % i) for i in range(2)]
                XNB = sb2([128, D], BF16, "XNB")
                XNT = sb2([128, 8, 128], BF16, "XNT")
                QT = sb2([128, 16, 128], BF16, "QT")
                JD2 = sb2([128, D], BF16, "JD2")
                FC = sb2([128, 2048], F32, "FC")
                FDD = sb2([128, 2048], F32, "FDD")
                ss2 = sb2([128, 1], F32, "ss2")
                sd2 = sb2([128, 1], F32, "sd2")
                rstd2 = sb2([128, 1], F32, "rstd2")
                ss3 = sb2([128, 1], F32, "ss3")
                sd3 = sb2([128, 1], F32, "sd3")
                rstd3 = sb2([128, 1], F32, "rstd3")
                HS = sb2([128, 16, 16], F32, "HS")
                HI = sb2([128, 16, 16], U32, "HI")
                HIF = sb2([128, 16, 16], F32, "HIF")
                HI0 = sb2([128, 8, 16], F32, "HI0")
                TS = sb2([128, 8, 16], F32, "TS")
                PI = sb2([128, 8, 16], U32, "PI")
                PA = sb2([128, 8, 16], U32, "PA")
                PBI = sb2([128, 8, 16], U32, "PBI")
                PAF = sb2([128, 8, 16], F32, "PAF")
                PXBF = sb2([128, 8, 16], F32, "PXBF")
                I1F = sb2([128, 8, 16], F32, "I1F")
                I2F = sb2([128, 8, 16], F32, "I2F")
                IOTA = sb2([128, 256], F32, "IOTA")
                S.dma("sp", lambda e: e.dma_start(out=IOTA.t[:], in_=iota_d), W=[IOTA.b], key="IOTA")
                EIF = sb2([128, 128], F32, "EIF")
                EIIs = [sb2([128, 128], I32, "EII%d" % i) for i in range(2)]
                GEs = [sb2([128, 128], F32, "GE%d" % i) for i in range(2)]
                gz = sb2([128, 8], F32, "gz")
                HCR = [sb2([128, 1], F32, "HC%d" % i) for i in range(4)]
                AAR = [sb2([128, 1], F32, "AA%d" % i) for i in range(4)]
                DG = [sb2([128, 128], BF16, "DG%d" % i) for i in range(4)]
                GT = [sb2([128, 2 * D], BF16, "GT%d" % i) for i in range(NSLOT)]
                gcount = [0]
                USE_F32R = os.environ.get("KF32R", "0") == "1"

                def mmview(ap):
                    return ap.bitcast(mybir.dt.float32r) if USE_F32R else ap

                def front(c):
                    xt = XT2[c % 3]
                    XN = XNs[c % 2]
                    EII = EIIs[c % 2]
                    GE = GEs[c % 2]
                    S.dma("sp", lambda e: e.dma_start(out=xt.t[:], in_=h1_d[c * 128:(c + 1) * 128, :]), W=[xt.b],
                          key=xt.b.name)
                    S.op("dve", lambda e: e.scalar_tensor_tensor(out=JD2.t[:], in0=xt.t[:], scalar=1.0, in1=xt.t[:],
                                                                 op0=ALU.mult, op1=ALU.mult, accum_out=ss2.t[:]),
                         R=[xt.b], W=[JD2.b, ss2.b])
                    yield
                    S.op("dve", lambda e: e.tensor_scalar(out=sd2.t[:], in0=ss2.t[:], scalar1=1.0 / D, scalar2=EPS,
                                                          op0=ALU.mult, op1=ALU.add), R=[ss2.b], W=[sd2.b])
                    S.op("act", lambda e: e.activation(out=sd2.t[:], in_=sd2.t[:], func=AF.Sqrt), R=[sd2.b],
                         W=[sd2.b])
                    S.op("dve", lambda e: e.reciprocal(out=rstd2.t[:], in_=sd2.t[:]), R=[sd2.b], W=[rstd2.b])
                    yield
                    S.op("dve", lambda e: e.scalar_tensor_tensor(out=XN.t[:], in0=xt.t[:], scalar=rstd2.t[:],
                                                                 in1=G2B.t[:], op0=ALU.mult, op1=ALU.mult),
                         R=[xt.b, rstd2.b, G2B.b], W=[XN.b])
                    S.op("act", lambda e: e.activation(out=XNB.t[:], in_=XN.t[:], func=AF.Copy), R=[XN.b], W=[XNB.b])
                    yield
                    for kc in range(8):
                        S.op("pe", lambda e, kc=kc: e.transpose(out=PT.t[:, kc * 128:(kc + 1) * 128],
                                                                in_=XNB.t[:, kc * 128:(kc + 1) * 128],
                                                                identity=identb.t[:]), R=[XNB.b, identb.b], W=[PT.b])
                    S.op("act", lambda e: e.activation(out=XNT.t[:].rearrange("p a b -> p (a b)"), in_=PT.t[:],
                                                       func=AF.Copy), R=[PT.b], W=[XNT.b])
                    yield
                    for f in range(16):
                        pb = PB[(f // 4) % 3]
                        for kc in range(8):
                            S.op("pe", lambda e, f=f, kc=kc, pb=pb: e.matmul(
                                pb.t[:, (f % 4) * 128:(f % 4 + 1) * 128], lhsT=WQ.t[:, kc, f * 128:(f + 1) * 128],
                                rhs=XNT.t[:, kc, :], start=(kc == 0), stop=(kc == 7)),
                                R=[WQ.b, XNT.b], W=[pb.b])
                        if f % 4 == 3:
                            g = f // 4
                            S.op("act", lambda e, g=g, pb=pb: e.activation(
                                out=QT.t[:, 4 * g:4 * g + 4, :].rearrange("p a b -> p (a b)"), in_=pb.t[:],
                                func=AF.Copy), R=[pb.b], W=[QT.b])
                        yield
                    for _ in range(4):
                        yield
                    for f in range(16):
                        pb = PB[(f // 4) % 3]
                        S.op("pe", lambda e, f=f, pb=pb: e.matmul(
                            pb.t[:, (f % 4) * 128:(f % 4 + 1) * 128], lhsT=QT.t[:, f, :], rhs=SKT.t[:, f, :],
                            start=True, stop=True), R=[QT.b, SKT.b], W=[pb.b])
                        if f % 4 == 3:
                            g = f // 4
                            S.op("act", lambda e, g=g, pb=pb: e.activation(
                                out=FA.t[:, g * 512:(g + 1) * 512], in_=pb.t[:], func=AF.Copy),
                                R=[pb.b], W=[FA.b])
                            yield
                    for _ in range(int(os.environ.get("KSPACE", "10"))):
                        yield
                    for f in range(16):
                        sc = FA.t[:, f * 128:(f + 1) * 128]
                        wk = FDD.t[:, f * 128:(f + 1) * 128]
                        S.op("dve", lambda e, f=f, sc=sc: e.max(out=HS.t[:, f, 0:8], in_=sc), R=[FA.b], W=[HS.b])
                        S.op("dve", lambda e, f=f, sc=sc, wk=wk: e.match_replace(
                            out=wk, in_to_replace=HS.t[:, f, 0:8], in_values=sc, imm_value=-1e30),
                            R=[FA.b, HS.b], W=[FDD.b])
                        yield
                        S.op("dve", lambda e, f=f, wk=wk: e.max(out=HS.t[:, f, 8:16], in_=wk), R=[FDD.b], W=[HS.b])
                        S.op("dve", lambda e, f=f, sc=sc: e.max_index(out=HI.t[:, f, 0:8], in_max=HS.t[:, f, 0:8],
                                                                      in_values=sc), R=[FA.b, HS.b], W=[HI.b])
                        yield
                        S.op("dve", lambda e, f=f, wk=wk: e.max_index(out=HI.t[:, f, 8:16], in_max=HS.t[:, f, 8:16],
                                                                      in_values=wk), R=[FDD.b, HS.b], W=[HI.b])
                        yield
                    S.op("dve", lambda e: e.tensor_copy(out=HIF.t[:], in_=HI.t[:]), R=[HI.b], W=[HIF.b])
                    hs4 = HS.t[:].rearrange("p (h a) k -> p h a k", a=2)
                    hif4 = HIF.t[:].rearrange("p (h a) k -> p h a k", a=2)
                    cs4 = FB.t[:, 0:2048].rearrange("p (h a b) -> p h a b", h=8, a=16)
                    ci4 = FC.t[:].rearrange("p (h a b) -> p h a b", h=8, a=16)
                    S.op("dve", lambda e: e.tensor_tensor(
                        out=cs4, in0=hs4[:, :, 0, :].unsqueeze(3).to_broadcast([128, 8, 16, 16]),
                        in1=hs4[:, :, 1, :].unsqueeze(2).to_broadcast([128, 8, 16, 16]), op=ALU.add),
                        R=[HS.b], W=[FB.b])
                    yield
                    for h in range(8):
                        cs = FB.t[:, h * 256:(h + 1) * 256]
                        wk = FDD.t[:, h * 256:(h + 1) * 256]
                        S.op("dve", lambda e, h=h, cs=cs: e.max(out=TS.t[:, h, 0:8], in_=cs), R=[FB.b], W=[TS.b])
                        S.op("dve", lambda e, h=h, cs=cs: e.max_index(out=PI.t[:, h, 0:8], in_max=TS.t[:, h, 0:8],
                                                                      in_values=cs), R=[FB.b, TS.b], W=[PI.b])
                        S.op("dve", lambda e, h=h, cs=cs, wk=wk: e.match_replace(
                            out=wk, in_to_replace=TS.t[:, h, 0:8], in_values=cs, imm_value=-1e30),
                            R=[FB.b, TS.b], W=[FDD.b])
                        yield
                        S.op("dve", lambda e, h=h, wk=wk: e.max(out=TS.t[:, h, 8:16], in_=wk), R=[FDD.b], W=[TS.b])
                        S.op("dve", lambda e, h=h, wk=wk: e.max_index(out=PI.t[:, h, 8:16], in_max=TS.t[:, h, 8:16],
                                                                      in_values=wk), R=[FDD.b, TS.b], W=[PI.b])
                        yield
                    S.op("dve", lambda e: e.tensor_scalar(out=PA.t[:], in0=PI.t[:], scalar1=4, scalar2=None,
                                                          op0=ALU.logical_shift_right), R=[PI.b], W=[PA.b])
                    S.op("dve", lambda e: e.tensor_scalar(out=PBI.t[:], in0=PI.t[:], scalar1=15, scalar2=None,
                                                          op0=ALU.bitwise_and), R=[PI.b], W=[PBI.b])
                    yield
                    S.op("dve", lambda e: e.tensor_copy(out=PAF.t[:], in_=PA.t[:]), R=[PA.b], W=[PAF.b])
                    S.op("dve", lambda e: e.tensor_copy(out=PXBF.t[:], in_=PBI.t[:]), R=[PBI.b], W=[PXBF.b])
                    yield
                    io4 = IOTA.t[:, 0:16].unsqueeze(1).unsqueeze(1).to_broadcast([128, 8, 16, 16])
                    eq4 = FDD.t[:].rearrange("p (h k a) -> p h k a", h=8, k=16)
                    for half, (pf, dst) in enumerate(((PAF, I1F), (PXBF, I2F))):
                        S.op("dve", lambda e, pf=pf: e.tensor_tensor(
                            out=eq4, in0=io4, in1=pf.t[:].unsqueeze(3).to_broadcast([128, 8, 16, 16]),
                            op=ALU.is_equal), R=[IOTA.b, pf.b], W=[FDD.b])
                        yield
                        S.op("dve", lambda e, half=half: e.tensor_tensor(
                            out=eq4, in0=eq4,
                            in1=hif4[:, :, half, :].unsqueeze(2).to_broadcast([128, 8, 16, 16]), op=ALU.mult),
                            R=[FDD.b, HIF.b], W=[FDD.b])
                        yield
                        S.op("dve", lambda e, dst=dst: e.tensor_reduce(out=dst.t[:], in_=eq4, axis=AX.X, op=ALU.add),
                             R=[FDD.b], W=[dst.b])
                        yield
                    S.op("dve", lambda e: e.scalar_tensor_tensor(
                        out=EIF.t[:].rearrange("p (h k) -> p h k", h=8), in0=I1F.t[:], scalar=128.0, in1=I2F.t[:],
                        op0=ALU.mult, op1=ALU.add), R=[I1F.b, I2F.b], W=[EIF.b])
                    S.op("dve", lambda e: e.tensor_copy(out=EII.t[:], in_=EIF.t[:]), R=[EIF.b], W=[EII.b])
                    ge3 = GE.t[:].rearrange("p (h k) -> p h k", h=8)
                    S.op("dve", lambda e: e.tensor_tensor(
                        out=ge3, in0=TS.t[:], in1=TS.t[:, :, 0:1].to_broadcast([128, 8, 16]), op=ALU.subtract),
                        R=[TS.b], W=[GE.b])
                    S.op("act", lambda e: e.activation(out=GE.t[:], in_=GE.t[:], func=AF.Exp), R=[GE.b], W=[GE.b])
                    yield
                    S.op("dve", lambda e: e.tensor_reduce(out=gz.t[:], in_=ge3, axis=AX.X, op=ALU.add),
                         R=[GE.b], W=[gz.b])
                    S.op("dve", lambda e: e.reciprocal(out=gz.t[:], in_=gz.t[:]), R=[gz.b], W=[gz.b])
                    S.op("dve", lambda e: e.tensor_tensor(out=ge3, in0=ge3,
                                                          in1=gz.t[:].unsqueeze(2).to_broadcast([128, 8, 16]),
                                                          op=ALU.mult), R=[GE.b, gz.b], W=[GE.b])
                    yield

                def accb(c):
                    return (PB[5], PB[6]) if c % 2 == 0 else (PB[3], PB[4])

                def back(c, pump, prev_tail):
                    xt = XT2[c % 3]
                    ACB = accb(c)
                    XN = XNs[c % 2]
                    EII = EIIs[c % 2]
                    GE = GEs[c % 2]
                    for s_ in range(128):
                        gt = GT[gcount[0] % NSLOT]
                        gcount[0] += 1
                        hc = HCR[s_ % 4]
                        aa = AAR[s_ % 4]
                        dg = DG[s_ % 4]
                        S.dma("pool", lambda e, gt=gt, s_=s_: e.indirect_dma_start(
                            out=gt.t[:], out_offset=None, in_=uvb_d,
                            in_offset=bass.IndirectOffsetOnAxis(ap=EII.t[:, s_:s_ + 1], axis=0)),
                            R=[EII.b], W=[gt.b], key=gt.b.name)
                        S.op("dve", lambda e, gt=gt, hc=hc: e.scalar_tensor_tensor(
                            out=JD2.t[:], in0=gt.t[:, 0:D], scalar=1.0, in1=XN.t[:], op0=ALU.mult,
                            op1=ALU.mult, accum_out=hc.t[:]), R=[gt.b, XN.b], W=[JD2.b, hc.b])
                        S.op("act", lambda e, hc=hc, aa=aa: e.activation(out=aa.t[:], in_=hc.t[:], func=AF.Gelu),
                             R=[hc.b], W=[aa.b])
                        S.op("act", lambda e, aa=aa, s_=s_: e.activation(out=aa.t[:], in_=aa.t[:], func=AF.Copy,
                                                                         scale=GE.t[:, s_:s_ + 1]),
                             R=[aa.b, GE.b], W=[aa.b])
                        S.op("act", lambda e, dg=dg, aa=aa: e.activation(out=dg.t[:], in_=identf.t[:], func=AF.Copy,
                                                                         scale=aa.t[:]),
                             R=[identf.b, aa.b], W=[dg.b])
                        for nh in range(2):
                            S.op("pe", lambda e, gt=gt, dg=dg, s_=s_, nh=nh: e.matmul(
                                ACB[nh].t[:], lhsT=dg.t[:],
                                rhs=gt.t[:, D + nh * 512:D + (nh + 1) * 512],
                                start=(s_ == 0), stop=(s_ == 127)), R=[dg.b, gt.b], W=[ACB[nh].b])
                        pump(int(os.environ.get("KPUMP", "1")))
                        if s_ == 114 and prev_tail is not None:
                            prev_tail()
                    for _ in range(2):
                        S.op("pe", lambda e: e.transpose(out=PT.t[:, 0:128], in_=identb.t[:], identity=identb.t[:]),
                             R=[identb.b], W=[PT.b, ACB[0].b, ACB[1].b])

                def tail(c):
                    xt = XT2[c % 3]
                    ACB = accb(c)
                    for nh in range(2):
                        S.op("dve", lambda e, nh=nh: e.tensor_tensor(
                            out=xt.t[:, nh * 512:(nh + 1) * 512], in0=ACB[nh].t[:],
                            in1=xt.t[:, nh * 512:(nh + 1) * 512], op=ALU.add), R=[xt.b, ACB[nh].b], W=[xt.b])
                    S.op("dve", lambda e: e.scalar_tensor_tensor(out=JD2.t[:], in0=xt.t[:], scalar=1.0, in1=xt.t[:],
                                                                 op0=ALU.mult, op1=ALU.mult, accum_out=ss3.t[:]),
                         R=[xt.b], W=[JD2.b, ss3.b])
                    S.op("dve", lambda e: e.tensor_scalar(out=sd3.t[:], in0=ss3.t[:], scalar1=1.0 / D, scalar2=EPS,
                                                          op0=ALU.mult, op1=ALU.add), R=[ss3.b], W=[sd3.b])
                    S.op("act", lambda e: e.activation(out=sd3.t[:], in_=sd3.t[:], func=AF.Sqrt), R=[sd3.b],
                         W=[sd3.b])
                    S.op("dve", lambda e: e.reciprocal(out=rstd3.t[:], in_=sd3.t[:]), R=[sd3.b], W=[rstd3.b])
                    S.op("dve", lambda e: e.scalar_tensor_tensor(out=xt.t[:], in0=xt.t[:], scalar=rstd3.t[:],
                                                                 in1=FGB.t[:], op0=ALU.mult, op1=ALU.mult),
                         R=[xt.b, rstd3.b, FGB.b], W=[xt.b])
                    S.dma("sp", lambda e: e.dma_start(out=out_d[c * 128:(c + 1) * 128, :], in_=xt.t[:]), R=[xt.b],
                          key=xt.b.name)

                chs2 = list(range(NCH))
                if os.environ.get("KCH2"):
                    chs2 = [int(v) for v in os.environ["KCH2"].split(",")]
                for _ in front(chs2[0]):
                    pass
                prev_tail = None
                for i, c in enumerate(chs2):
                    gen = front(chs2[i + 1]) if i + 1 < len(chs2) else iter(())

                    def pump(n, gen=gen):
                        for _ in range(n):
                            next(gen, None)
                    back(c, pump, prev_tail)
                    for _ in gen:
                        pass
                    prev_tail = (lambda c=c: tail(c))
                    if os.environ.get("KDEFER", "1") != "1":
                        prev_tail()
                        prev_tail = None
                if prev_tail is not None:
                    prev_tail()
                S.barrier()
        if stage < 4:
            pass
        S.final()
        S.emit()
    return nc


def t5_bucket(rel):
    max_exact = 16
    n = np.maximum(rel, 0)
    large = max_exact + (np.log(np.maximum(n, 1).astype(np.float32) / max_exact)
                         / math.log(128 / max_exact) * (32 - max_exact)).astype(np.int32)
    large = np.minimum(large, 31)
    return np.where(n < max_exact, n, large).astype(np.int32)


def host_consts():
    f32 = np.float32
    lg = np.array(LOGG, dtype=f32)
    idx = np.arange(128, dtype=f32)
    diff = idx[:, None] - idx[None, :]
    dm = np.where(diff[None] >= 0, np.exp(np.maximum(diff, 0.0)[None] * lg[:, None, None]), 0.0).astype(f32)
    dmt = np.ascontiguousarray(dm.transpose(2, 0, 1)).reshape(128, 512)
    xi = np.exp((idx + 1.0)[None, :] * lg[:, None]).astype(f32)
    xit = np.zeros((128, 2, 128), f32)
    for h in range(4):
        xit[(h % 2) * 64:(h % 2 + 1) * 64, h // 2, :] = xi[h][None, :]
    zeta = np.exp((127.0 - idx)[None, :] * lg[:, None]).astype(f32).T
    W = 128
    rel = np.arange(W)[:, None] + W - np.arange(2 * W)[None, :]
    band = (rel >= 0) & (rel < W)
    mask = np.where(band, 0.0, NEG).astype(f32)
    bucket = t5_bucket(rel)
    return dict(dmt=dmt, xit=xit.reshape(128, 256), zeta=np.ascontiguousarray(zeta), mask=mask, bucket=bucket)


def rot_table(pos0):
    inv = (1.0 / (10000.0 ** (np.arange(0, 64, 2, dtype=np.float32) / 64))).astype(np.float32)
    pos = (pos0 + np.arange(2048)).astype(np.float32)
    ang = pos[:, None] * inv[None, :]
    cos = np.cos(ang).astype(np.float32)
    sin = np.sin(ang).astype(np.float32)
    c2 = np.concatenate([cos, cos], axis=1)
    s2 = np.concatenate([-sin, sin], axis=1)
    t = np.concatenate([c2, c2 * np.float32(0.125), s2, s2 * np.float32(0.125)], axis=1)
    return t.reshape(16, 128, 256)


_NC_CACHE = {}


def prep(x, norm1_g, w_in, ret_gn_g, swa_sinks, rel_bias, w_out, norm2_g, peer_wq, peer_subkeys, peer_u,
         peer_v, final_g, stage=99):
    f32 = np.float32
    x = np.asarray(x, f32)
    hc = host_consts()
    rb = np.asarray(rel_bias, f32)
    bias = rb[hc["bucket"]]
    bias = np.ascontiguousarray(bias.transpose(0, 2, 1)).reshape(128, 2048)
    rep = lambda v, n: np.ascontiguousarray(np.broadcast_to(np.asarray(v, f32).reshape(1, n), (128, n)))
    common = dict(
        w_in=np.ascontiguousarray(np.asarray(w_in, f32)[0]),
        g1=np.ascontiguousarray(np.asarray(norm1_g, f32)[0].reshape(8, 128).T),
        dmt=hc["dmt"], xit=hc["xit"], zeta=hc["zeta"],
        gnb=rep(np.asarray(ret_gn_g)[0], 512), sinkb=rep(np.asarray(swa_sinks)[0], 8),
        bias=bias, mask=hc["mask"], ident=np.eye(128, dtype=f32),
        w_out=np.ascontiguousarray(np.asarray(w_out, f32)[0]),
        g2b=rep(np.asarray(norm2_g)[0], D),
        wq=np.ascontiguousarray(np.asarray(peer_wq, f32)[0]),
        skt=np.ascontiguousarray(np.asarray(peer_subkeys, f32)[0].reshape(16, 128, 128).transpose(0, 2, 1)),
        uv=np.ascontiguousarray(np.concatenate([np.asarray(peer_u, f32)[0], np.asarray(peer_v, f32)[0]], axis=1)),
        fgb=rep(final_g, D),
        iota=np.ascontiguousarray(np.broadcast_to(np.arange(256, dtype=f32).reshape(1, 256), (128, 256))),
    )
    rot_lo = rot_table(0)
    rot_hi = rot_table(2048)
    if stage < 4:
        common.pop("uv")
        for nm in ("g2b", "wq", "skt", "fgb"):
            pass
    in_maps = []
    for k in range(NCORES):
        b, half = k // 2, k % 2
        own = x[b, half * TOK:(half + 1) * TOK]
        if half == 0:
            pre = np.zeros_like(own)
            rot = np.concatenate([rot_lo, rot_lo], axis=0)
            mask0 = np.zeros((128, 256), f32)
            mask0[:, 0:128] = NEG
        else:
            pre = x[b, 0:TOK]
            rot = np.concatenate([rot_lo, rot_hi], axis=0)
            mask0 = np.zeros((128, 256), f32)
        m = dict(common)
        m["xs"] = np.ascontiguousarray(np.concatenate([pre, own], axis=0))
        m["rot"] = np.ascontiguousarray(rot)
        m["mask0"] = mask0
        in_maps.append(m)
    return in_maps


def kernel(_stage=None, **inputs):
    stage = 99 if _stage is None else _stage
    in_maps = prep(stage=stage, **inputs)
    f32 = np.float32
    if stage not in _NC_CACHE:
        _NC_CACHE[stage] = build(stage)
    nc = _NC_CACHE[stage]
    if os.environ.get("KTRACE"):
        res = run_bass_kernel_spmd(nc, in_maps, core_ids=list(range(NCORES)), trace=True)
        print("EXEC_TIME_NS", res.exec_time_ns)
    else:
        res = run_bass_kernel_spmd(nc, in_maps, core_ids=list(range(NCORES)))
    if stage < 99:
        return [r["dbg"] for r in res.results]
    outs = [np.asarray(r["out"], f32) for r in res.results]
    return np.stack(outs, axis=0).reshape(4, 4096, D)
```

```python
import os
import math
from contextlib import ExitStack

import numpy as np
import concourse.bass as bass
import concourse.mybir as mybir
from concourse.bass_utils import run_bass_kernel_spmd

F32 = mybir.dt.float32
BF16 = mybir.dt.bfloat16
I32 = mybir.dt.int32
U32 = mybir.dt.uint32
AF = mybir.ActivationFunctionType
ALU = mybir.AluOpType
AX = mybir.AxisListType

NCORES = 8
D = 1024
TOK = 2048
NCH = 16
DIN = 2304
NEG = -30000.0
EPS = 1e-6
LOGG = [math.log(1.0 - 2.0 ** (-5.0 - h)) for h in range(4)]
CDEC = [math.exp(128.0 * lg) for lg in LOGG]
NSLOT = 12


class Buf:
    __slots__ = ("name", "w", "r", "excl")

    def __init__(self, name=""):
        self.name = name
        self.w = None
        self.r = {}
        self.excl = False


class Tl:
    def __init__(self, t, name):
        self.t = t
        self.b = Buf(name)


class Sched:
    ENG = ["pe", "act", "dve", "pool", "sp"]

    def __init__(self, nc, stack):
        self.nc = nc
        self.stack = stack
        self.cnt = {e: 0 for e in self.ENG}
        self.ops = {e: [] for e in self.ENG}
        self.waited = {e: {} for e in self.ENG}
        self.dma_cnt = {}
        self.sems = {}
        for e in self.ENG:
            self.sems[e] = stack.enter_context(nc.semaphore("s_" + e))

    def sem(self, key):
        if key not in self.sems:
            self.sems[key] = self.stack.enter_context(nc_semaphore(self.nc, "d%d" % len(self.sems)))
        return self.sems[key]

    def _need(self, eng, key, val):
        if val <= 0:
            return
        if self.waited[eng].get(key, 0) >= val:
            return
        self.waited[eng][key] = val
        self.ops[eng].append(("wait", key, val))

    def op(self, eng, fn, R=(), W=()):
        for b in R:
            if b.w is not None:
                self._need(eng, *b.w)
            if b.excl:
                for k, v in b.r.items():
                    if k != eng:
                        self._need(eng, k, v)
        strict = eng != "pe"
        for b in W:
            if b.w is not None and (strict or b.w[0] != eng):
                self._need(eng, *b.w)
            for k, v in b.r.items():
                if strict or k != eng:
                    self._need(eng, k, v)
        self.cnt[eng] += 1
        c = self.cnt[eng]
        self.ops[eng].append(("op", fn, eng, 1))
        for b in R:
            b.r[eng] = c
        for b in W:
            b.w = (eng, c)
            b.r = {}

    def dma(self, q, fn, R=(), W=(), key=None):
        key = "dma:" + key
        self.sem(key)
        prev = self.dma_cnt.get(key, 0)
        self._need(q, key, prev)
        for b in R:
            if b.w is not None and b.w[0] != key:
                self._need(q, *b.w)
        for b in W:
            if b.w is not None and b.w[0] != key:
                self._need(q, *b.w)
            for k, v in b.r.items():
                if k != key:
                    self._need(q, k, v)
        cur = prev + 16
        self.dma_cnt[key] = cur
        self.ops[q].append(("op", fn, key, 16))
        for b in R:
            b.r[key] = cur
        for b in W:
            b.w = (key, cur)
            b.r = {}

    def barrier(self):
        for e in self.ENG:
            for e2 in self.ENG:
                if e2 != e:
                    self._need(e, e2, self.cnt[e2])
            for k, v in self.dma_cnt.items():
                self._need(e, k, v)

    def final(self):
        for k, v in self.dma_cnt.items():
            self._need("sp", k, v)
        for e2 in self.ENG:
            if e2 != "sp":
                self._need("sp", e2, self.cnt[e2])

    def emit(self):
        nc = self.nc
        with nc.Block() as blk:
            for eng, deco in (("sp", blk.sync), ("act", blk.scalar), ("dve", blk.vector),
                              ("pool", blk.gpsimd), ("pe", blk.tensor)):
                def body(e, eng=eng):
                    for it in self.ops[eng]:
                        if it[0] == "wait":
                            e.wait_ge(self.sems[it[1]], it[2])
                        else:
                            it[1](e).then_inc(self.sems[it[2]], it[3])
                deco(body)


def nc_semaphore(nc, name):
    return nc.semaphore(name)


def build(stage=99):
    nc = bass.Bass("TRN2", target_bir_lowering=False)

    def din(name, shape, dt=F32):
        return nc.dram_tensor(name, list(shape), dt, kind="ExternalInput").ap()

    xs = din("xs", [2 * TOK, D])
    w_in = din("w_in", [D, DIN])
    g1 = din("g1", [128, 8])
    rot = din("rot", [2 * NCH, 128, 256])
    dmt = din("dmt", [128, 512])
    xit = din("xit", [128, 256])
    zeta = din("zeta", [128, 4])
    gnb = din("gnb", [128, 512])
    sinkb = din("sinkb", [128, 8])
    bias = din("bias", [128, 2048])
    mask = din("mask", [128, 256])
    mask0 = din("mask0", [128, 256])
    ident = din("ident", [128, 128])
    w_out = din("w_out", [D, D])
    g2b = din("g2b", [128, D])
    wq = din("wq", [D, 2048])
    skt = din("skt", [16, 128, 128])
    uv_d = din("uv", [16384, 2 * D]) if stage >= 4 else None
    fgb = din("fgb", [128, D])
    iota_d = din("iota", [128, 256])
    out_d = nc.dram_tensor("out", [TOK, D], F32, kind="ExternalOutput").ap()
    h1_d = nc.dram_tensor("h1s", [TOK, D], F32, kind="Internal").ap()
    uvb_d = nc.dram_tensor("uvb", [16384, 2 * D], BF16, kind="Internal").ap() if stage >= 4 else None
    dbg_d = None
    if stage < 99:
        dbg_d = nc.dram_tensor("dbg", [NCH, 128, DIN], F32, kind="ExternalOutput").ap()

    stack = ExitStack()
    with stack:
        S = Sched(nc, stack)
        names = [0]

        def sb(shape, dt, st=stack, name=None):
            names[0] += 1
            nm = name or ("t%d" % names[0])
            return Tl(st.enter_context(nc.sbuf_tensor(nm, list(shape), dt)), nm)

        def ps(shape, dt, name):
            t = Tl(stack.enter_context(nc.psum_tensor(name, list(shape), dt)), name)
            t.b.excl = True
            return t

        PT = ps([128, 1024], BF16, "pt")
        PB = [ps([128, 512], F32, "pb%d" % i) for i in range(1, 8)]

        identf = sb([128, 128], F32)
        identb = sb([128, 128], BF16)
        g1t = sb([128, 8], F32)
        S.dma("sp", lambda e: e.dma_start(out=identf.t[:], in_=ident), W=[identf.b], key="identf")
        S.dma("sp", lambda e: e.dma_start(out=g1t.t[:], in_=g1), W=[g1t.b], key="g1t")
        S.op("dve", lambda e: e.tensor_copy(out=identb.t[:], in_=identf.t[:]), R=[identf.b], W=[identb.b])

        FA = sb([128, DIN], F32, name="FA")
        FB = sb([128, DIN], F32, name="FB")

        def load_cast(dst_ap_fn, src_ap_fn, n, width, scale_fn=None, wb=None):
            for i in range(n):
                stg = FA if i % 2 == 0 else FB
                S.dma("sp", lambda e, i=i, stg=stg: e.dma_start(out=stg.t[:, 0:width], in_=src_ap_fn(i)),
                      W=[stg.b], key=stg.b.name)
                if i % 2 == 0:
                    if scale_fn is None:
                        S.op("act", lambda e, i=i, stg=stg: e.activation(out=dst_ap_fn(i), in_=stg.t[:, 0:width],
                                                                         func=AF.Copy), R=[stg.b], W=[wb])
                    else:
                        S.op("act", lambda e, i=i, stg=stg: e.activation(out=dst_ap_fn(i), in_=stg.t[:, 0:width],
                                                                         func=AF.Copy, scale=scale_fn(i)),
                             R=[stg.b, g1t.b], W=[wb])
                else:
                    if scale_fn is None:
                        S.op("dve", lambda e, i=i, stg=stg: e.tensor_copy(out=dst_ap_fn(i), in_=stg.t[:, 0:width]),
                             R=[stg.b], W=[wb])
                    else:
                        S.op("dve", lambda e, i=i, stg=stg: e.tensor_scalar(out=dst_ap_fn(i), in0=stg.t[:, 0:width],
                                                                            scalar1=scale_fn(i), scalar2=None,
                                                                            op0=ALU.mult), R=[stg.b, g1t.b], W=[wb])

        p1 = ExitStack()
        with p1:
            def sb1(shape, dt, name=None):
                return sb(shape, dt, st=p1, name=name)

            WIN = sb1([128, 8, DIN], BF16, "WIN")
            WOUT = sb1([128, 8, D], BF16, "WOUT")
            w_in_v = w_in.rearrange("(kc p) n -> p kc n", p=128)
            w_out_v = w_out.rearrange("(kc p) n -> p kc n", p=128)
            load_cast(lambda i: WIN.t[:, i, :], lambda i: w_in_v[:, i, :], 8, DIN,
                      scale_fn=lambda i: g1t.t[:, i:i + 1], wb=WIN.b)
            load_cast(lambda i: WOUT.t[:, 2 * i:2 * i + 2, :], lambda i: w_out_v[:, 2 * i:2 * i + 2, :], 4, 2048,
                      wb=WOUT.b)

            def cload(src, shape, name):
                t = sb1(shape, F32, name)
                S.dma("sp", lambda e: e.dma_start(out=t.t[:], in_=src), W=[t.b], key=name)
                return t

            DMT = cload(dmt, [128, 512], "DMT")
            XIT = cload(xit, [128, 256], "XIT")
            ZET = cload(zeta, [128, 4], "ZET")
            GNB = cload(gnb, [128, 512], "GNB")
            SNK = cload(sinkb, [128, 8], "SNK")
            BIA = cload(bias, [128, 2048], "BIA")
            MSK = cload(mask, [128, 256], "MSK")
            MS0 = cload(mask0, [128, 256], "MS0")
            S.op("dve", lambda e: e.tensor_tensor(
                out=BIA.t[:].rearrange("p (h k) -> p h k", h=8),
                in0=BIA.t[:].rearrange("p (h k) -> p h k", h=8),
                in1=MSK.t[:].unsqueeze(1).to_broadcast([128, 8, 256]), op=ALU.add),
                R=[BIA.b, MSK.b], W=[BIA.b])

            XT = [sb1([128, D], F32, "XT%d" % i) for i in range(2)]
            ROT = [sb1([128, 256], F32, "ROT%d" % i) for i in range(2)]
            JD = sb1([128, D], BF16, "JD")
            FD = sb1([128, 1024], F32, "FD")
            ss = sb1([128, 1], F32, "ss")
            sd = sb1([128, 1], F32, "sd")
            rstd = sb1([128, 1], F32, "rstd")
            YB = sb1([128, D], BF16, "YB")
            YT = sb1([128, 8, 128], BF16, "YT")
            RQK = sb1([128, 512], BF16, "RQK")
            RQKT = sb1([128, 4, 128], BF16, "RQKT")
            QXT = sb1([128, 2, 128], BF16, "QXT")
            KZ = sb1([128, 256], BF16, "KZ")
            RVB = sb1([128, 512], BF16, "RVB")
            STB = sb1([128, 4, 128], BF16, "STB")
            STATE = sb1([128, 2, 128], F32, "STATE")
            STATEB = sb1([128, 2, 128], BF16, "STATEB")
            SQB = sb1([128, 640], BF16, "SQB")
            SQT = sb1([128, 4, 128], BF16, "SQT")
            SKW = sb1([128, 256], BF16, "SKW")
            SVW = sb1([128, 2, 128], BF16, "SVW")
            RO = sb1([128, 512], F32, "RO")
            GS = sb1([128, 512], F32, "GS")
            st1 = sb1([128, 4], F32, "st1")
            st2 = sb1([128, 4], F32, "st2")
            mean = sb1([128, 4], F32, "mean")
            var = sb1([128, 4], F32, "var")
            grs = sb1([128, 4], F32, "grs")
            SL = sb1([128, 2048], F32, "SL")
            PBF = sb1([128, 8, 256], BF16, "PBF")
            PTT = sb1([128, 16, 128], BF16, "PTT")
            mx = sb1([128, 8], F32, "mx")
            negm = sb1([128, 8], F32, "negm")
            rs = sb1([128, 8], F32, "rs")
            es = sb1([128, 8], F32, "es")
            rden = sb1([128, 8], F32, "rden")
            MIX = sb1([128, D], BF16, "MIX")
            MIXT = sb1([128, 8, 128], BF16, "MIXT")

            S.op("pool", lambda e: e.memset(STATE.t[:], 0.0), W=[STATE.b])
            S.op("pool", lambda e: e.memset(STATEB.t[:], 0.0), W=[STATEB.b])
            S.op("pool", lambda e: e.memset(SKW.t[:], 0.0), W=[SKW.b])
            S.op("pool", lambda e: e.memset(SVW.t[:], 0.0), W=[SVW.b])

            GROUPS = [(0, 512), (512, 512), (1024, 512), (1536, 512), (2048, 256)]

            NP0 = 64
            if stage >= 4:
                STG = [sb1([128, 2, 2 * D], F32, "STG%d" % i) for i in range(2)]
                BFS = [sb1([128, 2, 2 * D], BF16, "BFS%d" % i) for i in range(2)]
                uv_v = uv_d.rearrange("(n p) d -> p n d", p=128)
                uvb_v = uvb_d.rearrange("(n p) d -> p n d", p=128)

                def p0_load(i):
                    st_ = STG[i % 2]
                    S.dma("pool", lambda e, i=i, st_=st_: e.dma_start(out=st_.t[:], in_=uv_v[:, 2 * i:2 * i + 2, :]),
                          W=[st_.b], key=st_.b.name)

                def p0_step(i):
                    if i == 0:
                        p0_load(0)
                    if i + 1 < NP0:
                        p0_load(i + 1)
                    st_ = STG[i % 2]
                    bf_ = BFS[i % 2]
                    S.op("act", lambda e, st_=st_, bf_=bf_: e.activation(
                        out=bf_.t[:].rearrange("p a b -> p (a b)"), in_=st_.t[:].rearrange("p a b -> p (a b)"),
                        func=AF.Copy), R=[st_.b], W=[bf_.b])
                    S.dma("pool", lambda e, i=i, bf_=bf_: e.dma_start(out=uvb_v[:, 2 * i:2 * i + 2, :], in_=bf_.t[:]),
                          R=[bf_.b], key=bf_.b.name)

            YBs = [YB, sb1([128, D], BF16, "YB1")]
            NST = [(ss, sd, rstd), (sb1([128, 1], F32, "ss_b"), sb1([128, 1], F32, "sd_b"), sb1([128, 1], F32, "rstd_b"))]

            def norm1(c):
                xt = XT[c % 2]
                rt = ROT[c % 2]
                yb = YBs[c % 2]
                ss_, sd_, rstd_ = NST[c % 2]
                S.dma("sp", lambda e: e.dma_start(out=xt.t[:], in_=xs[c * 128:(c + 1) * 128, :]), W=[xt.b],
                      key=xt.b.name)
                S.dma("sp", lambda e: e.dma_start(out=rt.t[:], in_=rot[c]), W=[rt.b], key=rt.b.name)
                S.op("dve", lambda e: e.scalar_tensor_tensor(out=JD.t[:], in0=xt.t[:], scalar=1.0, in1=xt.t[:],
                                                             op0=ALU.mult, op1=ALU.mult, accum_out=ss_.t[:]),
                     R=[xt.b], W=[JD.b, ss_.b])
                S.op("dve", lambda e: e.tensor_scalar(out=sd_.t[:], in0=ss_.t[:], scalar1=1.0 / D, scalar2=EPS,
                                                      op0=ALU.mult, op1=ALU.add), R=[ss_.b], W=[sd_.b])
                S.op("act", lambda e: e.activation(out=sd_.t[:], in_=sd_.t[:], func=AF.Sqrt), R=[sd_.b], W=[sd_.b])
                S.op("dve", lambda e: e.reciprocal(out=rstd_.t[:], in_=sd_.t[:]), R=[sd_.b], W=[rstd_.b])
                S.op("dve", lambda e: e.tensor_scalar(out=yb.t[:], in0=xt.t[:], scalar1=rstd_.t[:], scalar2=None,
                                                      op0=ALU.mult), R=[xt.b, rstd_.b], W=[yb.b])

            def chunk1(c):
                own = c >= NCH
                xt = XT[c % 2]
                rt = ROT[c % 2]
                yb = YBs[c % 2]
                for kc in range(8):
                    S.op("pe", lambda e, kc=kc: e.transpose(out=PT.t[:, kc * 128:(kc + 1) * 128],
                                                            in_=yb.t[:, kc * 128:(kc + 1) * 128],
                                                            identity=identb.t[:]), R=[yb.b, identb.b], W=[PT.b])
                S.op("act", lambda e: e.activation(out=YT.t[:].rearrange("p a b -> p (a b)"), in_=PT.t[:],
                                                   func=AF.Copy), R=[PT.b], W=[YT.b])
                glist = list(range(5)) if own else ([0, 1, 4] if c == NCH - 1 else [0, 1])
                for gi in glist:
                    c0, wd = GROUPS[gi]
                    pb = PB[gi]
                    for kc in range(8):
                        S.op("pe", lambda e, kc=kc, c0=c0, wd=wd, pb=pb: e.matmul(
                            pb.t[:, 0:wd], lhsT=YT.t[:, kc, :], rhs=WIN.t[:, kc, c0:c0 + wd],
                            start=(kc == 0), stop=(kc == 7)), R=[YT.b, WIN.b], W=[pb.b])
                    if gi % 2 == 0:
                        S.op("act", lambda e, c0=c0, wd=wd, pb=pb: e.activation(
                            out=FA.t[:, c0:c0 + wd], in_=pb.t[:, 0:wd], func=AF.Copy), R=[pb.b], W=[FA.b])
                    else:
                        S.op("dve", lambda e, c0=c0, wd=wd, pb=pb: e.tensor_copy(
                            out=FA.t[:, c0:c0 + wd], in_=pb.t[:, 0:wd]), R=[pb.b], W=[FA.b])
                if stage == 1:
                    if own:
                        S.dma("sp", lambda e: e.dma_start(out=dbg_d[c - NCH], in_=FA.t[:]), R=[FA.b], key="FA")
                    return
                qk = FA.t[:, 0:512].rearrange("p (a h d) -> p a h d", a=2, h=4)
                cosb = rt.t[:, 0:128].rearrange("p (a d) -> p a d", a=2).unsqueeze(2).to_broadcast([128, 2, 4, 64])
                sinv = rt.t[:, 128:256].rearrange("p (a d) -> p a d", a=2)
                t1 = FD.t[:, 0:512].rearrange("p (a h d) -> p a h d", a=2, h=4)
                t2 = FD.t[:, 512:1024].rearrange("p (a h d) -> p a h d", a=2, h=4)
                S.op("dve", lambda e: e.tensor_tensor(out=t1, in0=qk, in1=cosb, op=ALU.mult),
                     R=[FA.b, rt.b], W=[FD.b])
                S.op("dve", lambda e: e.tensor_tensor(
                    out=t2[:, :, :, 0:32], in0=qk[:, :, :, 32:64],
                    in1=sinv[:, :, 0:32].unsqueeze(2).to_broadcast([128, 2, 4, 32]), op=ALU.mult),
                    R=[FA.b, rt.b], W=[FD.b])
                S.op("dve", lambda e: e.tensor_tensor(
                    out=t2[:, :, :, 32:64], in0=qk[:, :, :, 0:32],
                    in1=sinv[:, :, 32:64].unsqueeze(2).to_broadcast([128, 2, 4, 32]), op=ALU.mult),
                    R=[FA.b, rt.b], W=[FD.b])
                S.op("dve", lambda e: e.tensor_tensor(out=RQK.t[:], in0=FD.t[:, 0:512], in1=FD.t[:, 512:1024],
                                                      op=ALU.add), R=[FD.b], W=[RQK.b])
                S.op("dve", lambda e: e.tensor_tensor(
                    out=KZ.t[:].rearrange("p (h d) -> p h d", h=4),
                    in0=RQK.t[:, 256:512].rearrange("p (h d) -> p h d", h=4),
                    in1=ZET.t[:].unsqueeze(2).to_broadcast([128, 4, 64]), op=ALU.mult),
                    R=[RQK.b, ZET.b], W=[KZ.b])
                S.op("act", lambda e: e.activation(out=RVB.t[:], in_=FA.t[:, 512:1024], func=AF.Copy),
                     R=[FA.b], W=[RVB.b])
                if own or c == NCH - 1:
                    if own:
                        for a in range(2):
                            S.op("act", lambda e, a=a: e.activation(
                                out=SQB.t[:, 0:512].rearrange("p (j a d) -> p j a d", j=4, a=2)[:, :, a, :],
                                in_=FA.t[:, 1536 + a * 256:1792 + a * 256].rearrange("p (j d) -> p j d", j=4),
                                func=AF.Copy), R=[FA.b], W=[SQB.b])
                    S.op("act", lambda e: e.activation(out=SQB.t[:, 512:640], in_=FA.t[:, 2048:2176], func=AF.Copy),
                         R=[FA.b], W=[SQB.b])
                    S.op("act", lambda e: e.activation(out=SVW.t[:, 1, :], in_=FA.t[:, 2176:2304], func=AF.Copy),
                         R=[FA.b], W=[SVW.b])
                if own:
                    for j in range(4):
                        S.op("pe", lambda e, j=j: e.transpose(out=PT.t[:, j * 128:(j + 1) * 128],
                                                              in_=RQK.t[:, j * 128:(j + 1) * 128],
                                                              identity=identb.t[:]), R=[RQK.b, identb.b], W=[PT.b])
                    for j in range(4):
                        S.op("pe", lambda e, j=j: e.transpose(out=PT.t[:, (4 + j) * 128:(5 + j) * 128],
                                                              in_=SQB.t[:, j * 128:(j + 1) * 128],
                                                              identity=identb.t[:]), R=[SQB.b, identb.b], W=[PT.b])
                    S.op("dve", lambda e: e.tensor_copy(out=RQKT.t[:].rearrange("p a b -> p (a b)"),
                                                        in_=PT.t[:, 0:512]), R=[PT.b], W=[RQKT.b])
                    S.op("act", lambda e: e.activation(out=SQT.t[:].rearrange("p a b -> p (a b)"),
                                                       in_=PT.t[:, 512:1024], func=AF.Copy), R=[PT.b], W=[SQT.b])
                elif c == NCH - 1:
                    S.op("pe", lambda e: e.transpose(out=PT.t[:, 0:128], in_=SQB.t[:, 512:640],
                                                     identity=identb.t[:]), R=[SQB.b, identb.b], W=[PT.b])
                    S.op("dve", lambda e: e.tensor_copy(out=SKW.t[:, 128:256], in_=PT.t[:, 0:128]),
                         R=[PT.b], W=[SKW.b])

            PARTS = os.environ.get("KPART", "ret,swa,st").split(",")

            def chunk1b(c):
                own = c >= NCH
                xt = XT[c % 2]
                if own and "swa" in PARTS:
                    S.op("pe", lambda e: e.transpose(out=PT.t[:, 0:128], in_=SQB.t[:, 512:640],
                                                     identity=identb.t[:]), R=[SQB.b, identb.b], W=[PT.b])
                    S.op("dve", lambda e: e.tensor_copy(out=SKW.t[:, 128:256], in_=PT.t[:, 0:128]),
                         R=[PT.b], W=[SKW.b])
                if own and "ret" in PARTS:
                    S.op("dve", lambda e: e.tensor_tensor(out=QXT.t[:], in0=RQKT.t[:, 0:2, :],
                                                          in1=XIT.t[:].rearrange("p (a i) -> p a i", a=2),
                                                          op=ALU.mult), R=[RQKT.b, XIT.b], W=[QXT.b])
                    for h in range(4):
                        a, jj = h % 2, h // 2
                        pb = PB[a]
                        S.op("pe", lambda e, a=a, jj=jj, pb=pb: e.matmul(
                            pb.t[:, jj * 128:(jj + 1) * 128], lhsT=RQKT.t[a * 64:(a + 1) * 64, 2 + jj, :],
                            rhs=RQKT.t[a * 64:(a + 1) * 64, jj, :], start=True, stop=True),
                            R=[RQKT.b], W=[pb.b])
                    for a in range(2):
                        S.op("dve", lambda e, a=a: e.tensor_tensor(
                            out=STB.t[:, a::2, :], in0=PB[a].t[:, 0:256].rearrange("p (j i) -> p j i", j=2),
                            in1=DMT.t[:].rearrange("p (h i) -> p h i", h=4)[:, a::2, :], op=ALU.mult),
                            R=[PB[a].b, DMT.b], W=[STB.b])
                    for h in range(4):
                        a, jj = h % 2, h // 2
                        pb = PB[2 + a]
                        S.op("pe", lambda e, h=h, jj=jj, pb=pb: e.matmul(
                            pb.t[:, jj * 128:(jj + 1) * 128], lhsT=STB.t[:, h, :],
                            rhs=RVB.t[:, h * 128:(h + 1) * 128], start=True, stop=False),
                            R=[STB.b, RVB.b], W=[pb.b])
                        S.op("pe", lambda e, a=a, jj=jj, pb=pb: e.matmul(
                            pb.t[:, jj * 128:(jj + 1) * 128], lhsT=QXT.t[a * 64:(a + 1) * 64, jj, :],
                            rhs=STATEB.t[a * 64:(a + 1) * 64, jj, :], start=False, stop=True),
                            R=[QXT.b, STATEB.b], W=[pb.b])
                    for a in range(2):
                        S.op("act", lambda e, a=a: e.activation(
                            out=RO.t[:].rearrange("p (h e) -> p h e", h=4)[:, a::2, :],
                            in_=PB[2 + a].t[:, 0:256].rearrange("p (j e) -> p j e", j=2), func=AF.Copy),
                            R=[PB[2 + a].b], W=[RO.b])
                for h in (range(4) if "st" in PARTS else []):
                    a, jj = h % 2, h // 2
                    S.op("pe", lambda e, h=h, a=a, jj=jj: e.matmul(
                        PB[4].t[a * 64:(a + 1) * 64, jj * 128:(jj + 1) * 128], lhsT=KZ.t[:, h * 64:(h + 1) * 64],
                        rhs=RVB.t[:, h * 128:(h + 1) * 128], start=True, stop=True),
                        R=[KZ.b, RVB.b], W=[PB[4].b])
                for h in (range(4) if "st" in PARTS else []):
                    a, jj = h % 2, h // 2
                    S.op("dve", lambda e, h=h, a=a, jj=jj: e.scalar_tensor_tensor(
                        out=STATE.t[a * 64:(a + 1) * 64, jj, :], in0=STATE.t[a * 64:(a + 1) * 64, jj, :],
                        scalar=CDEC[h], in1=PB[4].t[a * 64:(a + 1) * 64, jj * 128:(jj + 1) * 128],
                        op0=ALU.mult, op1=ALU.add), R=[STATE.b, PB[4].b], W=[STATE.b])
                S.op("dve", lambda e: e.tensor_copy(out=STATEB.t[:], in_=STATE.t[:]), R=[STATE.b], W=[STATEB.b])
                if not own:
                    if c == NCH - 1:
                        roll()
                    return
                if "swa" in PARTS:
                    swa_part(c)
                if "ret" in PARTS:
                    gn_part(c)
                tail_part(c)

            def gn_part(c):
                ro3 = RO.t[:].rearrange("p (h e) -> p h e", h=4)
                S.op("dve", lambda e: e.tensor_reduce(out=st1.t[:], in_=ro3, axis=AX.X, op=ALU.add),
                     R=[RO.b], W=[st1.b])
                S.op("dve", lambda e: e.tensor_tensor(out=FD.t[:, 0:512], in0=RO.t[:], in1=RO.t[:], op=ALU.mult),
                     R=[RO.b], W=[FD.b])
                S.op("dve", lambda e: e.tensor_reduce(out=st2.t[:], in_=FD.t[:, 0:512].rearrange(
                    "p (h e) -> p h e", h=4), axis=AX.X, op=ALU.add), R=[FD.b], W=[st2.b])
                S.op("dve", lambda e: e.tensor_scalar(out=mean.t[:], in0=st1.t[:], scalar1=1.0 / 128, scalar2=None,
                                                      op0=ALU.mult), R=[st1.b], W=[mean.b])
                S.op("dve", lambda e: e.tensor_tensor(out=var.t[:], in0=mean.t[:], in1=mean.t[:], op=ALU.mult),
                     R=[mean.b], W=[var.b])
                S.op("dve", lambda e: e.scalar_tensor_tensor(out=var.t[:], in0=st2.t[:], scalar=1.0 / 128,
                                                             in1=var.t[:], op0=ALU.mult, op1=ALU.subtract),
                     R=[st2.b, var.b], W=[var.b])
                S.op("dve", lambda e: e.tensor_scalar(out=var.t[:], in0=var.t[:], scalar1=EPS, scalar2=None,
                                                      op0=ALU.add), R=[var.b], W=[var.b])
                S.op("act", lambda e: e.activation(out=var.t[:], in_=var.t[:], func=AF.Sqrt), R=[var.b], W=[var.b])
                S.op("dve", lambda e: e.reciprocal(out=grs.t[:], in_=var.t[:]), R=[var.b], W=[grs.b])
                S.op("dve", lambda e: e.tensor_tensor(out=ro3, in0=ro3,
                                                      in1=mean.t[:].unsqueeze(2).to_broadcast([128, 4, 128]),
                                                      op=ALU.subtract), R=[RO.b, mean.b], W=[RO.b])
                S.op("dve", lambda e: e.tensor_tensor(out=ro3, in0=ro3,
                                                      in1=grs.t[:].unsqueeze(2).to_broadcast([128, 4, 128]),
                                                      op=ALU.mult), R=[RO.b, grs.b], W=[RO.b])
                S.op("act", lambda e: e.activation(out=GS.t[:], in_=FA.t[:, 1024:1536], func=AF.Silu),
                     R=[FA.b], W=[GS.b])
                S.op("dve", lambda e: e.tensor_tensor(out=GS.t[:], in0=GS.t[:], in1=GNB.t[:], op=ALU.mult),
                     R=[GS.b, GNB.b], W=[GS.b])
                S.op("dve", lambda e: e.tensor_tensor(out=MIX.t[:, 0:512], in0=RO.t[:], in1=GS.t[:], op=ALU.mult),
                     R=[RO.b, GS.b], W=[MIX.b])

            def swa_part(c):
                sl3 = SL.t[:].rearrange("p (h k) -> p h k", h=8)
                bi3 = BIA.t[:].rearrange("p (h k) -> p h k", h=8)
                for jp in range(2):
                    for j in (2 * jp, 2 * jp + 1):
                        for a in range(2):
                            pb = PB[5 + a]
                            S.op("pe", lambda e, j=j, a=a, pb=pb: e.matmul(
                                pb.t[:, (j % 2) * 256:(j % 2 + 1) * 256], lhsT=SQT.t[a * 64:(a + 1) * 64, j, :],
                                rhs=SKW.t[a * 64:(a + 1) * 64, :], start=True, stop=True),
                                R=[SQT.b, SKW.b], W=[pb.b])
                    for a in range(2):
                        h0 = 4 * a + 2 * jp
                        S.op("dve", lambda e, a=a, h0=h0: e.scalar_tensor_tensor(
                            out=SL.t[:, h0 * 256:(h0 + 2) * 256], in0=PB[5 + a].t[:], scalar=0.125,
                            in1=BIA.t[:, h0 * 256:(h0 + 2) * 256], op0=ALU.mult, op1=ALU.add),
                            R=[PB[5 + a].b, BIA.b], W=[SL.b])
                if c == NCH:
                    S.op("dve", lambda e: e.tensor_tensor(out=sl3, in0=sl3,
                                                          in1=MS0.t[:].unsqueeze(1).to_broadcast([128, 8, 256]),
                                                          op=ALU.add), R=[SL.b, MS0.b], W=[SL.b])
                S.op("dve", lambda e: e.tensor_reduce(out=mx.t[:], in_=sl3, axis=AX.X, op=ALU.max),
                     R=[SL.b], W=[mx.b])
                S.op("dve", lambda e: e.tensor_tensor(out=mx.t[:], in0=mx.t[:], in1=SNK.t[:], op=ALU.max),
                     R=[mx.b, SNK.b], W=[mx.b])
                S.op("dve", lambda e: e.tensor_scalar(out=negm.t[:], in0=mx.t[:], scalar1=-1.0, scalar2=None,
                                                      op0=ALU.mult), R=[mx.b], W=[negm.b])
                for h in range(8):
                    S.op("act", lambda e, h=h: e.activation(out=PBF.t[:, h, :], in_=SL.t[:, h * 256:(h + 1) * 256],
                                                            func=AF.Exp, bias=negm.t[:, h:h + 1], scale=1.0,
                                                            accum_out=rs.t[:, h:h + 1]),
                         R=[SL.b, negm.b], W=[PBF.b, rs.b])
                S.op("dve", lambda e: e.tensor_tensor(out=es.t[:], in0=SNK.t[:], in1=mx.t[:], op=ALU.subtract),
                     R=[SNK.b, mx.b], W=[es.b])
                S.op("act", lambda e: e.activation(out=es.t[:], in_=es.t[:], func=AF.Exp), R=[es.b], W=[es.b])
                S.op("dve", lambda e: e.tensor_tensor(out=es.t[:], in0=es.t[:], in1=rs.t[:], op=ALU.add),
                     R=[es.b, rs.b], W=[es.b])
                S.op("dve", lambda e: e.reciprocal(out=rden.t[:], in_=es.t[:]), R=[es.b], W=[rden.b])
                for rnd in range(2):
                    for i in range(8):
                        idx = rnd * 8 + i
                        h, half = idx // 2, idx % 2
                        S.op("pe", lambda e, i=i, h=h, half=half: e.transpose(
                            out=PT.t[:, i * 128:(i + 1) * 128], in_=PBF.t[:, h, half * 128:(half + 1) * 128],
                            identity=identb.t[:]), R=[PBF.b, identb.b], W=[PT.b])
                    if rnd == 0:
                        S.op("act", lambda e: e.activation(out=PTT.t[:, 0:8, :].rearrange("p a b -> p (a b)"),
                                                           in_=PT.t[:], func=AF.Copy), R=[PT.b], W=[PTT.b])
                    else:
                        S.op("dve", lambda e: e.tensor_copy(out=PTT.t[:, 8:16, :].rearrange("p a b -> p (a b)"),
                                                            in_=PT.t[:]), R=[PT.b], W=[PTT.b])
                for h in range(8):
                    kv = h // 4
                    for half in range(2):
                        S.op("pe", lambda e, h=h, kv=kv, half=half: e.matmul(
                            PB[4].t[:, h * 64:(h + 1) * 64], lhsT=PTT.t[:, 2 * h + half, :],
                            rhs=SVW.t[:, half, kv * 64:(kv + 1) * 64], start=(half == 0), stop=(half == 1)),
                            R=[PTT.b, SVW.b], W=[PB[4].b])
                S.op("dve", lambda e: e.tensor_tensor(
                    out=MIX.t[:, 512:1024].rearrange("p (h d) -> p h d", h=8),
                    in0=PB[4].t[:].rearrange("p (h d) -> p h d", h=8),
                    in1=rden.t[:].unsqueeze(2).to_broadcast([128, 8, 64]), op=ALU.mult),
                    R=[PB[4].b, rden.b], W=[MIX.b])

            def tail_part(c):
                xt = XT[c % 2]
                roll()
                if stage == 2:
                    S.op("dve", lambda e: e.tensor_copy(out=FA.t[:, 0:1024], in_=MIX.t[:]), R=[MIX.b], W=[FA.b])
                    S.dma("sp", lambda e: e.dma_start(out=dbg_d[c - NCH], in_=FA.t[:]), R=[FA.b], key="FA")
                    return
                for kc in range(8):
                    S.op("pe", lambda e, kc=kc: e.transpose(out=PT.t[:, kc * 128:(kc + 1) * 128],
                                                            in_=MIX.t[:, kc * 128:(kc + 1) * 128],
                                                            identity=identb.t[:]), R=[MIX.b, identb.b], W=[PT.b])
                S.op("act", lambda e: e.activation(out=MIXT.t[:].rearrange("p a b -> p (a b)"), in_=PT.t[:],
                                                   func=AF.Copy), R=[PT.b], W=[MIXT.b])
                for nh in range(2):
                    pb = PB[nh]
                    for kc in range(8):
                        S.op("pe", lambda e, kc=kc, nh=nh, pb=pb: e.matmul(
                            pb.t[:], lhsT=MIXT.t[:, kc, :], rhs=WOUT.t[:, kc, nh * 512:(nh + 1) * 512],
                            start=(kc == 0), stop=(kc == 7)), R=[MIXT.b, WOUT.b], W=[pb.b])
                    S.op("dve", lambda e, nh=nh, pb=pb: e.tensor_tensor(
                        out=xt.t[:, nh * 512:(nh + 1) * 512], in0=pb.t[:], in1=xt.t[:, nh * 512:(nh + 1) * 512],
                        op=ALU.add), R=[pb.b, xt.b], W=[xt.b])
                S.dma("sp", lambda e: e.dma_start(out=h1_d[(c - NCH) * 128:(c - NCH + 1) * 128, :], in_=xt.t[:]),
                      R=[xt.b], key=xt.b.name)
                if stage == 3:
                    S.dma("sp", lambda e: e.dma_start(out=dbg_d[c - NCH, :, 0:1024], in_=xt.t[:]), R=[xt.b],
                          key=xt.b.name)

            def roll():
                S.op("pool", lambda e: e.tensor_copy(out=SKW.t[:, 0:128], in_=SKW.t[:, 128:256]),
                     R=[SKW.b], W=[SKW.b])
                S.op("pool", lambda e: e.tensor_copy(out=SVW.t[:, 0, :], in_=SVW.t[:, 1, :]),
                     R=[SVW.b], W=[SVW.b])

            chs = range(2 * NCH)
            if os.environ.get("KCH"):
                chs = [int(v) for v in os.environ["KCH"].split(",")]
            p0_done = 0
            chs = list(chs)
            norm1(chs[0])
            for ci, c in enumerate(chs):
                chunk1(c)
                if ci + 1 < len(chs):
                    norm1(chs[ci + 1])
                if stage >= 2:
                    chunk1b(c)
                if stage >= 4:
                    want = (NP0 * (ci + 1)) // len(chs)
                    while p0_done < want:
                        p0_step(p0_done)
                        p0_done += 1
            S.barrier()

        if stage >= 4:
            p2 = ExitStack()
            with p2:
                def sb2(shape, dt, name=None):
                    return sb(shape, dt, st=p2, name=name)

                WQ = sb2([128, 8, 2048], BF16, "WQ")
                SKT = sb2([128, 16, 128], BF16, "SKT")
                wq_v = wq.rearrange("(kc p) n -> p kc n", p=128)
                load_cast(lambda i: WQ.t[:, i, :], lambda i: wq_v[:, i, :], 8, 2048, wb=WQ.b)
                skt_v = skt.rearrange("f c k -> c f k")
                S.dma("sp", lambda e: e.dma_start(out=FA.t[:, 0:2048].rearrange("p (f k) -> p f k", f=16),
                                                  in_=skt_v), W=[FA.b], key="FA")
                S.op("dve", lambda e: e.tensor_copy(out=SKT.t[:].rearrange("p a b -> p (a b)"),
                                                    in_=FA.t[:, 0:2048]), R=[FA.b], W=[SKT.b])
                G2B = sb2([128, D], F32, "G2B")
                FGB = sb2([128, D], F32, "FGB")
                S.dma("sp", lambda e: e.dma_start(out=G2B.t[:], in_=g2b), W=[G2B.b], key="G2B")
                S.dma("sp", lambda e: e.dma_start(out=FGB.t[:], in_=fgb), W=[FGB.b], key="FGB")

                XT2 = [sb2([128, D], F32, "XT2%d" % i) for i in range(3)]
                XNs = [sb2([128, D], F32, "XN%d" % i) for i in range(2)]
                XNB = sb2([128, D], BF16, "XNB")
                XNT = sb2([128, 8, 128], BF16, "XNT")
                QT = sb2([128, 16, 128], BF16, "QT")
                JD2 = sb2([128, D], BF16, "JD2")
                FC = sb2([128, 2048], F32, "FC")
                FDD = sb2([128, 2048], F32, "FDD")
                ss2 = sb2([128, 1], F32, "ss2")
                sd2 = sb2([128, 1], F32, "sd2")
                rstd2 = sb2([128, 1], F32, "rstd2")
                ss3 = sb2([128, 1], F32, "ss3")
                sd3 = sb2([128, 1], F32, "sd3")
                rstd3 = sb2([128, 1], F32, "rstd3")
                HS = sb2([128, 16, 16], F32, "HS")
                HI = sb2([128, 16, 16], U32, "HI")
                HIF = sb2([128, 16, 16], F32, "HIF")
                HI0 = sb2([128, 8, 16], F32, "HI0")
                TS = sb2([128, 8, 16], F32, "TS")
                PI = sb2([128, 8, 16], U32, "PI")
                PA = sb2([128, 8, 16], U32, "PA")
                PBI = sb2([128, 8, 16], U32, "PBI")
                PAF = sb2([128, 8, 16], F32, "PAF")
                PXBF = sb2([128, 8, 16], F32, "PXBF")
                I1F = sb2([128, 8, 16], F32, "I1F")
                I2F = sb2([128, 8, 16], F32, "I2F")
                IOTA = sb2([128, 256], F32, "IOTA")
                S.dma("sp", lambda e: e.dma_start(out=IOTA.t[:], in_=iota_d), W=[IOTA.b], key="IOTA")
                EIF = sb2([128, 128], F32, "EIF")
                EIIs = [sb2([128, 128], I32, "EII%d" % i) for i in range(2)]
                GEs = [sb2([128, 128], F32, "GE%d" % i) for i in range(2)]
                gz = sb2([128, 8], F32, "gz")
                HCR = [sb2([128, 1], F32, "HC%d" % i) for i in range(4)]
                AAR = [sb2([128, 1], F32, "AA%d" % i) for i in range(4)]
                DG = [sb2([128, 128], BF16, "DG%d" % i) for i in range(4)]
                GT = [sb2([128, 2 * D], BF16, "GT%d" % i) for i in range(NSLOT)]
                gcount = [0]
                USE_F32R = os.environ.get("KF32R", "0") == "1"

                def mmview(ap):
                    return ap.bitcast(mybir.dt.float32r) if USE_F32R else ap

                def front(c):
                    xt = XT2[c % 3]
                    XN = XNs[c % 2]
                    EII = EIIs[c % 2]
                    GE = GEs[c % 2]
                    S.dma("sp", lambda e: e.dma_start(out=xt.t[:], in_=h1_d[c * 128:(c + 1) * 128, :]), W=[xt.b],
                          key=xt.b.name)
                    S.op("dve", lambda e: e.scalar_tensor_tensor(out=JD2.t[:], in0=xt.t[:], scalar=1.0, in1=xt.t[:],
                                                                 op0=ALU.mult, op1=ALU.mult, accum_out=ss2.t[:]),
                         R=[xt.b], W=[JD2.b, ss2.b])
                    S.op("dve", lambda e: e.tensor_scalar(out=sd2.t[:], in0=ss2.t[:], scalar1=1.0 / D, scalar2=EPS,
                                                          op0=ALU.mult, op1=ALU.add), R=[ss2.b], W=[sd2.b])
                    S.op("act", lambda e: e.activation(out=sd2.t[:], in_=sd2.t[:], func=AF.Sqrt), R=[sd2.b],
                         W=[sd2.b])
                    yield
                    yield
                    S.op("dve", lambda e: e.reciprocal(out=rstd2.t[:], in_=sd2.t[:]), R=[sd2.b], W=[rstd2.b])
                    S.op("dve", lambda e: e.scalar_tensor_tensor(out=XN.t[:], in0=xt.t[:], scalar=rstd2.t[:],
                                                                 in1=G2B.t[:], op0=ALU.mult, op1=ALU.mult),
                         R=[xt.b, rstd2.b, G2B.b], W=[XN.b])
                    S.op("act", lambda e: e.activation(out=XNB.t[:], in_=XN.t[:], func=AF.Copy), R=[XN.b], W=[XNB.b])
                    yield
                    for kc in range(8):
                        S.op("pe", lambda e, kc=kc: e.transpose(out=PT.t[:, kc * 128:(kc + 1) * 128],
                                                                in_=XNB.t[:, kc * 128:(kc + 1) * 128],
                                                                identity=identb.t[:]), R=[XNB.b, identb.b], W=[PT.b])
                    S.op("act", lambda e: e.activation(out=XNT.t[:].rearrange("p a b -> p (a b)"), in_=PT.t[:],
                                                       func=AF.Copy), R=[PT.b], W=[XNT.b])
                    yield
                    for f in range(16):
                        pb = PB[(f // 4) % 3]
                        for kc in range(8):
                            S.op("pe", lambda e, f=f, kc=kc, pb=pb: e.matmul(
                                pb.t[:, (f % 4) * 128:(f % 4 + 1) * 128], lhsT=WQ.t[:, kc, f * 128:(f + 1) * 128],
                                rhs=XNT.t[:, kc, :], start=(kc == 0), stop=(kc == 7)),
                                R=[WQ.b, XNT.b], W=[pb.b])
                        if f % 4 == 3:
                            g = f // 4
                            S.op("act", lambda e, g=g, pb=pb: e.activation(
                                out=QT.t[:, 4 * g:4 * g + 4, :].rearrange("p a b -> p (a b)"), in_=pb.t[:],
                                func=AF.Copy), R=[pb.b], W=[QT.b])
                        yield
                    for _ in range(4):
                        yield
                    for f in range(16):
                        pb = PB[(f // 4) % 3]
                        S.op("pe", lambda e, f=f, pb=pb: e.matmul(
                            pb.t[:, (f % 4) * 128:(f % 4 + 1) * 128], lhsT=QT.t[:, f, :], rhs=SKT.t[:, f, :],
                            start=True, stop=True), R=[QT.b, SKT.b], W=[pb.b])
                        if f % 4 == 3:
                            g = f // 4
                            S.op("act", lambda e, g=g, pb=pb: e.activation(
                                out=FA.t[:, g * 512:(g + 1) * 512], in_=pb.t[:], func=AF.Copy),
                                R=[pb.b], W=[FA.b])
                            yield
                    for _ in range(int(os.environ.get("KSPACE", "10"))):
                        yield
                    for f in range(16):
                        sc = FA.t[:, f * 128:(f + 1) * 128]
                        wk = FDD.t[:, f * 128:(f + 1) * 128]
                        S.op("dve", lambda e, f=f, sc=sc: e.max(out=HS.t[:, f, 0:8], in_=sc), R=[FA.b], W=[HS.b])
                        S.op("dve", lambda e, f=f, sc=sc, wk=wk: e.match_replace(
                            out=wk, in_to_replace=HS.t[:, f, 0:8], in_values=sc, imm_value=-1e30),
                            R=[FA.b, HS.b], W=[FDD.b])
                        yield
                        S.op("dve", lambda e, f=f, wk=wk: e.max(out=HS.t[:, f, 8:16], in_=wk), R=[FDD.b], W=[HS.b])
                        S.op("dve", lambda e, f=f, sc=sc: e.max_index(out=HI.t[:, f, 0:8], in_max=HS.t[:, f, 0:8],
                                                                      in_values=sc), R=[FA.b, HS.b], W=[HI.b])
                        yield
                        S.op("dve", lambda e, f=f, wk=wk: e.max_index(out=HI.t[:, f, 8:16], in_max=HS.t[:, f, 8:16],
                                                                      in_values=wk), R=[FDD.b, HS.b], W=[HI.b])
                        yield
                    S.op("dve", lambda e: e.tensor_copy(out=HIF.t[:], in_=HI.t[:]), R=[HI.b], W=[HIF.b])
                    hs4 = HS.t[:].rearrange("p (h a) k -> p h a k", a=2)
                    hif4 = HIF.t[:].rearrange("p (h a) k -> p h a k", a=2)
                    cs4 = FB.t[:, 0:2048].rearrange("p (h a b) -> p h a b", h=8, a=16)
                    ci4 = FC.t[:].rearrange("p (h a b) -> p h a b", h=8, a=16)
                    S.op("dve", lambda e: e.tensor_tensor(
                        out=cs4, in0=hs4[:, :, 0, :].unsqueeze(3).to_broadcast([128, 8, 16, 16]),
                        in1=hs4[:, :, 1, :].unsqueeze(2).to_broadcast([128, 8, 16, 16]), op=ALU.add),
                        R=[HS.b], W=[FB.b])
                    yield
                    for h in range(8):
                        cs = FB.t[:, h * 256:(h + 1) * 256]
                        wk = FDD.t[:, h * 256:(h + 1) * 256]
                        S.op("dve", lambda e, h=h, cs=cs: e.max(out=TS.t[:, h, 0:8], in_=cs), R=[FB.b], W=[TS.b])
                        S.op("dve", lambda e, h=h, cs=cs: e.max_index(out=PI.t[:, h, 0:8], in_max=TS.t[:, h, 0:8],
                                                                      in_values=cs), R=[FB.b, TS.b], W=[PI.b])
                        S.op("dve", lambda e, h=h, cs=cs, wk=wk: e.match_replace(
                            out=wk, in_to_replace=TS.t[:, h, 0:8], in_values=cs, imm_value=-1e30),
                            R=[FB.b, TS.b], W=[FDD.b])
                        yield
                        S.op("dve", lambda e, h=h, wk=wk: e.max(out=TS.t[:, h, 8:16], in_=wk), R=[FDD.b], W=[TS.b])
                        S.op("dve", lambda e, h=h, wk=wk: e.max_index(out=PI.t[:, h, 8:16], in_max=TS.t[:, h, 8:16],
                                                                      in_values=wk), R=[FDD.b, TS.b], W=[PI.b])
                        yield
                    S.op("dve", lambda e: e.tensor_scalar(out=PA.t[:], in0=PI.t[:], scalar1=4, scalar2=None,
                                                          op0=ALU.logical_shift_right), R=[PI.b], W=[PA.b])
                    S.op("dve", lambda e: e.tensor_scalar(out=PBI.t[:], in0=PI.t[:], scalar1=15, scalar2=None,
                                                          op0=ALU.bitwise_and), R=[PI.b], W=[PBI.b])
                    yield
                    S.op("dve", lambda e: e.tensor_copy(out=PAF.t[:], in_=PA.t[:]), R=[PA.b], W=[PAF.b])
                    S.op("dve", lambda e: e.tensor_copy(out=PXBF.t[:], in_=PBI.t[:]), R=[PBI.b], W=[PXBF.b])
                    yield
                    io4 = IOTA.t[:, 0:16].unsqueeze(1).unsqueeze(1).to_broadcast([128, 8, 16, 16])
                    eq4 = FDD.t[:].rearrange("p (h k a) -> p h k a", h=8, k=16)
                    for half, (pf, dst) in enumerate(((PAF, I1F), (PXBF, I2F))):
                        S.op("dve", lambda e, pf=pf: e.tensor_tensor(
                            out=eq4, in0=io4, in1=pf.t[:].unsqueeze(3).to_broadcast([128, 8, 16, 16]),
                            op=ALU.is_equal), R=[IOTA.b, pf.b], W=[FDD.b])
                        yield
                        S.op("dve", lambda e, half=half: e.tensor_tensor(
                            out=eq4, in0=eq4,
                            in1=hif4[:, :, half, :].unsqueeze(2).to_broadcast([128, 8, 16, 16]), op=ALU.mult),
                            R=[FDD.b, HIF.b], W=[FDD.b])
                        yield
                        S.op("dve", lambda e, dst=dst: e.tensor_reduce(out=dst.t[:], in_=eq4, axis=AX.X, op=ALU.add),
                             R=[FDD.b], W=[dst.b])
                        yield
                    S.op("dve", lambda e: e.scalar_tensor_tensor(
                        out=EIF.t[:].rearrange("p (h k) -> p h k", h=8), in0=I1F.t[:], scalar=128.0, in1=I2F.t[:],
                        op0=ALU.mult, op1=ALU.add), R=[I1F.b, I2F.b], W=[EIF.b])
                    S.op("dve", lambda e: e.tensor_copy(out=EII.t[:], in_=EIF.t[:]), R=[EIF.b], W=[EII.b])
                    ge3 = GE.t[:].rearrange("p (h k) -> p h k", h=8)
                    S.op("dve", lambda e: e.tensor_tensor(
                        out=ge3, in0=TS.t[:], in1=TS.t[:, :, 0:1].to_broadcast([128, 8, 16]), op=ALU.subtract),
                        R=[TS.b], W=[GE.b])
                    S.op("act", lambda e: e.activation(out=GE.t[:], in_=GE.t[:], func=AF.Exp), R=[GE.b], W=[GE.b])
                    yield
                    S.op("dve", lambda e: e.tensor_reduce(out=gz.t[:], in_=ge3, axis=AX.X, op=ALU.add),
                         R=[GE.b], W=[gz.b])
                    S.op("dve", lambda e: e.reciprocal(out=gz.t[:], in_=gz.t[:]), R=[gz.b], W=[gz.b])
                    S.op("dve", lambda e: e.tensor_tensor(out=ge3, in0=ge3,
                                                          in1=gz.t[:].unsqueeze(2).to_broadcast([128, 8, 16]),
                                                          op=ALU.mult), R=[GE.b, gz.b], W=[GE.b])
                    yield

                def accb(c):
                    return (PB[5], PB[6]) if c % 2 == 0 else (PB[3], PB[4])

                def back(c, pump, prev_tail):
                    xt = XT2[c % 3]
                    ACB = accb(c)
                    XN = XNs[c % 2]
                    EII = EIIs[c % 2]
                    GE = GEs[c % 2]
                    for s_ in range(128):
                        gt = GT[gcount[0] % NSLOT]
                        gcount[0] += 1
                        hc = HCR[s_ % 4]
                        aa = AAR[s_ % 4]
                        dg = DG[s_ % 4]
                        S.dma("pool", lambda e, gt=gt, s_=s_: e.indirect_dma_start(
                            out=gt.t[:], out_offset=None, in_=uvb_d,
                            in_offset=bass.IndirectOffsetOnAxis(ap=EII.t[:, s_:s_ + 1], axis=0)),
                            R=[EII.b], W=[gt.b], key=gt.b.name)
                        S.op("dve", lambda e, gt=gt, hc=hc: e.scalar_tensor_tensor(
                            out=JD2.t[:], in0=gt.t[:, 0:D], scalar=1.0, in1=XN.t[:], op0=ALU.mult,
                            op1=ALU.mult, accum_out=hc.t[:]), R=[gt.b, XN.b], W=[JD2.b, hc.b])
                        S.op("act", lambda e, hc=hc, aa=aa: e.activation(out=aa.t[:], in_=hc.t[:], func=AF.Gelu),
                             R=[hc.b], W=[aa.b])
                        S.op("act", lambda e, aa=aa, s_=s_: e.activation(out=aa.t[:], in_=aa.t[:], func=AF.Copy,
                                                                         scale=GE.t[:, s_:s_ + 1]),
                             R=[aa.b, GE.b], W=[aa.b])
                        S.op("act", lambda e, dg=dg, aa=aa: e.activation(out=dg.t[:], in_=identf.t[:], func=AF.Copy,
                                                                         scale=aa.t[:]),
                             R=[identf.b, aa.b], W=[dg.b])
                        for nh in range(2):
                            S.op("pe", lambda e, gt=gt, dg=dg, s_=s_, nh=nh: e.matmul(
                                ACB[nh].t[:], lhsT=dg.t[:],
                                rhs=gt.t[:, D + nh * 512:D + (nh + 1) * 512],
                                start=(s_ == 0), stop=(s_ == 127)), R=[dg.b, gt.b], W=[ACB[nh].b])
                        pump(int(os.environ.get("KPUMP", "1")))
                        if s_ == 114 and prev_tail is not None:
                            prev_tail[0]()
                        if s_ == 119 and prev_tail is not None:
                            prev_tail[1]()
                    for _ in range(2):
                        S.op("pe", lambda e: e.transpose(out=PT.t[:, 0:128], in_=identb.t[:], identity=identb.t[:]),
                             R=[identb.b], W=[PT.b, ACB[0].b, ACB[1].b])

                def tail(c):
                    xt = XT2[c % 3]
                    ACB = accb(c)
                    for nh in range(2):
                        S.op("dve", lambda e, nh=nh: e.tensor_tensor(
                            out=xt.t[:, nh * 512:(nh + 1) * 512], in0=ACB[nh].t[:],
                            in1=xt.t[:, nh * 512:(nh + 1) * 512], op=ALU.add), R=[xt.b, ACB[nh].b], W=[xt.b])
                    S.op("dve", lambda e: e.scalar_tensor_tensor(out=JD2.t[:], in0=xt.t[:], scalar=1.0, in1=xt.t[:],
                                                                 op0=ALU.mult, op1=ALU.mult, accum_out=ss3.t[:]),
                         R=[xt.b], W=[JD2.b, ss3.b])
                    S.op("dve", lambda e: e.tensor_scalar(out=sd3.t[:], in0=ss3.t[:], scalar1=1.0 / D, scalar2=EPS,
                                                          op0=ALU.mult, op1=ALU.add), R=[ss3.b], W=[sd3.b])
                    S.op("act", lambda e: e.activation(out=sd3.t[:], in_=sd3.t[:], func=AF.Sqrt), R=[sd3.b],
                         W=[sd3.b])

                def tail_b(c):
                    xt = XT2[c % 3]
                    S.op("dve", lambda e: e.reciprocal(out=rstd3.t[:], in_=sd3.t[:]), R=[sd3.b], W=[rstd3.b])
                    S.op("dve", lambda e: e.scalar_tensor_tensor(out=xt.t[:], in0=xt.t[:], scalar=rstd3.t[:],
                                                                 in1=FGB.t[:], op0=ALU.mult, op1=ALU.mult),
                         R=[xt.b, rstd3.b, FGB.b], W=[xt.b])
                    S.dma("sp", lambda e: e.dma_start(out=out_d[c * 128:(c + 1) * 128, :], in_=xt.t[:]), R=[xt.b],
                          key=xt.b.name)

                chs2 = list(range(NCH))
                if os.environ.get("KCH2"):
                    chs2 = [int(v) for v in os.environ["KCH2"].split(",")]
                for _ in front(chs2[0]):
                    pass
                prev_tail = None
                for i, c in enumerate(chs2):
                    gen = front(chs2[i + 1]) if i + 1 < len(chs2) else iter(())

                    def pump(n, gen=gen):
                        for _ in range(n):
                            next(gen, None)
                    back(c, pump, prev_tail)
                    for _ in gen:
                        pass
                    prev_tail = ((lambda c=c: tail(c)), (lambda c=c: tail_b(c)))
                if prev_tail is not None:
                    prev_tail[0]()
                    prev_tail[1]()
                S.barrier()
        if stage < 4:
            pass
        S.final()
        S.emit()
    return nc


def t5_bucket(rel):
    max_exact = 16
    n = np.maximum(rel, 0)
    large = max_exact + (np.log(np.maximum(n, 1).astype(np.float32) / max_exact)
                         / math.log(128 / max_exact) * (32 - max_exact)).astype(np.int32)
    large = np.minimum(large, 31)
    return np.where(n < max_exact, n, large).astype(np.int32)


def host_consts():
    f32 = np.float32
    lg = np.array(LOGG, dtype=f32)
    idx = np.arange(128, dtype=f32)
    diff = idx[:, None] - idx[None, :]
    dm = np.where(diff[None] >= 0, np.exp(np.maximum(diff, 0.0)[None] * lg[:, None, None]), 0.0).astype(f32)
    dmt = np.ascontiguousarray(dm.transpose(2, 0, 1)).reshape(128, 512)
    xi = np.exp((idx + 1.0)[None, :] * lg[:, None]).astype(f32)
    xit = np.zeros((128, 2, 128), f32)
    for h in range(4):
        xit[(h % 2) * 64:(h % 2 + 1) * 64, h // 2, :] = xi[h][None, :]
    zeta = np.exp((127.0 - idx)[None, :] * lg[:, None]).astype(f32).T
    W = 128
    rel = np.arange(W)[:, None] + W - np.arange(2 * W)[None, :]
    band = (rel >= 0) & (rel < W)
    mask = np.where(band, 0.0, NEG).astype(f32)
    bucket = t5_bucket(rel)
    return dict(dmt=dmt, xit=xit.reshape(128, 256), zeta=np.ascontiguousarray(zeta), mask=mask, bucket=bucket)


def rot_table(pos0):
    inv = (1.0 / (10000.0 ** (np.arange(0, 64, 2, dtype=np.float32) / 64))).astype(np.float32)
    pos = (pos0 + np.arange(2048)).astype(np.float32)
    ang = pos[:, None] * inv[None, :]
    cos = np.cos(ang).astype(np.float32)
    sin = np.sin(ang).astype(np.float32)
    c2 = np.concatenate([cos, cos], axis=1)
    s2 = np.concatenate([-sin, sin], axis=1)
    t = np.concatenate([c2, c2 * np.float32(0.125), s2, s2 * np.float32(0.125)], axis=1)
    return t.reshape(16, 128, 256)


_NC_CACHE = {}


def prep(x, norm1_g, w_in, ret_gn_g, swa_sinks, rel_bias, w_out, norm2_g, peer_wq, peer_subkeys, peer_u,
         peer_v, final_g, stage=99):
    f32 = np.float32
    x = np.asarray(x, f32)
    hc = host_consts()
    rb = np.asarray(rel_bias, f32)
    bias = rb[hc["bucket"]]
    bias = np.ascontiguousarray(bias.transpose(0, 2, 1)).reshape(128, 2048)
    rep = lambda v, n: np.ascontiguousarray(np.broadcast_to(np.asarray(v, f32).reshape(1, n), (128, n)))
    common = dict(
        w_in=np.ascontiguousarray(np.asarray(w_in, f32)[0]),
        g1=np.ascontiguousarray(np.asarray(norm1_g, f32)[0].reshape(8, 128).T),
        dmt=hc["dmt"], xit=hc["xit"], zeta=hc["zeta"],
        gnb=rep(np.asarray(ret_gn_g)[0], 512), sinkb=rep(np.asarray(swa_sinks)[0], 8),
        bias=bias, mask=hc["mask"], ident=np.eye(128, dtype=f32),
        w_out=np.ascontiguousarray(np.asarray(w_out, f32)[0]),
        g2b=rep(np.asarray(norm2_g)[0], D),
        wq=np.ascontiguousarray(np.asarray(peer_wq, f32)[0]),
        skt=np.ascontiguousarray(np.asarray(peer_subkeys, f32)[0].reshape(16, 128, 128).transpose(0, 2, 1)),
        uv=np.ascontiguousarray(np.concatenate([np.asarray(peer_u, f32)[0], np.asarray(peer_v, f32)[0]], axis=1)),
        fgb=rep(final_g, D),
        iota=np.ascontiguousarray(np.broadcast_to(np.arange(256, dtype=f32).reshape(1, 256), (128, 256))),
    )
    rot_lo = rot_table(0)
    rot_hi = rot_table(2048)
    if stage < 4:
        common.pop("uv")
        for nm in ("g2b", "wq", "skt", "fgb"):
            pass
    in_maps = []
    for k in range(NCORES):
        b, half = k // 2, k % 2
        own = x[b, half * TOK:(half + 1) * TOK]
        if half == 0:
            pre = np.zeros_like(own)
            rot = np.concatenate([rot_lo, rot_lo], axis=0)
            mask0 = np.zeros((128, 256), f32)
            mask0[:, 0:128] = NEG
        else:
            pre = x[b, 0:TOK]
            rot = np.concatenate([rot_lo, rot_hi], axis=0)
            mask0 = np.zeros((128, 256), f32)
        m = dict(common)
        m["xs"] = np.ascontiguousarray(np.concatenate([pre, own], axis=0))
        m["rot"] = np.ascontiguousarray(rot)
        m["mask0"] = mask0
        in_maps.append(m)
    return in_maps


def kernel(_stage=None, **inputs):
    stage = 99 if _stage is None else _stage
    in_maps = prep(stage=stage, **inputs)
    f32 = np.float32
    if stage not in _NC_CACHE:
        _NC_CACHE[stage] = build(stage)
    nc = _NC_CACHE[stage]
    if os.environ.get("KTRACE"):
        res = run_bass_kernel_spmd(nc, in_maps, core_ids=list(range(NCORES)), trace=True)
        print("EXEC_TIME_NS", res.exec_time_ns)
    else:
        res = run_bass_kernel_spmd(nc, in_maps, core_ids=list(range(NCORES)))
    if stage < 99:
        return [r["dbg"] for r in res.results]
    outs = [np.asarray(r["out"], f32) for r in res.results]
    return np.stack(outs, axis=0).reshape(4, 4096, D)
```

```python
import os
import math
from contextlib import ExitStack

import numpy as np
import concourse.bass as bass
import concourse.mybir as mybir
from concourse.bass_utils import run_bass_kernel_spmd

F32 = mybir.dt.float32
BF16 = mybir.dt.bfloat16
I32 = mybir.dt.int32
U32 = mybir.dt.uint32
AF = mybir.ActivationFunctionType
ALU = mybir.AluOpType
AX = mybir.AxisListType

NCORES = 8
D = 1024
TOK = 2048
NCH = 16
DIN = 2304
NEG = -30000.0
EPS = 1e-6
LOGG = [math.log(1.0 - 2.0 ** (-5.0 - h)) for h in range(4)]
CDEC = [math.exp(128.0 * lg) for lg in LOGG]
NSLOT = 12


class Buf:
    __slots__ = ("name", "w", "r", "excl")

    def __init__(self, name=""):
        self.name = name
        self.w = None
        self.r = {}
        self.excl = False


class Tl:
    def __init__(self, t, name):
        self.t = t
        self.b = Buf(name)


class Sched:
    ENG = ["pe", "act", "dve", "pool", "sp"]

    def __init__(self, nc, stack):
        self.nc = nc
        self.stack = stack
        self.cnt = {e: 0 for e in self.ENG}
        self.ops = {e: [] for e in self.ENG}
        self.waited = {e: {} for e in self.ENG}
        self.dma_cnt = {}
        self.sems = {}
        for e in self.ENG:
            self.sems[e] = stack.enter_context(nc.semaphore("s_" + e))

    def sem(self, key):
        if key not in self.sems:
            self.sems[key] = self.stack.enter_context(nc_semaphore(self.nc, "d%d" % len(self.sems)))
        return self.sems[key]

    def _need(self, eng, key, val):
        if val <= 0:
            return
        if self.waited[eng].get(key, 0) >= val:
            return
        self.waited[eng][key] = val
        self.ops[eng].append(("wait", key, val))

    def op(self, eng, fn, R=(), W=()):
        for b in R:
            if b.w is not None:
                self._need(eng, *b.w)
            if b.excl:
                for k, v in b.r.items():
                    if k != eng:
                        self._need(eng, k, v)
        strict = eng != "pe"
        for b in W:
            if b.w is not None and (strict or b.w[0] != eng):
                self._need(eng, *b.w)
            for k, v in b.r.items():
                if strict or k != eng:
                    self._need(eng, k, v)
        self.cnt[eng] += 1
        c = self.cnt[eng]
        self.ops[eng].append(("op", fn, eng, 1))
        for b in R:
            b.r[eng] = c
        for b in W:
            b.w = (eng, c)
            b.r = {}

    def dma(self, q, fn, R=(), W=(), key=None):
        key = "dma:" + key
        self.sem(key)
        prev = self.dma_cnt.get(key, 0)
        self._need(q, key, prev)
        for b in R:
            if b.w is not None and b.w[0] != key:
                self._need(q, *b.w)
        for b in W:
            if b.w is not None and b.w[0] != key:
                self._need(q, *b.w)
            for k, v in b.r.items():
                if k != key:
                    self._need(q, k, v)
        cur = prev + 16
        self.dma_cnt[key] = cur
        self.ops[q].append(("op", fn, key, 16))
        for b in R:
            b.r[key] = cur
        for b in W:
            b.w = (key, cur)
            b.r = {}

    def barrier(self):
        for e in self.ENG:
            for e2 in self.ENG:
                if e2 != e:
                    self._need(e, e2, self.cnt[e2])
            for k, v in self.dma_cnt.items():
                self._need(e, k, v)

    def final(self):
        for k, v in self.dma_cnt.items():
            self._need("sp", k, v)
        for e2 in self.ENG:
            if e2 != "sp":
                self._need("sp", e2, self.cnt[e2])

    def emit(self):
        nc = self.nc
        with nc.Block() as blk:
            for eng, deco in (("sp", blk.sync), ("act", blk.scalar), ("dve", blk.vector),
                              ("pool", blk.gpsimd), ("pe", blk.tensor)):
                def body(e, eng=eng):
                    for it in self.ops[eng]:
                        if it[0] == "wait":
                            e.wait_ge(self.sems[it[1]], it[2])
                        else:
                            it[1](e).then_inc(self.sems[it[2]], it[3])
                deco(body)


def nc_semaphore(nc, name):
    return nc.semaphore(name)


def build(stage=99):
    nc = bass.Bass("TRN2", target_bir_lowering=False)

    def din(name, shape, dt=F32):
        return nc.dram_tensor(name, list(shape), dt, kind="ExternalInput").ap()

    xs = din("xs", [2 * TOK, D])
    w_in = din("w_in", [D, DIN])
    g1 = din("g1", [128, 8])
    rot = din("rot", [2 * NCH, 128, 256])
    dmt = din("dmt", [128, 512])
    xit = din("xit", [128, 256])
    zeta = din("zeta", [128, 4])
    gnb = din("gnb", [128, 512])
    sinkb = din("sinkb", [128, 8])
    bias = din("bias", [128, 2048])
    mask = din("mask", [128, 256])
    mask0 = din("mask0", [128, 256])
    ident = din("ident", [128, 128])
    w_out = din("w_out", [D, D])
    g2b = din("g2b", [128, D])
    wq = din("wq", [D, 2048])
    skt = din("skt", [16, 128, 128])
    uv_d = din("uv", [16384, 2 * D]) if stage >= 4 else None
    fgb = din("fgb", [128, D])
    iota_d = din("iota", [128, 256])
    out_d = nc.dram_tensor("out", [TOK, D], F32, kind="ExternalOutput").ap()
    h1_d = nc.dram_tensor("h1s", [TOK, D], F32, kind="Internal").ap()
    uvb_d = nc.dram_tensor("uvb", [16384, 2 * D], BF16, kind="Internal").ap() if stage >= 4 else None
    dbg_d = None
    if stage < 99:
        dbg_d = nc.dram_tensor("dbg", [NCH, 128, DIN], F32, kind="ExternalOutput").ap()

    stack = ExitStack()
    with stack:
        S = Sched(nc, stack)
        names = [0]

        def sb(shape, dt, st=stack, name=None):
            names[0] += 1
            nm = name or ("t%d" % names[0])
            return Tl(st.enter_context(nc.sbuf_tensor(nm, list(shape), dt)), nm)

        def ps(shape, dt, name):
            t = Tl(stack.enter_context(nc.psum_tensor(name, list(shape), dt)), name)
            t.b.excl = True
            return t

        PT = ps([128, 1024], BF16, "pt")
        PB = [ps([128, 512], F32, "pb%d" % i) for i in range(1, 8)]

        identf = sb([128, 128], F32)
        identb = sb([128, 128], BF16)
        g1t = sb([128, 8], F32)
        S.dma("sp", lambda e: e.dma_start(out=identf.t[:], in_=ident), W=[identf.b], key="identf")
        S.dma("sp", lambda e: e.dma_start(out=g1t.t[:], in_=g1), W=[g1t.b], key="g1t")
        S.op("dve", lambda e: e.tensor_copy(out=identb.t[:], in_=identf.t[:]), R=[identf.b], W=[identb.b])

        FA = sb([128, DIN], F32, name="FA")
        FB = sb([128, DIN], F32, name="FB")

        def load_cast(dst_ap_fn, src_ap_fn, n, width, scale_fn=None, wb=None):
            for i in range(n):
                stg = FA if i % 2 == 0 else FB
                S.dma("sp", lambda e, i=i, stg=stg: e.dma_start(out=stg.t[:, 0:width], in_=src_ap_fn(i)),
                      W=[stg.b], key=stg.b.name)
                if i % 2 == 0:
                    if scale_fn is None:
                        S.op("act", lambda e, i=i, stg=stg: e.activation(out=dst_ap_fn(i), in_=stg.t[:, 0:width],
                                                                         func=AF.Copy), R=[stg.b], W=[wb])
                    else:
                        S.op("act", lambda e, i=i, stg=stg: e.activation(out=dst_ap_fn(i), in_=stg.t[:, 0:width],
                                                                         func=AF.Copy, scale=scale_fn(i)),
                             R=[stg.b, g1t.b], W=[wb])
                else:
                    if scale_fn is None:
                        S.op("dve", lambda e, i=i, stg=stg: e.tensor_copy(out=dst_ap_fn(i), in_=stg.t[:, 0:width]),
                             R=[stg.b], W=[wb])
                    else:
                        S.op("dve", lambda e, i=i, stg=stg: e.tensor_scalar(out=dst_ap_fn(i), in0=stg.t[:, 0:width],
                                                                            scalar1=scale_fn(i), scalar2=None,
                                                                            op0=ALU.mult), R=[stg.b, g1t.b], W=[wb])

        p1 = ExitStack()
        with p1:
            def sb1(shape, dt, name=None):
                return sb(shape, dt, st=p1, name=name)

            WIN = sb1([128, 8, DIN], BF16, "WIN")
            WOUT = sb1([128, 8, D], BF16, "WOUT")
            w_in_v = w_in.rearrange("(kc p) n -> p kc n", p=128)
            w_out_v = w_out.rearrange("(kc p) n -> p kc n", p=128)
            load_cast(lambda i: WIN.t[:, i, :], lambda i: w_in_v[:, i, :], 8, DIN,
                      scale_fn=lambda i: g1t.t[:, i:i + 1], wb=WIN.b)
            load_cast(lambda i: WOUT.t[:, 2 * i:2 * i + 2, :], lambda i: w_out_v[:, 2 * i:2 * i + 2, :], 4, 2048,
                      wb=WOUT.b)

            def cload(src, shape, name):
                t = sb1(shape, F32, name)
                S.dma("sp", lambda e: e.dma_start(out=t.t[:], in_=src), W=[t.b], key=name)
                return t

            DMT = cload(dmt, [128, 512], "DMT")
            XIT = cload(xit, [128, 256], "XIT")
            ZET = cload(zeta, [128, 4], "ZET")
            GNB = cload(gnb, [128, 512], "GNB")
            SNK = cload(sinkb, [128, 8], "SNK")
            BIA = cload(bias, [128, 2048], "BIA")
            MSK = cload(mask, [128, 256], "MSK")
            MS0 = cload(mask0, [128, 256], "MS0")
            S.op("dve", lambda e: e.tensor_tensor(
                out=BIA.t[:].rearrange("p (h k) -> p h k", h=8),
                in0=BIA.t[:].rearrange("p (h k) -> p h k", h=8),
                in1=MSK.t[:].unsqueeze(1).to_broadcast([128, 8, 256]), op=ALU.add),
                R=[BIA.b, MSK.b], W=[BIA.b])

            XT = [sb1([128, D], F32, "XT%d" % i) for i in range(2)]
            ROT = [sb1([128, 256], F32, "ROT%d" % i) for i in range(2)]
            JD = sb1([128, D], BF16, "JD")
            FD = sb1([128, 1024], F32, "FD")
            ss = sb1([128, 1], F32, "ss")
            sd = sb1([128, 1], F32, "sd")
            rstd = sb1([128, 1], F32, "rstd")
            YB = sb1([128, D], BF16, "YB")
            YT = sb1([128, 8, 128], BF16, "YT")
            RQK = sb1([128, 512], BF16, "RQK")
            RQKT = sb1([128, 4, 128], BF16, "RQKT")
            QXT = sb1([128, 2, 128], BF16, "QXT")
            KZ = sb1([128, 256], BF16, "KZ")
            RVB = sb1([128, 512], BF16, "RVB")
            STB = sb1([128, 4, 128], BF16, "STB")
            STATE = sb1([128, 2, 128], F32, "STATE")
            STATEB = sb1([128, 2, 128], BF16, "STATEB")
            SQB = sb1([128, 640], BF16, "SQB")
            SQT = sb1([128, 4, 128], BF16, "SQT")
            SKW = sb1([128, 256], BF16, "SKW")
            SVW = sb1([128, 2, 128], BF16, "SVW")
            RO = sb1([128, 512], F32, "RO")
            GS = sb1([128, 512], F32, "GS")
            st1 = sb1([128, 4], F32, "st1")
            st2 = sb1([128, 4], F32, "st2")
            mean = sb1([128, 4], F32, "mean")
            var = sb1([128, 4], F32, "var")
            grs = sb1([128, 4], F32, "grs")
            SL = sb1([128, 2048], F32, "SL")
            PBF = sb1([128, 8, 256], BF16, "PBF")
            PTT = sb1([128, 16, 128], BF16, "PTT")
            mx = sb1([128, 8], F32, "mx")
            negm = sb1([128, 8], F32, "negm")
            rs = sb1([128, 8], F32, "rs")
            es = sb1([128, 8], F32, "es")
            rden = sb1([128, 8], F32, "rden")
            MIX = sb1([128, D], BF16, "MIX")
            MIXT = sb1([128, 8, 128], BF16, "MIXT")

            S.op("pool", lambda e: e.memset(STATE.t[:], 0.0), W=[STATE.b])
            S.op("pool", lambda e: e.memset(STATEB.t[:], 0.0), W=[STATEB.b])
            S.op("pool", lambda e: e.memset(SKW.t[:], 0.0), W=[SKW.b])
            S.op("pool", lambda e: e.memset(SVW.t[:], 0.0), W=[SVW.b])

            GROUPS = [(0, 512), (512, 512), (1024, 512), (1536, 512), (2048, 256)]

            NP0 = 64
            if stage >= 4:
                STG = [sb1([128, 2, 2 * D], F32, "STG%d" % i) for i in range(2)]
                BFS = [sb1([128, 2, 2 * D], BF16, "BFS%d" % i) for i in range(2)]
                uv_v = uv_d.rearrange("(n p) d -> p n d", p=128)
                uvb_v = uvb_d.rearrange("(n p) d -> p n d", p=128)

                def p0_load(i):
                    st_ = STG[i % 2]
                    S.dma("pool", lambda e, i=i, st_=st_: e.dma_start(out=st_.t[:], in_=uv_v[:, 2 * i:2 * i + 2, :]),
                          W=[st_.b], key=st_.b.name)

                def p0_step(i):
                    if i == 0:
                        p0_load(0)
                    if i + 1 < NP0:
                        p0_load(i + 1)
                    st_ = STG[i % 2]
                    bf_ = BFS[i % 2]
                    S.op("act", lambda e, st_=st_, bf_=bf_: e.activation(
                        out=bf_.t[:].rearrange("p a b -> p (a b)"), in_=st_.t[:].rearrange("p a b -> p (a b)"),
                        func=AF.Copy), R=[st_.b], W=[bf_.b])
                    S.dma("pool", lambda e, i=i, bf_=bf_: e.dma_start(out=uvb_v[:, 2 * i:2 * i + 2, :], in_=bf_.t[:]),
                          R=[bf_.b], key=bf_.b.name)

            YBs = [YB, sb1([128, D], BF16, "YB1")]
            NST = [(ss, sd, rstd), (sb1([128, 1], F32, "ss_b"), sb1([128, 1], F32, "sd_b"), sb1([128, 1], F32, "rstd_b"))]

            def norm1(c):
                xt = XT[c % 2]
                rt = ROT[c % 2]
                yb = YBs[c % 2]
                ss_, sd_, rstd_ = NST[c % 2]
                S.dma("sp", lambda e: e.dma_start(out=xt.t[:], in_=xs[c * 128:(c + 1) * 128, :]), W=[xt.b],
                      key=xt.b.name)
                S.dma("sp", lambda e: e.dma_start(out=rt.t[:], in_=rot[c]), W=[rt.b], key=rt.b.name)
                S.op("dve", lambda e: e.scalar_tensor_tensor(out=JD.t[:], in0=xt.t[:], scalar=1.0, in1=xt.t[:],
                                                             op0=ALU.mult, op1=ALU.mult, accum_out=ss_.t[:]),
                     R=[xt.b], W=[JD.b, ss_.b])
                S.op("dve", lambda e: e.tensor_scalar(out=sd_.t[:], in0=ss_.t[:], scalar1=1.0 / D, scalar2=EPS,
                                                      op0=ALU.mult, op1=ALU.add), R=[ss_.b], W=[sd_.b])
                S.op("act", lambda e: e.activation(out=sd_.t[:], in_=sd_.t[:], func=AF.Sqrt), R=[sd_.b], W=[sd_.b])
                S.op("dve", lambda e: e.reciprocal(out=rstd_.t[:], in_=sd_.t[:]), R=[sd_.b], W=[rstd_.b])
                S.op("dve", lambda e: e.tensor_scalar(out=yb.t[:], in0=xt.t[:], scalar1=rstd_.t[:], scalar2=None,
                                                      op0=ALU.mult), R=[xt.b, rstd_.b], W=[yb.b])

            def chunk1(c):
                own = c >= NCH
                xt = XT[c % 2]
                rt = ROT[c % 2]
                yb = YBs[c % 2]
                for kc in range(8):
                    S.op("pe", lambda e, kc=kc: e.transpose(out=PT.t[:, kc * 128:(kc + 1) * 128],
                                                            in_=yb.t[:, kc * 128:(kc + 1) * 128],
                                                            identity=identb.t[:]), R=[yb.b, identb.b], W=[PT.b])
                S.op("act", lambda e: e.activation(out=YT.t[:].rearrange("p a b -> p (a b)"), in_=PT.t[:],
                                                   func=AF.Copy), R=[PT.b], W=[YT.b])
                glist = list(range(5)) if own else ([0, 1, 4] if c == NCH - 1 else [0, 1])
                for gi in glist:
                    c0, wd = GROUPS[gi]
                    pb = PB[gi]
                    for kc in range(8):
                        S.op("pe", lambda e, kc=kc, c0=c0, wd=wd, pb=pb: e.matmul(
                            pb.t[:, 0:wd], lhsT=YT.t[:, kc, :], rhs=WIN.t[:, kc, c0:c0 + wd],
                            start=(kc == 0), stop=(kc == 7)), R=[YT.b, WIN.b], W=[pb.b])
                    if gi % 2 == 0:
                        S.op("act", lambda e, c0=c0, wd=wd, pb=pb: e.activation(
                            out=FA.t[:, c0:c0 + wd], in_=pb.t[:, 0:wd], func=AF.Copy), R=[pb.b], W=[FA.b])
                    else:
                        S.op("dve", lambda e, c0=c0, wd=wd, pb=pb: e.tensor_copy(
                            out=FA.t[:, c0:c0 + wd], in_=pb.t[:, 0:wd]), R=[pb.b], W=[FA.b])
                if stage == 1:
                    if own:
                        S.dma("sp", lambda e: e.dma_start(out=dbg_d[c - NCH], in_=FA.t[:]), R=[FA.b], key="FA")
                    return
                qk = FA.t[:, 0:512].rearrange("p (a h d) -> p a h d", a=2, h=4)
                cosb = rt.t[:, 0:128].rearrange("p (a d) -> p a d", a=2).unsqueeze(2).to_broadcast([128, 2, 4, 64])
                sinv = rt.t[:, 128:256].rearrange("p (a d) -> p a d", a=2)
                t1 = FD.t[:, 0:512].rearrange("p (a h d) -> p a h d", a=2, h=4)
                t2 = FD.t[:, 512:1024].rearrange("p (a h d) -> p a h d", a=2, h=4)
                S.op("dve", lambda e: e.tensor_tensor(out=t1, in0=qk, in1=cosb, op=ALU.mult),
                     R=[FA.b, rt.b], W=[FD.b])
                S.op("dve", lambda e: e.tensor_tensor(
                    out=t2[:, :, :, 0:32], in0=qk[:, :, :, 32:64],
                    in1=sinv[:, :, 0:32].unsqueeze(2).to_broadcast([128, 2, 4, 32]), op=ALU.mult),
                    R=[FA.b, rt.b], W=[FD.b])
                S.op("dve", lambda e: e.tensor_tensor(
                    out=t2[:, :, :, 32:64], in0=qk[:, :, :, 0:32],
                    in1=sinv[:, :, 32:64].unsqueeze(2).to_broadcast([128, 2, 4, 32]), op=ALU.mult),
                    R=[FA.b, rt.b], W=[FD.b])
                S.op("dve", lambda e: e.tensor_tensor(out=RQK.t[:], in0=FD.t[:, 0:512], in1=FD.t[:, 512:1024],
                                                      op=ALU.add), R=[FD.b], W=[RQK.b])
                S.op("dve", lambda e: e.tensor_tensor(
                    out=KZ.t[:].rearrange("p (h d) -> p h d", h=4),
                    in0=RQK.t[:, 256:512].rearrange("p (h d) -> p h d", h=4),
                    in1=ZET.t[:].unsqueeze(2).to_broadcast([128, 4, 64]), op=ALU.mult),
                    R=[RQK.b, ZET.b], W=[KZ.b])
                S.op("act", lambda e: e.activation(out=RVB.t[:], in_=FA.t[:, 512:1024], func=AF.Copy),
                     R=[FA.b], W=[RVB.b])
                if own or c == NCH - 1:
                    if own:
                        for a in range(2):
                            S.op("act", lambda e, a=a: e.activation(
                                out=SQB.t[:, 0:512].rearrange("p (j a d) -> p j a d", j=4, a=2)[:, :, a, :],
                                in_=FA.t[:, 1536 + a * 256:1792 + a * 256].rearrange("p (j d) -> p j d", j=4),
                                func=AF.Copy), R=[FA.b], W=[SQB.b])
                    S.op("act", lambda e: e.activation(out=SQB.t[:, 512:640], in_=FA.t[:, 2048:2176], func=AF.Copy),
                         R=[FA.b], W=[SQB.b])
                    S.op("act", lambda e: e.activation(out=SVW.t[:, 1, :], in_=FA.t[:, 2176:2304], func=AF.Copy),
                         R=[FA.b], W=[SVW.b])
                if own:
                    for j in range(4):
                        S.op("pe", lambda e, j=j: e.transpose(out=PT.t[:, j * 128:(j + 1) * 128],
                                                              in_=RQK.t[:, j * 128:(j + 1) * 128],
                                                              identity=identb.t[:]), R=[RQK.b, identb.b], W=[PT.b])
                    for j in range(4):
                        S.op("pe", lambda e, j=j: e.transpose(out=PT.t[:, (4 + j) * 128:(5 + j) * 128],
                                                              in_=SQB.t[:, j * 128:(j + 1) * 128],
                                                              identity=identb.t[:]), R=[SQB.b, identb.b], W=[PT.b])
                    S.op("dve", lambda e: e.tensor_copy(out=RQKT.t[:].rearrange("p a b -> p (a b)"),
                                                        in_=PT.t[:, 0:512]), R=[PT.b], W=[RQKT.b])
                    S.op("act", lambda e: e.activation(out=SQT.t[:].rearrange("p a b -> p (a b)"),
                                                       in_=PT.t[:, 512:1024], func=AF.Copy), R=[PT.b], W=[SQT.b])
                elif c == NCH - 1:
                    S.op("pe", lambda e: e.transpose(out=PT.t[:, 0:128], in_=SQB.t[:, 512:640],
                                                     identity=identb.t[:]), R=[SQB.b, identb.b], W=[PT.b])
                    S.op("dve", lambda e: e.tensor_copy(out=SKW.t[:, 128:256], in_=PT.t[:, 0:128]),
                         R=[PT.b], W=[SKW.b])

            PARTS = os.environ.get("KPART", "ret,swa,st").split(",")

            def chunk1b(c):
                own = c >= NCH
                xt = XT[c % 2]
                if own and "swa" in PARTS:
                    S.op("pe", lambda e: e.transpose(out=PT.t[:, 0:128], in_=SQB.t[:, 512:640],
                                                     identity=identb.t[:]), R=[SQB.b, identb.b], W=[PT.b])
                    S.op("dve", lambda e: e.tensor_copy(out=SKW.t[:, 128:256], in_=PT.t[:, 0:128]),
                         R=[PT.b], W=[SKW.b])
                if own and "ret" in PARTS:
                    S.op("dve", lambda e: e.tensor_tensor(out=QXT.t[:], in0=RQKT.t[:, 0:2, :],
                                                          in1=XIT.t[:].rearrange("p (a i) -> p a i", a=2),
                                                          op=ALU.mult), R=[RQKT.b, XIT.b], W=[QXT.b])
                    for h in range(4):
                        a, jj = h % 2, h // 2
                        pb = PB[a]
                        S.op("pe", lambda e, a=a, jj=jj, pb=pb: e.matmul(
                            pb.t[:, jj * 128:(jj + 1) * 128], lhsT=RQKT.t[a * 64:(a + 1) * 64, 2 + jj, :],
                            rhs=RQKT.t[a * 64:(a + 1) * 64, jj, :], start=True, stop=True),
                            R=[RQKT.b], W=[pb.b])
                    for a in range(2):
                        S.op("dve", lambda e, a=a: e.tensor_tensor(
                            out=STB.t[:, a::2, :], in0=PB[a].t[:, 0:256].rearrange("p (j i) -> p j i", j=2),
                            in1=DMT.t[:].rearrange("p (h i) -> p h i", h=4)[:, a::2, :], op=ALU.mult),
                            R=[PB[a].b, DMT.b], W=[STB.b])
                    for h in range(4):
                        a, jj = h % 2, h // 2
                        pb = PB[2 + a]
                        S.op("pe", lambda e, h=h, jj=jj, pb=pb: e.matmul(
                            pb.t[:, jj * 128:(jj + 1) * 128], lhsT=STB.t[:, h, :],
                            rhs=RVB.t[:, h * 128:(h + 1) * 128], start=True, stop=False),
                            R=[STB.b, RVB.b], W=[pb.b])
                        S.op("pe", lambda e, a=a, jj=jj, pb=pb: e.matmul(
                            pb.t[:, jj * 128:(jj + 1) * 128], lhsT=QXT.t[a * 64:(a + 1) * 64, jj, :],
                            rhs=STATEB.t[a * 64:(a + 1) * 64, jj, :], start=False, stop=True),
                            R=[QXT.b, STATEB.b], W=[pb.b])
                    for a in range(2):
                        S.op("act", lambda e, a=a: e.activation(
                            out=RO.t[:].rearrange("p (h e) -> p h e", h=4)[:, a::2, :],
                            in_=PB[2 + a].t[:, 0:256].rearrange("p (j e) -> p j e", j=2), func=AF.Copy),
                            R=[PB[2 + a].b], W=[RO.b])
                for h in (range(4) if "st" in PARTS else []):
                    a, jj = h % 2, h // 2
                    S.op("pe", lambda e, h=h, a=a, jj=jj: e.matmul(
                        PB[4].t[a * 64:(a + 1) * 64, jj * 128:(jj + 1) * 128], lhsT=KZ.t[:, h * 64:(h + 1) * 64],
                        rhs=RVB.t[:, h * 128:(h + 1) * 128], start=True, stop=True),
                        R=[KZ.b, RVB.b], W=[PB[4].b])
                for h in (range(4) if "st" in PARTS else []):
                    a, jj = h % 2, h // 2
                    S.op("dve", lambda e, h=h, a=a, jj=jj: e.scalar_tensor_tensor(
                        out=STATE.t[a * 64:(a + 1) * 64, jj, :], in0=STATE.t[a * 64:(a + 1) * 64, jj, :],
                        scalar=CDEC[h], in1=PB[4].t[a * 64:(a + 1) * 64, jj * 128:(jj + 1) * 128],
                        op0=ALU.mult, op1=ALU.add), R=[STATE.b, PB[4].b], W=[STATE.b])
                S.op("dve", lambda e: e.tensor_copy(out=STATEB.t[:], in_=STATE.t[:]), R=[STATE.b], W=[STATEB.b])
                if not own:
                    if c == NCH - 1:
                        roll()
                    return
                if "swa" in PARTS:
                    swa_part(c)
                if "ret" in PARTS:
                    gn_part(c)
                tail_part(c)

            def gn_part(c):
                ro3 = RO.t[:].rearrange("p (h e) -> p h e", h=4)
                S.op("dve", lambda e: e.tensor_reduce(out=st1.t[:], in_=ro3, axis=AX.X, op=ALU.add),
                     R=[RO.b], W=[st1.b])
                S.op("dve", lambda e: e.tensor_tensor(out=FD.t[:, 0:512], in0=RO.t[:], in1=RO.t[:], op=ALU.mult),
                     R=[RO.b], W=[FD.b])
                S.op("dve", lambda e: e.tensor_reduce(out=st2.t[:], in_=FD.t[:, 0:512].rearrange(
                    "p (h e) -> p h e", h=4), axis=AX.X, op=ALU.add), R=[FD.b], W=[st2.b])
                S.op("dve", lambda e: e.tensor_scalar(out=mean.t[:], in0=st1.t[:], scalar1=1.0 / 128, scalar2=None,
                                                      op0=ALU.mult), R=[st1.b], W=[mean.b])
                S.op("dve", lambda e: e.tensor_tensor(out=var.t[:], in0=mean.t[:], in1=mean.t[:], op=ALU.mult),
                     R=[mean.b], W=[var.b])
                S.op("dve", lambda e: e.scalar_tensor_tensor(out=var.t[:], in0=st2.t[:], scalar=1.0 / 128,
                                                             in1=var.t[:], op0=ALU.mult, op1=ALU.subtract),
                     R=[st2.b, var.b], W=[var.b])
                S.op("dve", lambda e: e.tensor_scalar(out=var.t[:], in0=var.t[:], scalar1=EPS, scalar2=None,
                                                      op0=ALU.add), R=[var.b], W=[var.b])
                S.op("act", lambda e: e.activation(out=var.t[:], in_=var.t[:], func=AF.Sqrt), R=[var.b], W=[var.b])
                S.op("dve", lambda e: e.reciprocal(out=grs.t[:], in_=var.t[:]), R=[var.b], W=[grs.b])
                S.op("dve", lambda e: e.tensor_tensor(out=ro3, in0=ro3,
                                                      in1=mean.t[:].unsqueeze(2).to_broadcast([128, 4, 128]),
                                                      op=ALU.subtract), R=[RO.b, mean.b], W=[RO.b])
                S.op("dve", lambda e: e.tensor_tensor(out=ro3, in0=ro3,
                                                      in1=grs.t[:].unsqueeze(2).to_broadcast([128, 4, 128]),
                                                      op=ALU.mult), R=[RO.b, grs.b], W=[RO.b])
                S.op("act", lambda e: e.activation(out=GS.t[:], in_=FA.t[:, 1024:1536], func=AF.Silu),
                     R=[FA.b], W=[GS.b])
                S.op("dve", lambda e: e.tensor_tensor(out=GS.t[:], in0=GS.t[:], in1=GNB.t[:], op=ALU.mult),
                     R=[GS.b, GNB.b], W=[GS.b])
                S.op("dve", lambda e: e.tensor_tensor(out=MIX.t[:, 0:512], in0=RO.t[:], in1=GS.t[:], op=ALU.mult),
                     R=[RO.b, GS.b], W=[MIX.b])

            def swa_part(c):
                sl3 = SL.t[:].rearrange("p (h k) -> p h k", h=8)
                bi3 = BIA.t[:].rearrange("p (h k) -> p h k", h=8)
                for jp in range(2):
                    for j in (2 * jp, 2 * jp + 1):
                        for a in range(2):
                            pb = PB[5 + a]
                            S.op("pe", lambda e, j=j, a=a, pb=pb: e.matmul(
                                pb.t[:, (j % 2) * 256:(j % 2 + 1) * 256], lhsT=SQT.t[a * 64:(a + 1) * 64, j, :],
                                rhs=SKW.t[a * 64:(a + 1) * 64, :], start=True, stop=True),
                                R=[SQT.b, SKW.b], W=[pb.b])
                    for a in range(2):
                        h0 = 4 * a + 2 * jp
                        S.op("dve", lambda e, a=a, h0=h0: e.scalar_tensor_tensor(
                            out=SL.t[:, h0 * 256:(h0 + 2) * 256], in0=PB[5 + a].t[:], scalar=0.125,
                            in1=BIA.t[:, h0 * 256:(h0 + 2) * 256], op0=ALU.mult, op1=ALU.add),
                            R=[PB[5 + a].b, BIA.b], W=[SL.b])
                if c == NCH:
                    S.op("dve", lambda e: e.tensor_tensor(out=sl3, in0=sl3,
                                                          in1=MS0.t[:].unsqueeze(1).to_broadcast([128, 8, 256]),
                                                          op=ALU.add), R=[SL.b, MS0.b], W=[SL.b])
                S.op("dve", lambda e: e.tensor_reduce(out=mx.t[:], in_=sl3, axis=AX.X, op=ALU.max),
                     R=[SL.b], W=[mx.b])
                S.op("dve", lambda e: e.tensor_tensor(out=mx.t[:], in0=mx.t[:], in1=SNK.t[:], op=ALU.max),
                     R=[mx.b, SNK.b], W=[mx.b])
                S.op("dve", lambda e: e.tensor_scalar(out=negm.t[:], in0=mx.t[:], scalar1=-1.0, scalar2=None,
                                                      op0=ALU.mult), R=[mx.b], W=[negm.b])
                for h in range(8):
                    S.op("act", lambda e, h=h: e.activation(out=PBF.t[:, h, :], in_=SL.t[:, h * 256:(h + 1) * 256],
                                                            func=AF.Exp, bias=negm.t[:, h:h + 1], scale=1.0,
                                                            accum_out=rs.t[:, h:h + 1]),
                         R=[SL.b, negm.b], W=[PBF.b, rs.b])
                S.op("dve", lambda e: e.tensor_tensor(out=es.t[:], in0=SNK.t[:], in1=mx.t[:], op=ALU.subtract),
                     R=[SNK.b, mx.b], W=[es.b])
                S.op("act", lambda e: e.activation(out=es.t[:], in_=es.t[:], func=AF.Exp), R=[es.b], W=[es.b])
                S.op("dve", lambda e: e.tensor_tensor(out=es.t[:], in0=es.t[:], in1=rs.t[:], op=ALU.add),
                     R=[es.b, rs.b], W=[es.b])
                S.op("dve", lambda e: e.reciprocal(out=rden.t[:], in_=es.t[:]), R=[es.b], W=[rden.b])
                for rnd in range(2):
                    for i in range(8):
                        idx = rnd * 8 + i
                        h, half = idx // 2, idx % 2
                        S.op("pe", lambda e, i=i, h=h, half=half: e.transpose(
                            out=PT.t[:, i * 128:(i + 1) * 128], in_=PBF.t[:, h, half * 128:(half + 1) * 128],
                            identity=identb.t[:]), R=[PBF.b, identb.b], W=[PT.b])
                    if rnd == 0:
                        S.op("act", lambda e: e.activation(out=PTT.t[:, 0:8, :].rearrange("p a b -> p (a b)"),
                                                           in_=PT.t[:], func=AF.Copy), R=[PT.b], W=[PTT.b])
                    else:
                        S.op("dve", lambda e: e.tensor_copy(out=PTT.t[:, 8:16, :].rearrange("p a b -> p (a b)"),
                                                            in_=PT.t[:]), R=[PT.b], W=[PTT.b])
                for h in range(8):
                    kv = h // 4
                    for half in range(2):
                        S.op("pe", lambda e, h=h, kv=kv, half=half: e.matmul(
                            PB[4].t[:, h * 64:(h + 1) * 64], lhsT=PTT.t[:, 2 * h + half, :],
                            rhs=SVW.t[:, half, kv * 64:(kv + 1) * 64], start=(half == 0), stop=(half == 1)),
                            R=[PTT.b, SVW.b], W=[PB[4].b])
                S.op("dve", lambda e: e.tensor_tensor(
                    out=MIX.t[:, 512:1024].rearrange("p (h d) -> p h d", h=8),
                    in0=PB[4].t[:].rearrange("p (h d) -> p h d", h=8),
                    in1=rden.t[:].unsqueeze(2).to_broadcast([128, 8, 64]), op=ALU.mult),
                    R=[PB[4].b, rden.b], W=[MIX.b])

            def tail_part(c):
                xt = XT[c % 2]
                roll()
                if stage == 2:
                    S.op("dve", lambda e: e.tensor_copy(out=FA.t[:, 0:1024], in_=MIX.t[:]), R=[MIX.b], W=[FA.b])
                    S.dma("sp", lambda e: e.dma_start(out=dbg_d[c - NCH], in_=FA.t[:]), R=[FA.b], key="FA")
                    return
                for kc in range(8):
                    S.op("pe", lambda e, kc=kc: e.transpose(out=PT.t[:, kc * 128:(kc + 1) * 128],
                                                            in_=MIX.t[:, kc * 128:(kc + 1) * 128],
                                                            identity=identb.t[:]), R=[MIX.b, identb.b], W=[PT.b])
                S.op("act", lambda e: e.activation(out=MIXT.t[:].rearrange("p a b -> p (a b)"), in_=PT.t[:],
                                                   func=AF.Copy), R=[PT.b], W=[MIXT.b])
                for nh in range(2):
                    pb = PB[nh]
                    for kc in range(8):
                        S.op("pe", lambda e, kc=kc, nh=nh, pb=pb: e.matmul(
                            pb.t[:], lhsT=MIXT.t[:, kc, :], rhs=WOUT.t[:, kc, nh * 512:(nh + 1) * 512],
                            start=(kc == 0), stop=(kc == 7)), R=[MIXT.b, WOUT.b], W=[pb.b])
                    S.op("dve", lambda e, nh=nh, pb=pb: e.tensor_tensor(
                        out=xt.t[:, nh * 512:(nh + 1) * 512], in0=pb.t[:], in1=xt.t[:, nh * 512:(nh + 1) * 512],
                        op=ALU.add), R=[pb.b, xt.b], W=[xt.b])
                S.dma("sp", lambda e: e.dma_start(out=h1_d[(c - NCH) * 128:(c - NCH + 1) * 128, :], in_=xt.t[:]),
                      R=[xt.b], key=xt.b.name)
                if stage == 3:
                    S.dma("sp", lambda e: e.dma_start(out=dbg_d[c - NCH, :, 0:1024], in_=xt.t[:]), R=[xt.b],
                          key=xt.b.name)

            def roll():
                S.op("pool", lambda e: e.tensor_copy(out=SKW.t[:, 0:128], in_=SKW.t[:, 128:256]),
                     R=[SKW.b], W=[SKW.b])
                S.op("pool", lambda e: e.tensor_copy(out=SVW.t[:, 0, :], in_=SVW.t[:, 1, :]),
                     R=[SVW.b], W=[SVW.b])

            chs = range(2 * NCH)
            if os.environ.get("KCH"):
                chs = [int(v) for v in os.environ["KCH"].split(",")]
            p0_done = 0
            chs = list(chs)
            norm1(chs[0])
            for ci, c in enumerate(chs):
                chunk1(c)
                if ci + 1 < len(chs):
                    norm1(chs[ci + 1])
                if stage >= 2:
                    chunk1b(c)
                if stage >= 4:
                    want = (NP0 * (ci + 1)) // len(chs)
                    while p0_done < want:
                        p0_step(p0_done)
                        p0_done += 1
            S.barrier()

        if stage >= 4:
            p2 = ExitStack()
            with p2:
                def sb2(shape, dt, name=None):
                    return sb(shape, dt, st=p2, name=name)

                WQ = sb2([128, 8, 2048], BF16, "WQ")
                SKT = sb2([128, 16, 128], BF16, "SKT")
                wq_v = wq.rearrange("(kc p) n -> p kc n", p=128)
                load_cast(lambda i: WQ.t[:, i, :], lambda i: wq_v[:, i, :], 8, 2048, wb=WQ.b)
                skt_v = skt.rearrange("f c k -> c f k")
                S.dma("sp", lambda e: e.dma_start(out=FA.t[:, 0:2048].rearrange("p (f k) -> p f k", f=16),
                                                  in_=skt_v), W=[FA.b], key="FA")
                S.op("dve", lambda e: e.tensor_copy(out=SKT.t[:].rearrange("p a b -> p (a b)"),
                                                    in_=FA.t[:, 0:2048]), R=[FA.b], W=[SKT.b])
                G2B = sb2([128, D], F32, "G2B")
                FGB = sb2([128, D], F32, "FGB")
                S.dma("sp", lambda e: e.dma_start(out=G2B.t[:], in_=g2b), W=[G2B.b], key="G2B")
                S.dma("sp", lambda e: e.dma_start(out=FGB.t[:], in_=fgb), W=[FGB.b], key="FGB")

                XT2 = [sb2([128, D], F32, "XT2%d" % i) for i in range(3)]
                XNs = [sb2([128, D], F32, "XN%d" % i) for i in range(2)]
                XNB = sb2([128, D], BF16, "XNB")
                XNT = sb2([128, 8, 128], BF16, "XNT")
                QT = sb2([128, 16, 128], BF16, "QT")
                JD2 = sb2([128, D], BF16, "JD2")
                FC = sb2([128, 2048], F32, "FC")
                FDD = sb2([128, 2048], F32, "FDD")
                ss2 = sb2([128, 1], F32, "ss2")
                sd2 = sb2([128, 1], F32, "sd2")
                rstd2 = sb2([128, 1], F32, "rstd2")
                ss3 = sb2([128, 1], F32, "ss3")
                sd3 = sb2([128, 1], F32, "sd3")
                rstd3 = sb2([128, 1], F32, "rstd3")
                HS = sb2([128, 16, 16], F32, "HS")
                HI = sb2([128, 16, 16], U32, "HI")
                HIF = sb2([128, 16, 16], F32, "HIF")
                HI0 = sb2([128, 8, 16], F32, "HI0")
                TS = sb2([128, 8, 16], F32, "TS")
                PI = sb2([128, 8, 16], U32, "PI")
                PA = sb2([128, 8, 16], U32, "PA")
                PBI = sb2([128, 8, 16], U32, "PBI")
                PAF = sb2([128, 8, 16], F32, "PAF")
                PXBF = sb2([128, 8, 16], F32, "PXBF")
                I1F = sb2([128, 8, 16], F32, "I1F")
                I2F = sb2([128, 8, 16], F32, "I2F")
                IOTA = sb2([128, 256], F32, "IOTA")
                S.dma("sp", lambda e: e.dma_start(out=IOTA.t[:], in_=iota_d), W=[IOTA.b], key="IOTA")
                EIF = sb2([128, 128], F32, "EIF")
                EIIs = [sb2([128, 128], I32, "EII%d" % i) for i in range(2)]
                GEs = [sb2([128, 128], F32, "GE%d" % i) for i in range(2)]
                gz = sb2([128, 8], F32, "gz")
                HCR = [sb2([128, 1], F32, "HC%d" % i) for i in range(4)]
                AAR = [sb2([128, 1], F32, "AA%d" % i) for i in range(4)]
                DG = [sb2([128, 128], BF16, "DG%d" % i) for i in range(4)]
                GT = [sb2([128, 2 * D], BF16, "GT%d" % i) for i in range(NSLOT)]
                gcount = [0]
                USE_F32R = os.environ.get("KF32R", "0") == "1"

                def mmview(ap):
                    return ap.bitcast(mybir.dt.float32r) if USE_F32R else ap

                def front(c):
                    xt = XT2[c % 3]
                    XN = XNs[c % 2]
                    EII = EIIs[c % 2]
                    GE = GEs[c % 2]
                    S.dma("sp", lambda e: e.dma_start(out=xt.t[:], in_=h1_d[c * 128:(c + 1) * 128, :]), W=[xt.b],
                          key=xt.b.name)
                    S.op("dve", lambda e: e.scalar_tensor_tensor(out=JD2.t[:], in0=xt.t[:], scalar=1.0, in1=xt.t[:],
                                                                 op0=ALU.mult, op1=ALU.mult, accum_out=ss2.t[:]),
                         R=[xt.b], W=[JD2.b, ss2.b])
                    S.op("dve", lambda e: e.tensor_scalar(out=sd2.t[:], in0=ss2.t[:], scalar1=1.0 / D, scalar2=EPS,
                                                          op0=ALU.mult, op1=ALU.add), R=[ss2.b], W=[sd2.b])
                    S.op("act", lambda e: e.activation(out=sd2.t[:], in_=sd2.t[:], func=AF.Sqrt), R=[sd2.b],
                         W=[sd2.b])
                    yield
                    yield
                    S.op("dve", lambda e: e.reciprocal(out=rstd2.t[:], in_=sd2.t[:]), R=[sd2.b], W=[rstd2.b])
                    S.op("dve", lambda e: e.scalar_tensor_tensor(out=XN.t[:], in0=xt.t[:], scalar=rstd2.t[:],
                                                                 in1=G2B.t[:], op0=ALU.mult, op1=ALU.mult),
                         R=[xt.b, rstd2.b, G2B.b], W=[XN.b])
                    S.op("act", lambda e: e.activation(out=XNB.t[:], in_=XN.t[:], func=AF.Copy), R=[XN.b], W=[XNB.b])
                    yield
                    for kc in range(8):
                        S.op("pe", lambda e, kc=kc: e.transpose(out=PT.t[:, kc * 128:(kc + 1) * 128],
                                                                in_=XNB.t[:, kc * 128:(kc + 1) * 128],
                                                                identity=identb.t[:]), R=[XNB.b, identb.b], W=[PT.b])
                    yield
                    S.op("act", lambda e: e.activation(out=XNT.t[:].rearrange("p a b -> p (a b)"), in_=PT.t[:],
                                                       func=AF.Copy), R=[PT.b], W=[XNT.b])
                    yield
                    yield
                    pend = []
                    for f in range(16):
                        pb = PB[(f // 4) % 3]
                        for kc in range(8):
                            S.op("pe", lambda e, f=f, kc=kc, pb=pb: e.matmul(
                                pb.t[:, (f % 4) * 128:(f % 4 + 1) * 128], lhsT=WQ.t[:, kc, f * 128:(f + 1) * 128],
                                rhs=XNT.t[:, kc, :], start=(kc == 0), stop=(kc == 7)),
                                R=[WQ.b, XNT.b], W=[pb.b])
                        if f % 4 == 1 and pend:
                            pend.pop(0)()
                        if f % 4 == 3:
                            g = f // 4
                            pend.append(lambda g=g, pb=pb: S.op("act", lambda e: e.activation(
                                out=QT.t[:, 4 * g:4 * g + 4, :].rearrange("p a b -> p (a b)"), in_=pb.t[:],
                                func=AF.Copy), R=[pb.b], W=[QT.b]))
                        yield
                    for i_ in range(4):
                        if i_ == 1 and pend:
                            pend.pop(0)()
                        yield
                    for f in range(16):
                        pb = PB[(f // 4) % 3]
                        S.op("pe", lambda e, f=f, pb=pb: e.matmul(
                            pb.t[:, (f % 4) * 128:(f % 4 + 1) * 128], lhsT=QT.t[:, f, :], rhs=SKT.t[:, f, :],
                            start=True, stop=True), R=[QT.b, SKT.b], W=[pb.b])
                        if f % 4 == 3:
                            g = f // 4
                            if pend:
                                pend.pop(0)()
                            pend.append(lambda g=g, pb=pb: S.op("act", lambda e: e.activation(
                                out=FA.t[:, g * 512:(g + 1) * 512], in_=pb.t[:], func=AF.Copy),
                                R=[pb.b], W=[FA.b]))
                            yield
                    for i_ in range(int(os.environ.get("KSPACE", "10"))):
                        if i_ == 1 and pend:
                            pend.pop(0)()
                        yield
                    assert not pend
                    for f in range(16):
                        sc = FA.t[:, f * 128:(f + 1) * 128]
                        wk = FDD.t[:, f * 128:(f + 1) * 128]
                        S.op("dve", lambda e, f=f, sc=sc: e.max(out=HS.t[:, f, 0:8], in_=sc), R=[FA.b], W=[HS.b])
                        S.op("dve", lambda e, f=f, sc=sc, wk=wk: e.match_replace(
                            out=wk, in_to_replace=HS.t[:, f, 0:8], in_values=sc, imm_value=-1e30),
                            R=[FA.b, HS.b], W=[FDD.b])
                        yield
                        S.op("dve", lambda e, f=f, wk=wk: e.max(out=HS.t[:, f, 8:16], in_=wk), R=[FDD.b], W=[HS.b])
                        S.op("dve", lambda e, f=f, sc=sc: e.max_index(out=HI.t[:, f, 0:8], in_max=HS.t[:, f, 0:8],
                                                                      in_values=sc), R=[FA.b, HS.b], W=[HI.b])
                        yield
                        S.op("dve", lambda e, f=f, wk=wk: e.max_index(out=HI.t[:, f, 8:16], in_max=HS.t[:, f, 8:16],
                                                                      in_values=wk), R=[FDD.b, HS.b], W=[HI.b])
                        yield
                    S.op("dve", lambda e: e.tensor_copy(out=HIF.t[:], in_=HI.t[:]), R=[HI.b], W=[HIF.b])
                    hs4 = HS.t[:].rearrange("p (h a) k -> p h a k", a=2)
                    hif4 = HIF.t[:].rearrange("p (h a) k -> p h a k", a=2)
                    cs4 = FB.t[:, 0:2048].rearrange("p (h a b) -> p h a b", h=8, a=16)
                    ci4 = FC.t[:].rearrange("p (h a b) -> p h a b", h=8, a=16)
                    S.op("dve", lambda e: e.tensor_tensor(
                        out=cs4, in0=hs4[:, :, 0, :].unsqueeze(3).to_broadcast([128, 8, 16, 16]),
                        in1=hs4[:, :, 1, :].unsqueeze(2).to_broadcast([128, 8, 16, 16]), op=ALU.add),
                        R=[HS.b], W=[FB.b])
                    yield
                    for h in range(8):
                        cs = FB.t[:, h * 256:(h + 1) * 256]
                        wk = FDD.t[:, h * 256:(h + 1) * 256]
                        S.op("dve", lambda e, h=h, cs=cs: e.max(out=TS.t[:, h, 0:8], in_=cs), R=[FB.b], W=[TS.b])
                        S.op("dve", lambda e, h=h, cs=cs: e.max_index(out=PI.t[:, h, 0:8], in_max=TS.t[:, h, 0:8],
                                                                      in_values=cs), R=[FB.b, TS.b], W=[PI.b])
                        S.op("dve", lambda e, h=h, cs=cs, wk=wk: e.match_replace(
                            out=wk, in_to_replace=TS.t[:, h, 0:8], in_values=cs, imm_value=-1e30),
                            R=[FB.b, TS.b], W=[FDD.b])
                        yield
                        S.op("dve", lambda e, h=h, wk=wk: e.max(out=TS.t[:, h, 8:16], in_=wk), R=[FDD.b], W=[TS.b])
                        S.op("dve", lambda e, h=h, wk=wk: e.max_index(out=PI.t[:, h, 8:16], in_max=TS.t[:, h, 8:16],
                                                                      in_values=wk), R=[FDD.b, TS.b], W=[PI.b])
                        yield
                    S.op("dve", lambda e: e.tensor_scalar(out=PA.t[:], in0=PI.t[:], scalar1=4, scalar2=None,
                                                          op0=ALU.logical_shift_right), R=[PI.b], W=[PA.b])
                    S.op("dve", lambda e: e.tensor_scalar(out=PBI.t[:], in0=PI.t[:], scalar1=15, scalar2=None,
                                                          op0=ALU.bitwise_and), R=[PI.b], W=[PBI.b])
                    yield
                    S.op("dve", lambda e: e.tensor_copy(out=PAF.t[:], in_=PA.t[:]), R=[PA.b], W=[PAF.b])
                    S.op("dve", lambda e: e.tensor_copy(out=PXBF.t[:], in_=PBI.t[:]), R=[PBI.b], W=[PXBF.b])
                    yield
                    io4 = IOTA.t[:, 0:16].unsqueeze(1).unsqueeze(1).to_broadcast([128, 8, 16, 16])
                    eq4 = FDD.t[:].rearrange("p (h k a) -> p h k a", h=8, k=16)
                    for half, (pf, dst) in enumerate(((PAF, I1F), (PXBF, I2F))):
                        S.op("dve", lambda e, pf=pf: e.tensor_tensor(
                            out=eq4, in0=io4, in1=pf.t[:].unsqueeze(3).to_broadcast([128, 8, 16, 16]),
                            op=ALU.is_equal), R=[IOTA.b, pf.b], W=[FDD.b])
                        yield
                        S.op("dve", lambda e, half=half: e.tensor_tensor(
                            out=eq4, in0=eq4,
                            in1=hif4[:, :, half, :].unsqueeze(2).to_broadcast([128, 8, 16, 16]), op=ALU.mult),
                            R=[FDD.b, HIF.b], W=[FDD.b])
                        yield
                        S.op("dve", lambda e, dst=dst: e.tensor_reduce(out=dst.t[:], in_=eq4, axis=AX.X, op=ALU.add),
                             R=[FDD.b], W=[dst.b])
                        yield
                    S.op("dve", lambda e: e.scalar_tensor_tensor(
                        out=EIF.t[:].rearrange("p (h k) -> p h k", h=8), in0=I1F.t[:], scalar=128.0, in1=I2F.t[:],
                        op0=ALU.mult, op1=ALU.add), R=[I1F.b, I2F.b], W=[EIF.b])
                    S.op("dve", lambda e: e.tensor_copy(out=EII.t[:], in_=EIF.t[:]), R=[EIF.b], W=[EII.b])
                    ge3 = GE.t[:].rearrange("p (h k) -> p h k", h=8)
                    S.op("dve", lambda e: e.tensor_tensor(
                        out=ge3, in0=TS.t[:], in1=TS.t[:, :, 0:1].to_broadcast([128, 8, 16]), op=ALU.subtract),
                        R=[TS.b], W=[GE.b])
                    S.op("act", lambda e: e.activation(out=GE.t[:], in_=GE.t[:], func=AF.Exp), R=[GE.b], W=[GE.b])
                    yield
                    S.op("dve", lambda e: e.tensor_reduce(out=gz.t[:], in_=ge3, axis=AX.X, op=ALU.add),
                         R=[GE.b], W=[gz.b])
                    S.op("dve", lambda e: e.reciprocal(out=gz.t[:], in_=gz.t[:]), R=[gz.b], W=[gz.b])
                    S.op("dve", lambda e: e.tensor_tensor(out=ge3, in0=ge3,
                                                          in1=gz.t[:].unsqueeze(2).to_broadcast([128, 8, 16]),
                                                          op=ALU.mult), R=[GE.b, gz.b], W=[GE.b])
                    yield

                def accb(c):
                    return (PB[5], PB[6]) if c % 2 == 0 else (PB[3], PB[4])

                def back(c, pump, prev_tail):
                    xt = XT2[c % 3]
                    ACB = accb(c)
                    XN = XNs[c % 2]
                    EII = EIIs[c % 2]
                    GE = GEs[c % 2]
                    for s_ in range(128):
                        gt = GT[gcount[0] % NSLOT]
                        gcount[0] += 1
                        hc = HCR[s_ % 4]
                        aa = AAR[s_ % 4]
                        dg = DG[s_ % 4]
                        S.dma("pool", lambda e, gt=gt, s_=s_: e.indirect_dma_start(
                            out=gt.t[:], out_offset=None, in_=uvb_d,
                            in_offset=bass.IndirectOffsetOnAxis(ap=EII.t[:, s_:s_ + 1], axis=0)),
                            R=[EII.b], W=[gt.b], key=gt.b.name)
                        S.op("dve", lambda e, gt=gt, hc=hc: e.scalar_tensor_tensor(
                            out=JD2.t[:], in0=gt.t[:, 0:D], scalar=1.0, in1=XN.t[:], op0=ALU.mult,
                            op1=ALU.mult, accum_out=hc.t[:]), R=[gt.b, XN.b], W=[JD2.b, hc.b])
                        S.op("act", lambda e, hc=hc, aa=aa: e.activation(out=aa.t[:], in_=hc.t[:], func=AF.Gelu),
                             R=[hc.b], W=[aa.b])
                        S.op("act", lambda e, aa=aa, s_=s_: e.activation(out=aa.t[:], in_=aa.t[:], func=AF.Copy,
                                                                         scale=GE.t[:, s_:s_ + 1]),
                             R=[aa.b, GE.b], W=[aa.b])
                        S.op("act", lambda e, dg=dg, aa=aa: e.activation(out=dg.t[:], in_=identf.t[:], func=AF.Copy,
                                                                         scale=aa.t[:]),
                             R=[identf.b, aa.b], W=[dg.b])
                        for nh in range(2):
                            S.op("pe", lambda e, gt=gt, dg=dg, s_=s_, nh=nh: e.matmul(
                                ACB[nh].t[:], lhsT=dg.t[:],
                                rhs=gt.t[:, D + nh * 512:D + (nh + 1) * 512],
                                start=(s_ == 0), stop=(s_ == 127)), R=[dg.b, gt.b], W=[ACB[nh].b])
                        pump(int(os.environ.get("KPUMP", "1")))
                        if s_ == 114 and prev_tail is not None:
                            prev_tail[0]()
                        if s_ == 119 and prev_tail is not None:
                            prev_tail[1]()
                    for _ in range(2):
                        S.op("pe", lambda e: e.transpose(out=PT.t[:, 0:128], in_=identb.t[:], identity=identb.t[:]),
                             R=[identb.b], W=[PT.b, ACB[0].b, ACB[1].b])

                def tail(c):
                    xt = XT2[c % 3]
                    ACB = accb(c)
                    for nh in range(2):
                        S.op("dve", lambda e, nh=nh: e.tensor_tensor(
                            out=xt.t[:, nh * 512:(nh + 1) * 512], in0=ACB[nh].t[:],
                            in1=xt.t[:, nh * 512:(nh + 1) * 512], op=ALU.add), R=[xt.b, ACB[nh].b], W=[xt.b])
                    S.op("dve", lambda e: e.scalar_tensor_tensor(out=JD2.t[:], in0=xt.t[:], scalar=1.0, in1=xt.t[:],
                                                                 op0=ALU.mult, op1=ALU.mult, accum_out=ss3.t[:]),
                         R=[xt.b], W=[JD2.b, ss3.b])
                    S.op("dve", lambda e: e.tensor_scalar(out=sd3.t[:], in0=ss3.t[:], scalar1=1.0 / D, scalar2=EPS,
                                                          op0=ALU.mult, op1=ALU.add), R=[ss3.b], W=[sd3.b])
                    S.op("act", lambda e: e.activation(out=sd3.t[:], in_=sd3.t[:], func=AF.Sqrt), R=[sd3.b],
                         W=[sd3.b])

                def tail_b(c):
                    xt = XT2[c % 3]
                    S.op("dve", lambda e: e.reciprocal(out=rstd3.t[:], in_=sd3.t[:]), R=[sd3.b], W=[rstd3.b])
                    S.op("dve", lambda e: e.scalar_tensor_tensor(out=xt.t[:], in0=xt.t[:], scalar=rstd3.t[:],
                                                                 in1=FGB.t[:], op0=ALU.mult, op1=ALU.mult),
                         R=[xt.b, rstd3.b, FGB.b], W=[xt.b])
                    S.dma("sp", lambda e: e.dma_start(out=out_d[c * 128:(c + 1) * 128, :], in_=xt.t[:]), R=[xt.b],
                          key=xt.b.name)

                chs2 = list(range(NCH))
                if os.environ.get("KCH2"):
                    chs2 = [int(v) for v in os.environ["KCH2"].split(",")]
                for _ in front(chs2[0]):
                    pass
                prev_tail = None
                for i, c in enumerate(chs2):
                    gen = front(chs2[i + 1]) if i + 1 < len(chs2) else iter(())

                    def pump(n, gen=gen):
                        for _ in range(n):
                            next(gen, None)
                    back(c, pump, prev_tail)
                    for _ in gen:
                        pass
                    prev_tail = ((lambda c=c: tail(c)), (lambda c=c: tail_b(c)))
                if prev_tail is not None:
                    prev_tail[0]()
                    prev_tail[1]()
                S.barrier()
        if stage < 4:
            pass
        S.final()
        S.emit()
    return nc


def t5_bucket(rel):
    max_exact = 16
    n = np.maximum(rel, 0)
    large = max_exact + (np.log(np.maximum(n, 1).astype(np.float32) / max_exact)
                         / math.log(128 / max_exact) * (32 - max_exact)).astype(np.int32)
    large = np.minimum(large, 31)
    return np.where(n < max_exact, n, large).astype(np.int32)


def host_consts():
    f32 = np.float32
    lg = np.array(LOGG, dtype=f32)
    idx = np.arange(128, dtype=f32)
    diff = idx[:, None] - idx[None, :]
    dm = np.where(diff[None] >= 0, np.exp(np.maximum(diff, 0.0)[None] * lg[:, None, None]), 0.0).astype(f32)
    dmt = np.ascontiguousarray(dm.transpose(2, 0, 1)).reshape(128, 512)
    xi = np.exp((idx + 1.0)[None, :] * lg[:, None]).astype(f32)
    xit = np.zeros((128, 2, 128), f32)
    for h in range(4):
        xit[(h % 2) * 64:(h % 2 + 1) * 64, h // 2, :] = xi[h][None, :]
    zeta = np.exp((127.0 - idx)[None, :] * lg[:, None]).astype(f32).T
    W = 128
    rel = np.arange(W)[:, None] + W - np.arange(2 * W)[None, :]
    band = (rel >= 0) & (rel < W)
    mask = np.where(band, 0.0, NEG).astype(f32)
    bucket = t5_bucket(rel)
    return dict(dmt=dmt, xit=xit.reshape(128, 256), zeta=np.ascontiguousarray(zeta), mask=mask, bucket=bucket)


def rot_table(pos0):
    inv = (1.0 / (10000.0 ** (np.arange(0, 64, 2, dtype=np.float32) / 64))).astype(np.float32)
    pos = (pos0 + np.arange(2048)).astype(np.float32)
    ang = pos[:, None] * inv[None, :]
    cos = np.cos(ang).astype(np.float32)
    sin = np.sin(ang).astype(np.float32)
    c2 = np.concatenate([cos, cos], axis=1)
    s2 = np.concatenate([-sin, sin], axis=1)
    t = np.concatenate([c2, c2 * np.float32(0.125), s2, s2 * np.float32(0.125)], axis=1)
    return t.reshape(16, 128, 256)


_NC_CACHE = {}


def prep(x, norm1_g, w_in, ret_gn_g, swa_sinks, rel_bias, w_out, norm2_g, peer_wq, peer_subkeys, peer_u,
         peer_v, final_g, stage=99):
    f32 = np.float32
    x = np.asarray(x, f32)
    hc = host_consts()
    rb = np.asarray(rel_bias, f32)
    bias = rb[hc["bucket"]]
    bias = np.ascontiguousarray(bias.transpose(0, 2, 1)).reshape(128, 2048)
    rep = lambda v, n: np.ascontiguousarray(np.broadcast_to(np.asarray(v, f32).reshape(1, n), (128, n)))
    common = dict(
        w_in=np.ascontiguousarray(np.asarray(w_in, f32)[0]),
        g1=np.ascontiguousarray(np.asarray(norm1_g, f32)[0].reshape(8, 128).T),
        dmt=hc["dmt"], xit=hc["xit"], zeta=hc["zeta"],
        gnb=rep(np.asarray(ret_gn_g)[0], 512), sinkb=rep(np.asarray(swa_sinks)[0], 8),
        bias=bias, mask=hc["mask"], ident=np.eye(128, dtype=f32),
        w_out=np.ascontiguousarray(np.asarray(w_out, f32)[0]),
        g2b=rep(np.asarray(norm2_g)[0], D),
        wq=np.ascontiguousarray(np.asarray(peer_wq, f32)[0]),
        skt=np.ascontiguousarray(np.asarray(peer_subkeys, f32)[0].reshape(16, 128, 128).transpose(0, 2, 1)),
        uv=np.ascontiguousarray(np.concatenate([np.asarray(peer_u, f32)[0], np.asarray(peer_v, f32)[0]], axis=1)),
        fgb=rep(final_g, D),
        iota=np.ascontiguousarray(np.broadcast_to(np.arange(256, dtype=f32).reshape(1, 256), (128, 256))),
    )
    rot_lo = rot_table(0)
    rot_hi = rot_table(2048)
    if stage < 4:
        common.pop("uv")
        for nm in ("g2b", "wq", "skt", "fgb"):
            pass
    in_maps = []
    for k in range(NCORES):
        b, half = k // 2, k % 2
        own = x[b, half * TOK:(half + 1) * TOK]
        if half == 0:
            pre = np.zeros_like(own)
            rot = np.concatenate([rot_lo, rot_lo], axis=0)
            mask0 = np.zeros((128, 256), f32)
            mask0[:, 0:128] = NEG
        else:
            pre = x[b, 0:TOK]
            rot = np.concatenate([rot_lo, rot_hi], axis=0)
            mask0 = np.zeros((128, 256), f32)
        m = dict(common)
        m["xs"] = np.ascontiguousarray(np.concatenate([pre, own], axis=0))
        m["rot"] = np.ascontiguousarray(rot)
        m["mask0"] = mask0
        in_maps.append(m)
    return in_maps


def kernel(_stage=None, **inputs):
    stage = 99 if _stage is None else _stage
    in_maps = prep(stage=stage, **inputs)
    f32 = np.float32
    if stage not in _NC_CACHE:
        _NC_CACHE[stage] = build(stage)
    nc = _NC_CACHE[stage]
    if os.environ.get("KTRACE"):
        res = run_bass_kernel_spmd(nc, in_maps, core_ids=list(range(NCORES)), trace=True)
        print("EXEC_TIME_NS", res.exec_time_ns)
    else:
        res = run_bass_kernel_spmd(nc, in_maps, core_ids=list(range(NCORES)))
    if stage < 99:
        return [r["dbg"] for r in res.results]
    outs = [np.asarray(r["out"], f32) for r in res.results]
    return np.stack(outs, axis=0).reshape(4, 4096, D)
```
